# Optimizing a Trainium2 kernel written in Bass

```python
import math
import jax, jax.numpy as jnp
from jax import lax
import numpy as np

D_MODEL = 2048
BATCH = 2
SEQ = 4096
DEPTH = 1
DEC_BATCH = 4
DEC_SEQ = 4096
PAST_LEN = 128

RWKV_HEADS = 16
RWKV_HEAD_DIM = 64
RWKV_DIM = RWKV_HEADS * RWKV_HEAD_DIM
DECAY_LORA = 64
ICLR_LORA = 64
GATE_LORA = 128
S5_GROUPS = 32
S5_GROUP_CH = 16
S5_DIM = S5_GROUPS * S5_GROUP_CH
S5_STATE = 64
DT_MIN = 1e-3
DT_MAX = 1e-1
N_BRANCH = 2
D_FF = 5504
RMS_EPS = 1e-6
GN_EPS = 64e-5
R_OFF = 0
K_OFF = RWKV_DIM
V_OFF = 2 * RWKV_DIM
WLOW_OFF = 3 * RWKV_DIM
ALOW_OFF = WLOW_OFF + 2 * DECAY_LORA
GLOW_OFF = ALOW_OFF + 2 * ICLR_LORA
SHIFT_COLS = GLOW_OFF + GATE_LORA
U_OFF = SHIFT_COLS
GATE_OFF = U_OFF + S5_DIM
IN_COLS = GATE_OFF + N_BRANCH * D_MODEL

kernel_name = 'hybrid_rwkv7_s5_macaron_encoder'


def _rmsnorm(x, g):
    xf = x.astype(jnp.float32)
    y = xf * lax.rsqrt(jnp.mean(xf * xf, -1, keepdims=True) + RMS_EPS)
    return (y * g.astype(jnp.float32)).astype(x.dtype)


def _swiglu(x, w_gate, w_up, w_down):
    return (jax.nn.silu(x @ w_gate) * (x @ w_up)) @ w_down


def _centred_shift(p, mu):
    prev = jnp.pad(p[:, :-1], ((0, 0), (1, 0), (0, 0)))
    nxt = jnp.pad(p[:, 1:], ((0, 0), (0, 1), (0, 0)))
    return p + (0.5 * (prev + nxt) - p) * mu


def _rwkv_step(S, inp):
    r, w, k, v, kk, b = inp
    sa = jnp.einsum('bhvk,bhk->bhv', S, -kk)
    S = S * w[:, :, None, :] + sa[..., None] * b[:, :, None, :] + v[..., None] * k[:, :, None, :]
    y = jnp.einsum('bhvk,bhk->bhv', S, r)
    return S, y


def _rwkv_mixer(p, w0, w2, a0, a2, g2, k_k, k_a, r_k, ln_w, ln_b):
    Bsz, L = p.shape[0], p.shape[1]
    H, N = RWKV_HEADS, RWKV_HEAD_DIM
    pf = p.astype(jnp.float32)
    r = pf[..., R_OFF:K_OFF].reshape(Bsz, L, H, N)
    k = pf[..., K_OFF:V_OFF].reshape(Bsz, L, H, N)
    v = pf[..., V_OFF:WLOW_OFF].reshape(Bsz, L, H, N)
    wlow = jnp.tanh(pf[..., WLOW_OFF:ALOW_OFF]).reshape(Bsz, L, 2, DECAY_LORA)
    alow = pf[..., ALOW_OFF:GLOW_OFF].reshape(Bsz, L, 2, ICLR_LORA)
    glow = pf[..., GLOW_OFF:SHIFT_COLS]
    wpre = jnp.einsum('bldr,drc->dblc', wlow, w2) + w0[:, None, None, :]
    decay = jnp.exp(-jnp.exp(-jax.nn.softplus(-wpre) - 0.5)).reshape(2, Bsz, L, H, N)
    a = jax.nn.sigmoid(jnp.einsum('bldr,drc->dblc', alow, a2) + a0[:, None, None, :]).reshape(2, Bsz, L, H, N)
    kk = k * k_k.reshape(H, N)
    kk = kk * lax.rsqrt(jnp.sum(kk * kk, -1, keepdims=True) + 1e-12)
    k_dir = k[None] * (1.0 + (a - 1.0) * k_a.reshape(H, N))
    b_dir = kk[None] * a

    def both(t):
        return jnp.swapaxes(jnp.concatenate([t, jnp.flip(t, 1)], 0), 0, 1)

    def per_dir(t):
        return jnp.swapaxes(jnp.concatenate([t[0], jnp.flip(t[1], 1)], 0), 0, 1)

    xs = (both(r), per_dir(decay), per_dir(k_dir), both(v), both(kk), per_dir(b_dir))
    s0 = jnp.zeros((2 * Bsz, H, N, N), jnp.float32)
    _, ys = lax.scan(_rwkv_step, s0, xs)
    ys = jnp.swapaxes(ys, 0, 1)
    y = ys[:Bsz] + jnp.flip(ys[Bsz:], 1)
    mean = jnp.mean(y, -1, keepdims=True)
    var = jnp.mean(jnp.square(y - mean), -1, keepdims=True)
    y = (y - mean) * lax.rsqrt(var + GN_EPS) * ln_w.reshape(H, N) + ln_b.reshape(H, N)
    bonus = jnp.sum(r * (k_dir[0] + k_dir[1]) * r_k.reshape(H, N), -1, keepdims=True) * v
    y = (y + bonus).reshape(Bsz, L, RWKV_DIM)
    g = jax.nn.sigmoid(glow) @ g2
    return y * g


def _s5_combine(e1, e2):
    a1, b1 = e1
    a2, b2 = e2
    return a2 * a1, a2 * b1 + b2


def _s5_direction(ug, a_re, a_im, log_step, b_re, b_im):
    lam = lax.complex(a_re.astype(jnp.float32), a_im.astype(jnp.float32))
    dt = jnp.exp(log_step.astype(jnp.float32))[:, None]
    lam_bar = jnp.exp(lam * dt)
    b_bar = ((lam_bar - 1.0) / lam)[..., None] * lax.complex(b_re.astype(jnp.float32), b_im.astype(jnp.float32))
    bu = lax.complex(jnp.einsum('blgc,gpc->blgp', ug, jnp.real(b_bar)),
                     jnp.einsum('blgc,gpc->blgp', ug, jnp.imag(b_bar)))
    lam_seq = jnp.broadcast_to(lam_bar, bu.shape)
    _, states = lax.associative_scan(_s5_combine, (lam_seq, bu), axis=1)
    return states


def _s5_mixer(u, a_re, a_im, log_step, b_re, b_im, c_re, c_im, d_skip, w_glu, b_glu):
    Bsz, L = u.shape[0], u.shape[1]
    uf = u.astype(jnp.float32)
    ug = uf.reshape(Bsz, L, S5_GROUPS, S5_GROUP_CH)
    s_fwd = _s5_direction(ug, a_re[0], a_im[0], log_step[0], b_re[0], b_im[0])
    s_bwd = jnp.flip(_s5_direction(jnp.flip(ug, 1), a_re[1], a_im[1], log_step[1], b_re[1], b_im[1]), 1)
    st = s_fwd + s_bwd
    y = (jnp.einsum('blgp,gcp->blgc', jnp.real(st), c_re)
         - jnp.einsum('blgp,gcp->blgc', jnp.imag(st), c_im))
    y = y.reshape(Bsz, L, S5_DIM) + d_skip * uf
    y = jax.nn.gelu(y)
    return y * jax.nn.sigmoid(y @ w_glu + b_glu)


def _layer(x, norm_ffn1, ffn1_w_gate, ffn1_w_up, ffn1_w_down, norm_mix, w_in, shift_mu,
           rwkv_w0, rwkv_w2, rwkv_a0, rwkv_a2, rwkv_g2, rwkv_k_k, rwkv_k_a, rwkv_r_k,
           rwkv_ln_w, rwkv_ln_b, s5_a_re, s5_a_im, s5_log_step, s5_b_re, s5_b_im,
           s5_c_re, s5_c_im, s5_d, s5_w_glu, s5_b_glu, proj_rwkv, proj_s5, w_out,
           norm_ffn2, ffn2_w_gate, ffn2_w_up, ffn2_w_down):
    x = x + 0.5 * _swiglu(_rmsnorm(x, norm_ffn1), ffn1_w_gate, ffn1_w_up, ffn1_w_down)
    h = _rmsnorm(x, norm_mix)
    proj = h @ w_in
    p_rwkv = _centred_shift(proj[..., :SHIFT_COLS], shift_mu)
    y_rwkv = _rwkv_mixer(p_rwkv, rwkv_w0, rwkv_w2, rwkv_a0, rwkv_a2, rwkv_g2, rwkv_k_k,
                         rwkv_k_a, rwkv_r_k, rwkv_ln_w, rwkv_ln_b).astype(x.dtype) @ proj_rwkv
    y_s5 = _s5_mixer(proj[..., U_OFF:GATE_OFF], s5_a_re, s5_a_im, s5_log_step, s5_b_re, s5_b_im,
                     s5_c_re, s5_c_im, s5_d, s5_w_glu, s5_b_glu).astype(x.dtype) @ proj_s5
    gates = jax.nn.sigmoid(proj[..., GATE_OFF:]).reshape(x.shape[0], x.shape[1], N_BRANCH, D_MODEL)
    merged = gates[..., 0, :] * y_rwkv + gates[..., 1, :] * y_s5
    x = x + merged @ w_out
    x = x + 0.5 * _swiglu(_rmsnorm(x, norm_ffn2), ffn2_w_gate, ffn2_w_up, ffn2_w_down)
    return x


def _trunk(x, layer_params, norm_final):
    for l in range(DEPTH):
        x = _layer(x, *[p[l] for p in layer_params])
    return _rmsnorm(x, norm_final)


def setup_inputs(seed: int = 0) -> dict:
    key = jax.random.key(seed)
    ks = iter(jax.random.split(key, 48))
    f32 = jnp.float32

    def nrm(shape, scale):
        return jax.random.normal(next(ks), shape, f32) * scale

    def gain(shape):
        return 1.0 + 0.05 * jax.random.normal(next(ks), shape, f32)

    Lr, P, G, CH = DEPTH, S5_STATE, S5_GROUPS, S5_GROUP_CH
    n_idx = jnp.arange(P, dtype=f32)
    return {
        'x_prompt': nrm((BATCH, SEQ, D_MODEL), 1.0),
        'x_sample': nrm((DEC_BATCH, DEC_SEQ, D_MODEL), 1.0),
        'norm_ffn1': gain((Lr, D_MODEL)),
        'ffn1_w_gate': nrm((Lr, D_MODEL, D_FF), D_MODEL ** -0.5),
        'ffn1_w_up': nrm((Lr, D_MODEL, D_FF), D_MODEL ** -0.5),
        'ffn1_w_down': nrm((Lr, D_FF, D_MODEL), D_FF ** -0.5),
        'norm_mix': gain((Lr, D_MODEL)),
        'w_in': nrm((Lr, D_MODEL, IN_COLS), D_MODEL ** -0.5),
        'shift_mu': jax.random.uniform(next(ks), (Lr, SHIFT_COLS), f32),
        'rwkv_w0': jax.random.uniform(next(ks), (Lr, 2, RWKV_DIM), f32, -6.0, 1.0),
        'rwkv_w2': nrm((Lr, 2, DECAY_LORA, RWKV_DIM), 0.1 * DECAY_LORA ** -0.5),
        'rwkv_a0': nrm((Lr, 2, RWKV_DIM), 0.5),
        'rwkv_a2': nrm((Lr, 2, ICLR_LORA, RWKV_DIM), 0.5 * ICLR_LORA ** -0.5),
        'rwkv_g2': nrm((Lr, GATE_LORA, RWKV_DIM), GATE_LORA ** -0.5),
        'rwkv_k_k': 0.85 + nrm((Lr, RWKV_DIM), 0.05),
        'rwkv_k_a': gain((Lr, RWKV_DIM)),
        'rwkv_r_k': nrm((Lr, RWKV_DIM), 0.1),
        'rwkv_ln_w': gain((Lr, RWKV_DIM)),
        'rwkv_ln_b': nrm((Lr, RWKV_DIM), 0.02),
        's5_a_re': -0.5 + nrm((Lr, 2, G, P), 0.01),
        's5_a_im': math.pi * n_idx + nrm((Lr, 2, G, P), 0.01),
        's5_log_step': jax.random.uniform(next(ks), (Lr, 2, G), f32, math.log(DT_MIN), math.log(DT_MAX)),
        's5_b_re': nrm((Lr, 2, G, P, CH), (2.0 * CH) ** -0.5),
        's5_b_im': nrm((Lr, 2, G, P, CH), (2.0 * CH) ** -0.5),
        's5_c_re': nrm((Lr, G, CH, P), (2.0 * P) ** -0.5),
        's5_c_im': nrm((Lr, G, CH, P), (2.0 * P) ** -0.5),
        's5_d': nrm((Lr, S5_DIM), 1.0),
        's5_w_glu': nrm((Lr, S5_DIM, S5_DIM), S5_DIM ** -0.5),
        's5_b_glu': nrm((Lr, S5_DIM), 0.02),
        'proj_rwkv': nrm((Lr, RWKV_DIM, D_MODEL), RWKV_DIM ** -0.5),
        'proj_s5': nrm((Lr, S5_DIM, D_MODEL), S5_DIM ** -0.5),
        'w_out': nrm((Lr, D_MODEL, D_MODEL), D_MODEL ** -0.5),
        'norm_ffn2': gain((Lr, D_MODEL)),
        'ffn2_w_gate': nrm((Lr, D_MODEL, D_FF), D_MODEL ** -0.5),
        'ffn2_w_up': nrm((Lr, D_MODEL, D_FF), D_MODEL ** -0.5),
        'ffn2_w_down': nrm((Lr, D_FF, D_MODEL), D_FF ** -0.5),
        'norm_final': gain((D_MODEL,)),
    }


def reference(x_prompt, x_sample, norm_ffn1, ffn1_w_gate, ffn1_w_up, ffn1_w_down, norm_mix,
              w_in, shift_mu, rwkv_w0, rwkv_w2, rwkv_a0, rwkv_a2, rwkv_g2, rwkv_k_k, rwkv_k_a,
              rwkv_r_k, rwkv_ln_w, rwkv_ln_b, s5_a_re, s5_a_im, s5_log_step, s5_b_re, s5_b_im,
              s5_c_re, s5_c_im, s5_d, s5_w_glu, s5_b_glu, proj_rwkv, proj_s5, w_out,
              norm_ffn2, ffn2_w_gate, ffn2_w_up, ffn2_w_down, norm_final):
    layer_params = (norm_ffn1, ffn1_w_gate, ffn1_w_up, ffn1_w_down, norm_mix, w_in, shift_mu,
                    rwkv_w0, rwkv_w2, rwkv_a0, rwkv_a2, rwkv_g2, rwkv_k_k, rwkv_k_a, rwkv_r_k,
                    rwkv_ln_w, rwkv_ln_b, s5_a_re, s5_a_im, s5_log_step, s5_b_re, s5_b_im,
                    s5_c_re, s5_c_im, s5_d, s5_w_glu, s5_b_glu, proj_rwkv, proj_s5, w_out,
                    norm_ffn2, ffn2_w_gate, ffn2_w_up, ffn2_w_down)
    y_prompt = _trunk(x_prompt, layer_params, norm_final)
    y_sample = _trunk(x_sample, layer_params, norm_final)
    return (y_prompt, y_sample)
```

```python
import numpy as np
import concourse.bass as bass
import concourse.mybir as mybir

F32 = mybir.dt.float32
BF16 = mybir.dt.bfloat16
AF = mybir.ActivationFunctionType
ALU = mybir.AluOpType
AX = mybir.AxisListType


class Buf:
    __slots__ = ("w", "r", "name")

    def __init__(self, name=""):
        self.w = {}
        self.r = {}
        self.name = name


import os as _os
NO_SELF_WAIT = int(_os.environ.get("K_NOSELF", "0"))


class Sched:
    COMPUTE = ("pe", "act", "dve", "pool")
    NDMA = {"sp": 8, "pool": 4, "act": 2}
    EPOCH = 30000

    def __init__(self, nc, stack):
        self.nc = nc
        self.eng = {"pe": nc.tensor, "act": nc.scalar, "dve": nc.vector,
                    "pool": nc.gpsimd, "sp": nc.sync}
        self.stack = stack
        self.sems = {}
        self.cnt = {}
        self.waited = {e: {} for e in self.eng}
        self.dma_i = {q: 0 for q in self.NDMA}
        self.epoch = {e: 0 for e in self.COMPUTE}
        self.n_wait = 0
        self.n_ins = 0

    def _sem(self, key):
        if key not in self.sems:
            self.sems[key] = self.stack.enter_context(self.nc.semaphore(key.replace("#", "_")))
            self.cnt[key] = 0
        return self.sems[key]

    def _waits(self, eng, toks):
        need = {}
        for d in toks:
            for k, v in d.items():
                if v > need.get(k, 0):
                    need[k] = v
        out = []
        wd = self.waited[eng]
        for k, v in need.items():
            if wd.get(k, 0) < v:
                wd[k] = v
                out.append((k, v))
        return out

    def op(self, eng, fn, reads=(), writes=(), wadd=(), dma=False, pe_acc=False):
        toks = []
        for b in reads:
            toks.append(b.w)
        for b in writes:
            toks.append(b.w)
            toks.append(b.r)
        for b in wadd:
            toks.append(b.r)
        if dma:
            n = self.NDMA[eng]
            i = self.dma_i[eng]
            self.dma_i[eng] = i + 1
            key = "d_%s_%d" % (eng, i % n)
            self._sem(key)
            if self.cnt[key] > 0:
                toks.append({key: self.cnt[key]})
            inc = 16
        else:
            if self.cnt.get("%s#%d" % (eng, self.epoch[eng]), 0) >= self.EPOCH:
                self.epoch[eng] += 1
            key = "%s#%d" % (eng, self.epoch[eng])
            self._sem(key)
            inc = 1
        waits = self._waits(eng, toks)
        if eng == "pe":
            waits = [(k, v) for (k, v) in waits if not k.startswith("pe#")]
        elif NO_SELF_WAIT and not dma:
            waits = [(k, v) for (k, v) in waits if not k.startswith(eng + "#")]
        self.cnt[key] += inc
        val = self.cnt[key]
        eh = self.eng[eng]
        for (k, v) in waits[1:]:
            eh.wait_ge(self.sems[k], v)
        ins = fn(eh)
        if waits:
            ins._wait_ge(self.sems[waits[0][0]], waits[0][1])
        ins.then_inc(self.sems[key], inc)
        self.n_wait += len(waits)
        self.n_ins += 1
        for b in reads:
            if b.r.get(key, 0) < val:
                b.r[key] = val
        for b in writes:
            b.w = {key: val}
            b.r = {}
        for b in wadd:
            b.w = dict(b.w)
            b.w[key] = val
        return (key, val)

    def wait_all(self, eng, toks):
        for (k, v) in self._waits(eng, toks):
            self.eng[eng].wait_ge(self.sems[k], v)

    def final_wait(self, eng, bufs):
        self.wait_all(eng, [b.w for b in bufs])

    def barrier(self):
        allc = {k: v for k, v in self.cnt.items() if v > 0}
        for e in self.eng:
            self.wait_all(e, [allc])
from concourse.bass_utils import run_bass_kernel_spmd
import contextlib

D = 2048
DFF = 5504
NKC = 16
TB = 512
NTT = 4
RW = 1024
S5D = 512
SHIFT = 3456
UOFF = 3456
GOFF = 3968
INC = 8064
FF_GROUPS = [(g * 512, min(512, DFF - g * 512)) for g in range(11)]
WNAMES = ["norm_ffn1", "ffn1_w_gate", "ffn1_w_up", "ffn1_w_down", "norm_mix", "w_in", "shift_mu",
          "rwkv_w0", "rwkv_w2", "rwkv_a0", "rwkv_a2", "rwkv_g2", "rwkv_k_k", "rwkv_k_a", "rwkv_r_k",
          "rwkv_ln_w", "rwkv_ln_b", "s5_a_re", "s5_a_im", "s5_log_step", "s5_b_re", "s5_b_im",
          "s5_c_re", "s5_c_im", "s5_d", "s5_w_glu", "s5_b_glu", "proj_rwkv", "proj_s5", "w_out",
          "norm_ffn2", "ffn2_w_gate", "ffn2_w_up", "ffn2_w_down", "norm_final"]
WSHAPES = {
    "norm_ffn1": [D], "ffn1_w_gate": [D, DFF], "ffn1_w_up": [D, DFF], "ffn1_w_down": [DFF, D],
    "norm_mix": [D], "w_in": [D, INC], "shift_mu": [SHIFT], "rwkv_w0": [2, RW], "rwkv_w2": [2, 64, RW],
    "rwkv_a0": [2, RW], "rwkv_a2": [2, 64, RW], "rwkv_g2": [128, RW], "rwkv_k_k": [RW], "rwkv_k_a": [RW],
    "rwkv_r_k": [RW], "rwkv_ln_w": [RW], "rwkv_ln_b": [RW], "s5_a_re": [2, 32, 64], "s5_a_im": [2, 32, 64],
    "s5_log_step": [2, 32], "s5_b_re": [2, 32, 64, 16], "s5_b_im": [2, 32, 64, 16], "s5_c_re": [32, 16, 64],
    "s5_c_im": [32, 16, 64], "s5_d": [S5D], "s5_w_glu": [S5D, S5D], "s5_b_glu": [S5D],
    "proj_rwkv": [RW, D], "proj_s5": [S5D, D], "w_out": [D, D], "norm_ffn2": [D],
    "ffn2_w_gate": [D, DFF], "ffn2_w_up": [D, DFF], "ffn2_w_down": [DFF, D], "norm_final": [D],
}


_UID = [0]


def U(name):
    _UID[0] += 1
    return "%s_%d" % (name, _UID[0])


class K:
    def dump(self, name, ap, shape, dt, bufs):
        if not self.dbg:
            return
        t = self.nc.dram_tensor("dbg_" + name, shape, dt, kind="ExternalOutput").ap()
        b = Buf()
        self.S.op("sp", lambda e: e.dma_start(out=t, in_=ap), reads=bufs, writes=[b], dma=True)
        self.dbg_bufs.append(b)

    pass


def build(L, stages="ABC", dbg=False):
    nc = bass.Bass("TRN2", target_bir_lowering=False)
    k = K()
    k.nc = nc
    k.L = L
    k.NB = L // TB
    k.dbg = dbg
    k.x = nc.dram_tensor("x", [L, D], F32, kind="ExternalInput").ap()
    k.w = {n: nc.dram_tensor(n, WSHAPES[n], F32, kind="ExternalInput").ap() for n in WNAMES}
    k.y = nc.dram_tensor("y", [L, D], F32, kind="ExternalOutput").ap()
    sk = "ExternalOutput" if dbg else "Internal"
    k.wgu = [nc.dram_tensor("wgu%d" % i, [11, 128, 2, NKC, 512], BF16, kind="Internal").ap() for i in (1, 2)]
    k.wdn = [nc.dram_tensor("wdn%d" % i, [DFF, D], BF16, kind="Internal").ap() for i in (1, 2)]
    k.winA = nc.dram_tensor("winA", [8, 128, NKC, 512], BF16, kind="Internal").ap()
    k.winC = nc.dram_tensor("winC", [8, 128, NKC, 512], BF16, kind="Internal").ap()
    k.wprj = nc.dram_tensor("wprj", [RW + S5D, D], BF16, kind="Internal").ap()
    k.wout = nc.dram_tensor("woutb", [4, 128, NKC, 512], BF16, kind="Internal").ap()
    k.X1 = nc.dram_tensor("X1", [L, D], F32, kind=sk).ap()
    k.PT = nc.dram_tensor("PT", [GOFF, L], F32, kind=sk).ap()
    k.YG = nc.dram_tensor("YG", [RW, L], BF16, kind="Internal").ap()
    k.YS = nc.dram_tensor("YS", [S5D, L], BF16, kind="Internal").ap()
    if dbg:
        k.YGd = nc.dram_tensor("YGd", [RW, L], F32, kind="ExternalOutput").ap()
        k.YSd = nc.dram_tensor("YSd", [S5D, L], F32, kind="ExternalOutput").ap()
    k.b_X1 = Buf(); k.b_PT = Buf(); k.b_YG = Buf(); k.b_YS = Buf(); k.b_y = Buf()
    k.b_conv = {}
    k.dbg_bufs = []

    with contextlib.ExitStack() as st:
        S = Sched(nc, st)
        k.S = S
        k.st = st
        stage0(k)
        if "A" in stages:
            stageA(k)
        S.barrier()
        if "B" in stages:
            stageB(k)
        S.barrier()
        if "C" in stages:
            stageC(k)
        S.barrier()
        S.final_wait("sp", [k.b_y, k.b_X1, k.b_PT, k.b_YG, k.b_YS] + k.dbg_bufs)
    return nc, k


def stage0(k):
    S = k.S
    w = k.w

    import os
    lim = int(os.environ.get("K_CONV_LIMIT", "100000"))
    cnt = [0]

    def conv(name, out_ap, in_ap):
        b = k.b_conv.setdefault(name, Buf())
        cnt[0] += 1
        if cnt[0] > lim:
            return
        S.op("pool", lambda e: e.dma_start(out=out_ap, in_=in_ap), wadd=[b], dma=True)

    def conv_ffn(i, pre):
        for g, (c0, wg) in enumerate(FF_GROUPS):
            for j, nm in enumerate(("_w_gate", "_w_up")):
                src = w[pre + nm].rearrange("(kc p) f -> p kc f", p=128)[:, :, c0:c0 + wg]
                conv("wgu%d_%d" % (i, g), k.wgu[i][g, :, j, :, 0:wg], src)
            src = w[pre + "_w_down"][c0:c0 + wg, :].rearrange("(p c) f -> p c f", p=128)
            dst = k.wdn[i][c0:c0 + wg, :].rearrange("(p c) f -> p c f", p=128)
            conv("wdn%d_%d" % (i, g), dst, src)

    conv_ffn(0, "ffn1")
    win = w["w_in"].rearrange("(kc p) f -> p kc f", p=128)
    for g in range(8):
        wg = min(512, GOFF - g * 512)
        conv("winA_%d" % g, k.winA[g, :, :, 0:wg], win[:, :, g * 512:g * 512 + wg])
    for g in range(8):
        conv("winC_%d" % g, k.winC[g, :, :, :], win[:, :, GOFF + g * 512:GOFF + (g + 1) * 512])
    conv("wprj", k.wprj[0:RW, :].rearrange("(p c) f -> p c f", p=128),
         w["proj_rwkv"].rearrange("(p c) f -> p c f", p=128))
    conv("wprj", k.wprj[RW:RW + S5D, :].rearrange("(p c) f -> p c f", p=128),
         w["proj_s5"].rearrange("(p c) f -> p c f", p=128))
    wo = w["w_out"].rearrange("(kc p) f -> p kc f", p=128)
    for g in range(4):
        conv("wout_%d" % g, k.wout[g, :, :, :], wo[:, :, g * 512:(g + 1) * 512])
    conv_ffn(1, "ffn2")


def consts(k, st):
    nc, S = k.nc, k.S
    c = K()
    c.ident_f = st.enter_context(nc.sbuf_tensor(U("ident_f"), [128, 128], F32))
    c.ident_b = st.enter_context(nc.sbuf_tensor(U("ident_b"), [128, 128], BF16))
    c.b_ident = Buf()
    S.op("pool", lambda e: e.memset(c.ident_f[:, :], 1.0), writes=[c.b_ident])
    S.op("pool", lambda e: e.affine_select(out=c.ident_f[:, :], in_=c.ident_f[:, :], pattern=[[-1, 128]],
                                           compare_op=ALU.is_equal, fill=0.0, base=0, channel_multiplier=1),
         writes=[c.b_ident])
    S.op("dve", lambda e: e.tensor_copy(out=c.ident_b[:, :], in_=c.ident_f[:, :]), reads=[c.b_ident], wadd=[c.b_ident])
    return c


class FFNBufs:
    pass


def alloc_block_bufs(k, st):
    nc = k.nc
    f = FFNBufs()
    f.xa = st.enter_context(nc.sbuf_tensor(U("xa"), [128, NTT, D], F32))
    f.hb = st.enter_context(nc.sbuf_tensor(U("hb"), [128, NTT, D], BF16))
    f.hT = st.enter_context(nc.sbuf_tensor(U("hT"), [128, NKC, TB], BF16))
    f.actT = [st.enter_context(nc.sbuf_tensor(U("actT%d" % i), [128, 4, TB], BF16)) for i in range(2)]
    f.sg = [st.enter_context(nc.sbuf_tensor(U("sg%d" % i), [128, TB], F32)) for i in range(2)]
    f.wgu = [st.enter_context(nc.sbuf_tensor(U("wgu_s%d" % i), [128, 2, NKC, 512], BF16)) for i in range(2)]
    f.wd = [st.enter_context(nc.sbuf_tensor(U("wd_s%d" % i), [128, 4, D], BF16)) for i in range(2)]
    f.ss = st.enter_context(nc.sbuf_tensor(U("ss"), [128, 8], F32))
    f.rstd = st.enter_context(nc.sbuf_tensor(U("rstd"), [128, 8], F32))
    f.gT = st.enter_context(nc.sbuf_tensor(U("gT"), [128, 3, NKC], F32))
    f.pg = [st.enter_context(nc.psum_tensor(U("pg%d" % i), [128, 512], F32)) for i in range(2)]
    f.pu = [st.enter_context(nc.psum_tensor(U("pu%d" % i), [128, 512], F32)) for i in range(2)]
    f.po = [st.enter_context(nc.psum_tensor(U("po%d" % i), [128, 512], F32)) for i in range(2)]
    f.ptr = [st.enter_context(nc.psum_tensor(U("ptr%d" % i), [128, 1024], BF16)) for i in range(2)]
    f.b_xa = [Buf() for _ in range(NTT)]
    f.b_hb = [Buf() for _ in range(NTT)]
    f.b_hT = Buf()
    f.b_actT = [Buf(), Buf()]
    f.b_sg = [Buf(), Buf()]
    f.b_wgu = [Buf(), Buf()]
    f.b_wd = [Buf(), Buf()]
    f.b_ss = Buf(); f.b_rstd = Buf(); f.b_gT = Buf(); f.b_junk = Buf()
    f.b_pg = [Buf(), Buf()]; f.b_pu = [Buf(), Buf()]; f.b_po = [Buf(), Buf()]
    f.b_ptr = [Buf(), Buf()]
    f.n_ptr = 0
    f.n_po = 0
    f.n_gu = 0
    return f


def load_gains(k, f, names):
    S = k.S
    with k.nc.allow_non_contiguous_dma(reason="tiny gain vectors"):
        for i, nm in enumerate(names):
            src = k.w[nm].rearrange("(c p) -> p c", p=128)
            S.op("sp", lambda e: e.dma_start(out=f.gT[:, i, :], in_=src), wadd=[f.b_gT], dma=True)


def rmsnorm_T(k, f, c, gi):
    S = k.S
    for tt in range(NTT):
        S.op("act", lambda e: e.activation(out=f.hb[:, tt, :], in_=f.xa[:, tt, :], func=AF.Square,
                                           accum_out=f.ss[:, tt:tt + 1]),
             reads=[f.b_xa[tt]], writes=[f.b_hb[tt]], wadd=[f.b_ss])
    import os
    lv = int(os.environ.get("K_RMS", "99"))
    if lv <= 1:
        return
    S.op("act", lambda e: e.activation(out=f.rstd[:, 0:NTT], in_=f.ss[:, 0:NTT], func=AF.Sqrt, scale=1.0 / D, bias=1e-6),
         reads=[f.b_ss], writes=[f.b_rstd])
    S.op("dve", lambda e: e.reciprocal(out=f.rstd[:, 0:NTT], in_=f.rstd[:, 0:NTT]), reads=[f.b_rstd], writes=[f.b_rstd])
    if lv <= 2:
        return
    for tt in range(NTT):
        eng = "act" if tt % 2 == 0 else "dve"
        if eng == "act":
            S.op("act", lambda e: e.activation(out=f.hb[:, tt, :], in_=f.xa[:, tt, :], func=AF.Copy,
                                               scale=f.rstd[:, tt:tt + 1]),
                 reads=[f.b_xa[tt], f.b_rstd], writes=[f.b_hb[tt]])
        else:
            S.op("dve", lambda e: e.tensor_scalar(out=f.hb[:, tt, :], in0=f.xa[:, tt, :], scalar1=f.rstd[:, tt:tt + 1],
                                                  scalar2=None, op0=ALU.mult),
                 reads=[f.b_xa[tt], f.b_rstd], writes=[f.b_hb[tt]])
    if lv <= 3:
        return
    first = True
    for kc in range(int(os.environ.get("K_NKC", NKC))):
        s = f.n_ptr % 2
        f.n_ptr += 1
        for tt in range(NTT):
            S.op("pe", lambda e: e.transpose(out=f.ptr[s][:, tt * 128:(tt + 1) * 128],
                                             in_=f.hb[:, tt, kc * 128:(kc + 1) * 128], identity=c.ident_b[:, :]),
                 reads=[f.b_hb[tt], c.b_ident], **({"writes": [f.b_ptr[s]]} if tt == 0 else {"wadd": [f.b_ptr[s]]}))
        if lv <= 4:
            continue
        eng = "act" if kc % 2 == 0 else "dve"
        eng = os.environ.get("K_EVAC", eng)
        wr = {"writes": [f.b_hT]} if first else {"wadd": [f.b_hT]}
        first = False
        if eng == "act":
            S.op("act", lambda e: e.activation(out=f.hT[:, kc, :], in_=f.ptr[s][:, 0:512], func=AF.Copy,
                                               scale=f.gT[:, gi, kc:kc + 1]),
                 reads=[f.b_ptr[s], f.b_gT], **wr)
        else:
            S.op("dve", lambda e: e.tensor_scalar(out=f.hT[:, kc, :], in0=f.ptr[s][:, 0:512], scalar1=f.gT[:, gi, kc:kc + 1],
                                                  scalar2=None, op0=ALU.mult),
                 reads=[f.b_ptr[s], f.b_gT], **wr)


def ffn(k, f, wi):
    S = k.S
    ng = len(FF_GROUPS)

    def load_gu(g):
        s = g % 2
        c0, wg = FF_GROUPS[g]
        S.op("sp", lambda e: e.dma_start(out=f.wgu[s][:, :, :, 0:wg], in_=k.wgu[wi][g, :, :, :, 0:wg]),
             reads=[k.b_conv["wgu%d_%d" % (wi, g)]], writes=[f.b_wgu[s]], dma=True)

    def load_d(g):
        s = g % 2
        c0, wg = FF_GROUPS[g]
        ncg = wg // 128
        src = k.wdn[wi].rearrange("(c p) f -> p c f", p=128)[:, c0 // 128:c0 // 128 + ncg, :]
        S.op("sp", lambda e: e.dma_start(out=f.wd[s][:, 0:ncg, :], in_=src),
             reads=[k.b_conv["wdn%d_%d" % (wi, g)]], writes=[f.b_wd[s]], dma=True)

    def gate_up(g):
        s = g % 2
        c0, wg = FF_GROUPS[g]
        ncg = wg // 128
        for c in range(ncg):
            q = f.n_gu % 2
            f.n_gu += 1
            for j, (pt, bp) in enumerate(((f.pg[q], f.b_pg[q]), (f.pu[q], f.b_pu[q]))):
                for kc in range(NKC):
                    S.op("pe", lambda e: e.matmul(pt[:, :], lhsT=f.wgu[s][:, j, kc, c * 128:(c + 1) * 128],
                                                  rhs=f.hT[:, kc, :], start=(kc == 0), stop=(kc == NKC - 1)),
                         reads=[f.b_wgu[s], f.b_hT], **({"writes": [bp]} if kc == 0 else {"wadd": [bp]}))
            S.op("act", lambda e: e.activation(out=f.sg[q][:, :], in_=f.pg[q][:, :], func=AF.Silu),
                 reads=[f.b_pg[q]], writes=[f.b_sg[q]])
            S.op("dve", lambda e: e.tensor_tensor(out=f.actT[s][:, c, :], in0=f.sg[q][:, :], in1=f.pu[q][:, :], op=ALU.mult),
                 reads=[f.b_sg[q], f.b_pu[q]], **({"writes": [f.b_actT[s]]} if c == 0 else {"wadd": [f.b_actT[s]]}))

    def down(g):
        s = g % 2
        c0, wg = FF_GROUPS[g]
        ncg = wg // 128
        for tt in range(NTT):
            for nb in range(4):
                q = f.n_po % 2
                f.n_po += 1
                for c in range(ncg):
                    S.op("pe", lambda e: e.matmul(f.po[q][:, :], lhsT=f.actT[s][:, c, tt * 128:(tt + 1) * 128],
                                                  rhs=f.wd[s][:, c, nb * 512:(nb + 1) * 512], start=(c == 0), stop=(c == ncg - 1)),
                         reads=[f.b_actT[s], f.b_wd[s]], **({"writes": [f.b_po[q]]} if c == 0 else {"wadd": [f.b_po[q]]}))
                S.op("dve", lambda e: e.tensor_tensor(out=f.xa[:, tt, nb * 512:(nb + 1) * 512], in0=f.xa[:, tt, nb * 512:(nb + 1) * 512],
                                                      in1=f.po[q][:, :], op=ALU.add),
                     reads=[f.b_po[q]], writes=[f.b_xa[tt]])

    load_gu(0)
    load_d(0)
    for g in range(ng + 1):
        if g < ng:
            if g + 1 < ng:
                load_gu(g + 1)
            gate_up(g)
        if g >= 1:
            down(g - 1)
        if g + 1 < ng:
            load_d(g + 1)
    for tt in range(NTT):
        S.op("act", lambda e: e.activation(out=f.xa[:, tt, :], in_=f.xa[:, tt, :], func=AF.Copy, scale=0.5),
             reads=[], writes=[f.b_xa[tt]])


def load_x_block(k, f, src, b, b_src=None):
    S = k.S
    for tt in range(NTT):
        r0 = b * TB + tt * 128
        S.op("sp", lambda e: e.dma_start(out=f.xa[:, tt, :], in_=src[r0:r0 + 128, :]),
             reads=([b_src] if b_src is not None else []), writes=[f.b_xa[tt]], dma=True)


def double_x(k, f):
    S = k.S
    for tt in range(NTT):
        S.op("pool", lambda e: e.tensor_scalar(out=f.xa[:, tt, :], in0=f.xa[:, tt, :], scalar1=2.0, scalar2=None, op0=ALU.mult),
             reads=[f.b_hb[tt]], writes=[f.b_xa[tt]])


def stageA(k):
    nc, S = k.nc, k.S
    with contextlib.ExitStack() as st:
        c = consts(k, st)
        f = alloc_block_bufs(k, st)
        stg = [st.enter_context(nc.sbuf_tensor(U("stg%d" % i), [128, 4, TB], F32)) for i in range(2)]
        b_stg = [Buf(), Buf()]
        load_gains(k, f, ["norm_ffn1", "norm_mix"])
        import os
        stop = int(os.environ.get("K_STOP", "99"))
        for b in range(k.NB):
            load_x_block(k, f, k.x, b)
            if stop <= 1:
                break
            rmsnorm_T(k, f, c, 0)
            if stop <= 2:
                break
            double_x(k, f)
            ffn(k, f, 0)
            if stop <= 3:
                break
            for tt in range(NTT):
                r0 = b * TB + tt * 128
                S.op("sp", lambda e: e.dma_start(out=k.X1[r0:r0 + 128, :], in_=f.xa[:, tt, :]),
                     reads=[f.b_xa[tt]], wadd=[k.b_X1], dma=True)
            rmsnorm_T(k, f, c, 1)
            for g in range(8):
                wg = min(512, GOFF - g * 512)
                ncg = wg // 128
                s = g % 2
                S.op("sp", lambda e: e.dma_start(out=f.wgu[s][:, 0, :, 0:wg], in_=k.winA[g, :, :, 0:wg]),
                     reads=[k.b_conv["winA_%d" % g]], writes=[f.b_wgu[s]], dma=True)
                for cc in range(ncg):
                    q = f.n_gu % 2
                    f.n_gu += 1
                    for kc in range(NKC):
                        S.op("pe", lambda e: e.matmul(f.pg[q][:, :], lhsT=f.wgu[s][:, 0, kc, cc * 128:(cc + 1) * 128],
                                                      rhs=f.hT[:, kc, :], start=(kc == 0), stop=(kc == NKC - 1)),
                             reads=[f.b_wgu[s], f.b_hT], **({"writes": [f.b_pg[q]]} if kc == 0 else {"wadd": [f.b_pg[q]]}))
                    S.op("act", lambda e: e.activation(out=stg[s][:, cc, :], in_=f.pg[q][:, :], func=AF.Copy),
                         reads=[f.b_pg[q]], **({"writes": [b_stg[s]]} if cc == 0 else {"wadd": [b_stg[s]]}))
                dst = k.PT[g * 512:g * 512 + wg, b * TB:(b + 1) * TB].rearrange("(c p) t -> p c t", p=128)
                S.op("sp", lambda e: e.dma_start(out=dst, in_=stg[s][:, 0:ncg, :]),
                     reads=[b_stg[s]], wadd=[k.b_PT], dma=True)
        S.barrier()


KAPPA = 0.6065306597126334
SEG = 512
CH = 128


class Banks:
    def __init__(self, k, st, n, prefix):
        self.t = [st.enter_context(k.nc.psum_tensor(U("%s%d" % (prefix, i)), [128, 512], F32)) for i in range(n)]
        self.tb = None
        self.b = [Buf() for _ in range(n)]
        self.i = 0

    def next(self):
        j = self.i % len(self.t)
        self.i += 1
        return self.t[j], self.b[j]


def mm_group(S, out_ap, bank_buf, pairs, extra_reads=()):
    n = len(pairs)
    for i, (lhsT, rhs, reads) in enumerate(pairs):
        S.op("pe", lambda e: e.matmul(out_ap, lhsT=lhsT, rhs=rhs, start=(i == 0), stop=(i == n - 1)),
             reads=list(reads) + list(extra_reads), wadd=[bank_buf])


def col_load(k, dst, name, sl=None):
    src = k.w[name] if sl is None else sl
    return src


def stageB(k):
    nc, S, L = k.nc, k.S, k.L
    NCH = L // CH
    NSEG = L // SEG
    with contextlib.ExitStack() as st:
        c = consts(k, st)
        sb = lambda name, shape, dt=F32: st.enter_context(nc.sbuf_tensor(U(name), shape, dt))
        wlowT = sb("wlowT", [128, L], BF16); alowT = sb("alowT", [128, L], BF16); glowT = sb("glowT", [128, L], BF16)
        b_low = Buf()
        w2s = sb("w2s", [128, RW], BF16); a2s = sb("a2s", [128, RW], BF16); g2s = sb("g2s", [128, RW], BF16)
        b_lw = Buf()
        muT = sb("muT", [128, 27]); omu = sb("omu", [128, 27]); hmu = sb("hmu", [128, 27])
        w0T = sb("w0T", [128, 2, 8]); a0T = sb("a0T", [128, 2, 8])
        pcol = sb("pcol", [128, 6, 8])
        b_par = Buf()
        ones_blk = sb("ones_blk", [128, 128])
        blk_b = sb("blk_b", [128, 128], F32)
        maskS_tj = sb("maskS_tj", [128, 4, 128], BF16)
        maskS_jt = sb("maskS_jt", [128, 4, 128], BF16)
        maskI_jt = sb("maskI_jt", [128, 4, 128], BF16)
        ident4 = sb("ident4", [128, 4, 128], BF16)
        eps12 = sb("eps12", [128, 2])
        rmask = sb("rmask", [128, SEG])
        rmaskB = sb("rmaskB", [128, SEG])
        b_cst = Buf()
        mtmp = sb("mtmp", [128, 4, 128])

        def mk_mask(dst, pat, cm, base, op):
            S.op("pool", lambda e: e.memset(mtmp[:, :, :], 1.0), writes=[b_cst])
            S.op("pool", lambda e: e.affine_select(out=mtmp[:, :, :], in_=mtmp[:, :, :], pattern=[[0, 4], [pat, 128]],
                                                   compare_op=op, fill=0.0, base=base, channel_multiplier=cm), writes=[b_cst])
            S.op("pool", lambda e: e.tensor_copy(out=dst, in_=mtmp[:, :, :]), writes=[b_cst])

        mk_mask(maskS_tj[:, :, :], -1, 1, -1, ALU.is_ge)
        mk_mask(maskS_jt[:, :, :], 1, -1, -1, ALU.is_ge)
        mk_mask(maskI_jt[:, :, :], 1, -1, 0, ALU.is_ge)
        mk_mask(ident4[:, :, :], -1, 1, 0, ALU.is_equal)
        S.op("pool", lambda e: e.memset(ones_blk[:, :], 0.0), writes=[b_cst])
        S.op("pool", lambda e: e.memset(ones_blk[0:64, 0:64], 1.0), writes=[b_cst])
        S.op("pool", lambda e: e.memset(ones_blk[64:128, 64:128], 1.0), writes=[b_cst])
        S.op("pool", lambda e: e.tensor_copy(out=blk_b[:, :], in_=ones_blk[:, :]), writes=[b_cst])
        S.op("pool", lambda e: e.tensor_copy(out=blk_b[:, :], in_=ones_blk[:, :]), writes=[b_cst])
        S.op("pool", lambda e: e.memset(eps12[:, 0:1], 1e-12), writes=[b_cst])
        S.op("pool", lambda e: e.memset(eps12[:, 1:2], 64e-5), writes=[b_cst])
        S.op("pool", lambda e: e.memset(rmask[:, :], 1.0), writes=[b_cst])
        S.op("pool", lambda e: e.memset(rmask[:, 0:SEG:CH], 0.0), writes=[b_cst])
        S.op("pool", lambda e: e.memset(rmaskB[:, :], 1.0), writes=[b_cst])
        S.op("pool", lambda e: e.memset(rmaskB[:, CH - 1:SEG:CH], 0.0), writes=[b_cst])
        with nc.allow_non_contiguous_dma(reason="tiny param vectors"):
            S.op("sp", lambda e: e.dma_start(out=muT[:, :], in_=k.w["shift_mu"].rearrange("(c p) -> p c", p=128)), wadd=[b_par], dma=True)
            for d in range(2):
                S.op("sp", lambda e: e.dma_start(out=w0T[:, d, :], in_=k.w["rwkv_w0"][d].rearrange("(c p) -> p c", p=128)), wadd=[b_par], dma=True)
                S.op("sp", lambda e: e.dma_start(out=a0T[:, d, :], in_=k.w["rwkv_a0"][d].rearrange("(c p) -> p c", p=128)), wadd=[b_par], dma=True)
            for i, nm in enumerate(["rwkv_k_k", "rwkv_k_a", "rwkv_r_k", "rwkv_ln_w", "rwkv_ln_b"]):
                S.op("sp", lambda e: e.dma_start(out=pcol[:, i, :], in_=k.w[nm].rearrange("(c p) -> p c", p=128)), wadd=[b_par], dma=True)
        S.op("dve", lambda e: e.tensor_scalar(out=omu[:, :], in0=muT[:, :], scalar1=-1.0, scalar2=1.0, op0=ALU.mult, op1=ALU.add), reads=[b_par], wadd=[b_par])
        S.op("dve", lambda e: e.tensor_scalar(out=hmu[:, :], in0=muT[:, :], scalar1=0.5, scalar2=None, op0=ALU.mult), reads=[b_par], wadd=[b_par])
        S.op("dve", lambda e: e.tensor_scalar(out=pcol[:, 5, :], in0=pcol[:, 1, :], scalar1=-1.0, scalar2=1.0, op0=ALU.mult, op1=ALU.add), reads=[b_par], wadd=[b_par])
        S.op("pool", lambda e: e.dma_start(out=w2s[:, :], in_=k.w["rwkv_w2"].rearrange("d r c -> (d r) c")), wadd=[b_lw], dma=True)
        S.op("pool", lambda e: e.dma_start(out=a2s[:, :], in_=k.w["rwkv_a2"].rearrange("d r c -> (d r) c")), wadd=[b_lw], dma=True)
        S.op("pool", lambda e: e.dma_start(out=g2s[:, :], in_=k.w["rwkv_g2"][:, :]), wadd=[b_lw], dma=True)

        import os
        bstop = int(os.environ.get("K_BSTOP", "99"))
        nhp = int(os.environ.get("K_NHP", "8"))
        banks = Banks(k, st, 7, "bk")
        pbt = st.enter_context(nc.psum_tensor(U("pbt"), [128, 1024], BF16))
        b_pbt = Buf()

        T = {}
        BT = {}
        for nm in ["raw", "t", "u", "r", "k", "v", "kk", "sq", "rn", "lwn", "a", "cs", "e1", "e2", "e3", "tmp", "kd", "kds"]:
            T[nm] = sb("tp_" + nm, [128, SEG + 2])
            BT[nm] = Buf()
        for nm, al in (("rk", "sq"), ("t1", "tmp"), ("m", "rn")):
            T[nm] = T[al]
            BT[nm] = BT[al]

        def shifted(dst, dbuf, cc, s0):
            lo = max(0, s0 - 1)
            hi = min(L, s0 + SEG + 1)
            off = lo - (s0 - 1)
            raw = T["raw"]
            if s0 == 0:
                S.op("pool", lambda e: e.memset(raw[:, 0:1], 0.0), writes=[BT["raw"]])
            if s0 + SEG == L:
                S.op("pool", lambda e: e.memset(raw[:, SEG + 1:SEG + 2], 0.0), writes=[BT["raw"]])
            S.op("sp", lambda e: e.dma_start(out=raw[:, off:off + hi - lo], in_=k.PT[cc * 128:(cc + 1) * 128, lo:hi]),
                 reads=[k.b_PT], writes=[BT["raw"]], dma=True)
            S.op("pool", lambda e: e.tensor_tensor(out=T["t"][:, 0:SEG], in0=raw[:, 0:SEG], in1=raw[:, 2:SEG + 2], op=ALU.add),
                 reads=[BT["raw"]], writes=[BT["t"]])
            S.op("act", lambda e: e.activation(out=T["u"][:, 0:SEG], in_=raw[:, 1:SEG + 1], func=AF.Copy, scale=omu[:, cc:cc + 1]),
                 reads=[BT["raw"], b_par], writes=[BT["u"]])
            S.op("dve", lambda e: e.scalar_tensor_tensor(out=dst, in0=T["t"][:, 0:SEG], scalar=hmu[:, cc:cc + 1], in1=T["u"][:, 0:SEG],
                                                         op0=ALU.mult, op1=ALU.add),
                 reads=[BT["t"], BT["u"], b_par], writes=[dbuf])

        for sg in range(NSEG if bstop >= 2 else 0):
            s0 = sg * SEG
            for cc, dstT, fn in ((24, wlowT, AF.Tanh), (25, alowT, AF.Copy), (26, glowT, AF.Sigmoid)):
                shifted(T["r"][:, 0:SEG], BT["r"], cc, s0)
                S.op("act", lambda e: e.activation(out=dstT[:, s0:s0 + SEG], in_=T["r"][:, 0:SEG], func=fn),
                     reads=[BT["r"]], wadd=[b_low])

        RH = [sb("RH%d" % d, [128, L], BF16) for d in range(2)]
        AH = [sb("AH%d" % d, [128, L], BF16) for d in range(2)]
        BH = [sb("BH%d" % d, [128, L], BF16) for d in range(2)]
        KH = [sb("KH%d" % d, [128, L], BF16) for d in range(2)]
        Vb = sb("Vb", [128, L], BF16)
        VbR = sb("VbR", [128, L], BF16)
        b_fm = [Buf() for _ in range(NSEG)]
        YT = sb("YT", [128, L])
        b_YT = [Buf() for _ in range(NCH)]
        bonT = sb("bonT", [128, L], BF16)
        b_bon = Buf()
        PcAll = sb("PcAll", [128, NCH, 2])
        b_pc = Buf()
        SW = sb("SW", [128, 2, 128]); SWb = sb("SWb", [128, 2, 128], BF16)
        b_SW = Buf(); b_SWb = Buf()
        tmpS = sb("tmpS", [128, 2, 128]); b_tmpS = Buf()
        NSL = 2
        Pm = [sb("Pm%d" % i, [128, 4, 128], BF16) for i in range(2)]
        Qm = [sb("Qm%d" % i, [128, 4, 128], BF16) for i in range(2)]
        Rm = [sb("Rm%d" % i, [128, 4, 128], BF16) for i in range(2)]
        b_Pm = [Buf(), Buf()]; b_Qm = [Buf(), Buf()]; b_Rm = [Buf(), Buf()]
        AAK = [sb("AAK%d" % i, [128, 4, 128], BF16) for i in range(NSL)]
        ARB = [sb("ARB%d" % i, [128, 4, 128], BF16) for i in range(NSL)]
        ARK = [sb("ARK%d" % i, [128, 4, 128], BF16) for i in range(NSL)]
        RF = [sb("RF%d" % i, [128, 4, 128], BF16) for i in range(NSL)]
        TOK = [sb("TOK%d" % i, [128, 4, 128], BF16) for i in range(NSL)]
        VZ = [sb("VZ%d" % i, [128, 2, 2, 128], BF16) for i in range(NSL)]
        b_AA = [[Buf() for _ in range(6)] for _ in range(NSL)]
        XS = sb("XS", [128, 2, 128], BF16); b_XS = Buf()
        USZ = sb("USZ", [128, 2, 2, 128], BF16); b_USZ = Buf()
        for i in range(NSL):
            S.op("pool", lambda e: e.memset(VZ[i][:, :, :, :], 0.0), writes=[b_AA[i][5]])
        S.op("pool", lambda e: e.memset(USZ[:, :, :, :], 0.0), writes=[b_USZ])

        def fm(arr, d, n):
            a = arr[:, n * CH:(n + 1) * CH]
            return a if d == 0 else a[:, ::-1]

        def pm(arr, s):
            return arr[:, s * CH:(s + 1) * CH]

        for hp in range(nhp if bstop >= 3 else 0):
            hc = slice(hp * 128, (hp + 1) * 128)
            for sg in range(NSEG):
                s0 = sg * SEG
                sl = slice(s0, s0 + SEG)
                W = slice(0, SEG)
                shifted(T["r"][:, W], BT["r"], hp, s0)
                shifted(T["k"][:, W], BT["k"], 8 + hp, s0)
                shifted(T["v"][:, W], BT["v"], 16 + hp, s0)
                def osl(arr, d):
                    return arr[:, sl] if d == 0 else arr[:, L - s0 - SEG:L - s0][:, ::-1]
                S.op("act", lambda e: e.activation(out=Vb[:, sl], in_=T["v"][:, W], func=AF.Copy), reads=[BT["v"]], writes=[b_fm[sg]])
                S.op("pool", lambda e: e.tensor_copy(out=osl(VbR, 1), in_=T["v"][:, W]), reads=[BT["v"]], wadd=[b_fm[sg]])
                S.op("act", lambda e: e.activation(out=T["kk"][:, W], in_=T["k"][:, W], func=AF.Copy, scale=pcol[:, 0, hp:hp + 1]),
                     reads=[BT["k"], b_par], writes=[BT["kk"]])
                S.op("act", lambda e: e.activation(out=T["sq"][:, W], in_=T["kk"][:, W], func=AF.Square), reads=[BT["kk"]], writes=[BT["sq"]])
                pt, pbuf = banks.next()
                S.op("pe", lambda e: e.matmul(pt[:, :], lhsT=ones_blk[:, :], rhs=T["sq"][:, W], start=True, stop=True),
                     reads=[BT["sq"], b_cst], writes=[pbuf])
                S.op("act", lambda e: e.activation(out=T["rn"][:, W], in_=pt[:, :], func=AF.Ln, bias=eps12[:, 0:1]), reads=[b_cst], writes=[pbuf, BT["rn"]])
                S.op("act", lambda e: e.activation(out=T["rn"][:, W], in_=T["rn"][:, W], func=AF.Exp, scale=-0.5), writes=[BT["rn"]])
                S.op("dve", lambda e: e.tensor_tensor(out=T["kk"][:, W], in0=T["kk"][:, W], in1=T["rn"][:, W], op=ALU.mult),
                     reads=[BT["rn"]], writes=[BT["kk"]])
                for d in range(2):
                    dr = slice(d * 64, (d + 1) * 64)
                    pt, pbuf = banks.next()
                    S.op("pe", lambda e: e.matmul(pt[:, :], lhsT=w2s[dr, hc], rhs=wlowT[dr, sl], start=True, stop=True),
                         reads=[b_lw, b_low], writes=[pbuf])
                    S.op("act", lambda e: e.activation(out=T["lwn"][:, W], in_=pt[:, :], func=AF.Sigmoid, bias=w0T[:, d, hp:hp + 1]),
                         reads=[b_par], writes=[pbuf, BT["lwn"]])
                    pt, pbuf = banks.next()
                    S.op("pe", lambda e: e.matmul(pt[:, :], lhsT=a2s[dr, hc], rhs=alowT[dr, sl], start=True, stop=True),
                         reads=[b_lw, b_low], writes=[pbuf])
                    S.op("act", lambda e: e.activation(out=T["a"][:, W], in_=pt[:, :], func=AF.Sigmoid, bias=a0T[:, d, hp:hp + 1]),
                         reads=[b_par], writes=[pbuf, BT["a"]])
                    rv = (lambda ap: ap) if d == 0 else (lambda ap: ap[:, ::-1])
                    S.op("dve", lambda e: e.tensor_tensor_scan(out=rv(T["cs"][:, W]), data0=rv((rmask if d == 0 else rmaskB)[:, :]), data1=rv(T["lwn"][:, W]),
                                                               initial=0.0, op0=ALU.mult, op1=ALU.add),
                         reads=[BT["lwn"], b_cst], writes=[BT["cs"]])
                    S.op("act", lambda e: e.activation(out=T["e1"][:, W], in_=T["cs"][:, W], func=AF.Exp, scale=-KAPPA), reads=[BT["cs"]], writes=[BT["e1"]])
                    S.op("act", lambda e: e.activation(out=T["e2"][:, W], in_=T["cs"][:, W], func=AF.Exp, scale=KAPPA), reads=[BT["cs"]], writes=[BT["e2"]])
                    S.op("dve", lambda e: e.tensor_tensor(out=T["tmp"][:, W], in0=T["cs"][:, W], in1=T["lwn"][:, W], op=ALU.subtract),
                         reads=[BT["cs"], BT["lwn"]], writes=[BT["tmp"]])
                    S.op("act", lambda e: e.activation(out=T["e3"][:, W], in_=T["tmp"][:, W], func=AF.Exp, scale=-KAPPA), reads=[BT["tmp"]], writes=[BT["e3"]])
                    n0 = s0 // CH
                    if d == 0:
                        S.op("act", lambda e: e.activation(func=AF.Copy, out=PcAll[:, n0:n0 + 4, 0], in_=T["e1"][:, CH - 1:SEG:CH]), reads=[BT["e1"]], wadd=[b_pc])
                    else:
                        st0 = NCH - 1 - (n0 + 3)
                        S.op("act", lambda e: e.activation(func=AF.Copy, out=PcAll[:, st0:st0 + 4, 1][:, ::-1], in_=T["e1"][:, 0:SEG:CH]), reads=[BT["e1"]], wadd=[b_pc])
                    S.op("dve", lambda e: e.tensor_tensor(out=osl(RH[d], d), in0=T["r"][:, W], in1=T["e1"][:, W], op=ALU.mult),
                         reads=[BT["r"], BT["e1"]], wadd=[b_fm[sg]])
                    S.op("dve", lambda e: e.scalar_tensor_tensor(out=osl(AH[d], d), in0=T["kk"][:, W], scalar=-1.0, in1=T["e3"][:, W], op0=ALU.mult, op1=ALU.mult),
                         reads=[BT["kk"], BT["e3"]], wadd=[b_fm[sg]])
                    S.op("dve", lambda e: e.tensor_tensor(out=T["t1"][:, W], in0=T["kk"][:, W], in1=T["a"][:, W], op=ALU.mult),
                         reads=[BT["kk"], BT["a"]], writes=[BT["t1"]])
                    S.op("dve", lambda e: e.tensor_tensor(out=osl(BH[d], d), in0=T["t1"][:, W], in1=T["e2"][:, W], op=ALU.mult),
                         reads=[BT["t1"], BT["e2"]], wadd=[b_fm[sg]])
                    S.op("act", lambda e: e.activation(out=T["m"][:, W], in_=T["a"][:, W], func=AF.Identity, scale=pcol[:, 1, hp:hp + 1], bias=pcol[:, 5, hp:hp + 1]),
                         reads=[BT["a"], b_par], writes=[BT["m"]])
                    S.op("dve", lambda e: e.tensor_tensor(out=T["kd"][:, W], in0=T["k"][:, W], in1=T["m"][:, W], op=ALU.mult),
                         reads=[BT["k"], BT["m"]], writes=[BT["kd"]])
                    S.op("dve", lambda e: e.tensor_tensor(out=osl(KH[d], d), in0=T["kd"][:, W], in1=T["e2"][:, W], op=ALU.mult),
                         reads=[BT["kd"], BT["e2"]], wadd=[b_fm[sg]])
                    if d == 0:
                        S.op("pool", lambda e: e.tensor_copy(out=T["kds"][:, W], in_=T["kd"][:, W]), reads=[BT["kd"]], writes=[BT["kds"]])
                    else:
                        S.op("pool", lambda e: e.tensor_tensor(out=T["kds"][:, W], in0=T["kds"][:, W], in1=T["kd"][:, W], op=ALU.add),
                             reads=[BT["kd"]], writes=[BT["kds"]])
                S.op("dve", lambda e: e.scalar_tensor_tensor(out=T["rk"][:, W], in0=T["r"][:, W], scalar=pcol[:, 2, hp:hp + 1], in1=T["kds"][:, W],
                                                             op0=ALU.mult, op1=ALU.mult), reads=[BT["r"], BT["kds"], b_par], writes=[BT["rk"]])
                pt, pbuf = banks.next()
                S.op("pe", lambda e: e.matmul(pt[:, :], lhsT=ones_blk[:, :], rhs=T["rk"][:, W], start=True, stop=True),
                     reads=[BT["rk"], b_cst], writes=[pbuf])
                S.op("dve", lambda e: e.tensor_tensor(out=bonT[:, sl], in0=pt[:, :], in1=T["v"][:, W], op=ALU.mult),
                     reads=[BT["v"]], writes=[pbuf], wadd=[b_bon])

            if hp == 0:
                for d in range(2):
                    k.dump("RH%d" % d, RH[d][:, :], [128, L], BF16, b_fm)
                    k.dump("AH%d" % d, AH[d][:, :], [128, L], BF16, b_fm)
                    k.dump("BH%d" % d, BH[d][:, :], [128, L], BF16, b_fm)
                    k.dump("KH%d" % d, KH[d][:, :], [128, L], BF16, b_fm)
                k.dump("Pc", PcAll[:, :, :], [128, NCH, 2], F32, [b_pc])
                k.dump("bon", bonT[:, :], [128, L], BF16, [b_bon])
            if bstop <= 3:
                continue
            S.op("pool", lambda e: e.memset(SW[:, :, :], 0.0), writes=[b_SW])
            S.op("pool", lambda e: e.memset(SWb[:, :, :], 0.0), writes=[b_SWb])
            UNITS = [(d, h) for d in range(2) for h in range(2)]

            def phase1(s):
                z = s % NSL
                nn = [s, NCH - 1 - s]
                segs = [b_fm[nn[0] * CH // SEG], b_fm[nn[1] * CH // SEG]]

                def five(lh_arr, rh_arr, mask, dst, dbuf):
                    for h in range(2):
                        hk = slice(h * 64, (h + 1) * 64)
                        pt, pbuf = banks.next()
                        for d in range(2):
                            lh = pm(lh_arr[d], s)[hk, :]
                            rh = pm(rh_arr[d], s)[hk, :]
                            S.op("pe", lambda e: e.matmul(pt[:, d * 128:(d + 1) * 128], lhsT=lh, rhs=rh, start=True, stop=True),
                                 reads=[segs[d]], **({"writes": [pbuf]} if d == 0 else {"wadd": [pbuf]}))
                        S.op("dve", lambda e: e.tensor_tensor(out=dst[:, h::2, :], in0=pt[:, 0:256].rearrange("p (u t) -> p u t", u=2), in1=mask[:, 0:2, :], op=ALU.mult),
                             reads=[b_cst], **({"writes": [pbuf, dbuf]} if h == 0 else {"writes": [pbuf], "wadd": [dbuf]}))

                five(AH, BH, maskS_tj, Pm[0], b_Pm[0])
                yield
                five(BH, AH, maskS_jt, Qm[0], b_Qm[0])
                yield
                five(KH, AH, maskS_jt, AAK[z], b_AA[z][0])
                yield
                five(BH, RH, maskI_jt, ARB[z], b_AA[z][1])
                yield
                five(KH, RH, maskI_jt, ARK[z], b_AA[z][2])
                yield
                first = True
                for i, (arr, d) in enumerate(((BH, 0), (BH, 1), (KH, 0), (KH, 1), (None, 0), (None, 1))):
                    src = pm((Vb if d == 0 else VbR) if arr is None else arr[d], s)
                    S.op("pe", lambda e: e.transpose(out=pbt[:, i * 128:(i + 1) * 128], in_=src, identity=c.ident_b[:, :]),
                         reads=[segs[d], c.b_ident], **({"writes": [b_pbt]} if first else {"wadd": [b_pbt]}))
                    first = False
                S.op("act", lambda e: e.activation(out=TOK[z][:, :, :], in_=pbt[:, 0:512].rearrange("p (u t) -> p u t", u=4), func=AF.Copy),
                     writes=[b_pbt, b_AA[z][4]])
                for d in range(2):
                    src = pbt[:, 512 + d * 128:512 + (d + 1) * 128].rearrange("p (h v) -> p h v", h=2)
                    dst = VZ[z][:, d, :, :].rearrange("p h (g v) -> p h g v", g=2)
                    for h in range(2):
                        S.op("act", lambda e: e.activation(out=dst[:, h, h, :], in_=src[:, h, :], func=AF.Copy),
                             writes=[b_pbt], wadd=[b_AA[z][5]])
                yield
                S.op("pool", lambda e: e.tensor_tensor(out=Rm[0][:, :, :], in0=Qm[0][:, :, :], in1=ident4[:, :, :], op=ALU.add),
                     reads=[b_Qm[0], b_cst], writes=[b_Rm[0]])
                cur = 0
                for lv in range(1, 7):
                    nx = 1 - cur
                    last = (lv == 6)
                    pt, pbuf = banks.next()
                    for u in range(4):
                        S.op("pe", lambda e: e.matmul(pt[:, u * 128:(u + 1) * 128], lhsT=Qm[cur][:, u, :], rhs=Pm[cur][:, u, :], start=True, stop=True),
                             reads=[b_Qm[cur], b_Pm[cur]], **({"writes": [pbuf]} if u == 0 else {"wadd": [pbuf]}))
                    S.op("act", lambda e: e.activation(out=Pm[nx][:, :, :], in_=pt[:, :].rearrange("p (u t) -> p u t", u=4), func=AF.Copy),
                         writes=[pbuf, b_Pm[nx]])
                    if not last:
                        pt2, pbuf2 = banks.next()
                        for u in range(4):
                            S.op("pe", lambda e: e.matmul(pt2[:, u * 128:(u + 1) * 128], lhsT=Pm[cur][:, u, :], rhs=Qm[cur][:, u, :], start=True, stop=True),
                                 reads=[b_Qm[cur], b_Pm[cur]], **({"writes": [pbuf2]} if u == 0 else {"wadd": [pbuf2]}))
                        S.op("dve", lambda e: e.tensor_copy(out=Qm[nx][:, :, :], in_=pt2[:, :].rearrange("p (u t) -> p u t", u=4)),
                             writes=[pbuf2, b_Qm[nx]])
                    pt3, pbuf3 = banks.next()
                    for u in range(4):
                        S.op("pe", lambda e: e.matmul(pt3[:, u * 128:(u + 1) * 128], lhsT=Pm[nx][:, u, :], rhs=Rm[cur][:, u, :], start=True, stop=True),
                             reads=[b_Pm[nx], b_Rm[cur]], **({"writes": [pbuf3]} if u == 0 else {"wadd": [pbuf3]}))
                    dstR = RF[z] if last else Rm[nx]
                    dbR = b_AA[z][3] if last else b_Rm[nx]
                    S.op("dve", lambda e: e.tensor_tensor(out=dstR[:, :, :], in0=pt3[:, :].rearrange("p (u t) -> p u t", u=4), in1=Rm[cur][:, :, :], op=ALU.add),
                         reads=[b_Rm[cur]], writes=[pbuf3, dbR])
                    cur = nx
                    yield

            def phase2(s):
                z = s % NSL
                nn = [s, NCH - 1 - s]
                segs = [b_fm[nn[0] * CH // SEG], b_fm[nn[1] * CH // SEG]]
                bz = b_AA[z]
                pt, pbuf = banks.next()
                for d in range(2):
                    out = pt[:, d * 128:(d + 1) * 128]
                    S.op("pe", lambda e: e.matmul(out, lhsT=pm(AH[d], s), rhs=SWb[:, d, :], start=True, stop=False),
                         reads=[segs[d], b_SWb], **({"writes": [pbuf]} if d == 0 else {"wadd": [pbuf]}))
                    for h in range(2):
                        S.op("pe", lambda e: e.matmul(out[:, h * 64:(h + 1) * 64], lhsT=AAK[z][:, d * 2 + h, :], rhs=VZ[z][:, d, h, h * 64:(h + 1) * 64],
                                                      start=False, stop=(h == 1)), reads=[bz[0], bz[5]], wadd=[pbuf])
                S.op("act", lambda e: e.activation(out=XS[:, :, :], in_=pt[:, 0:256].rearrange("p (d c) -> p d c", d=2), func=AF.Copy),
                     writes=[pbuf, b_XS])
                yield
                pt, pbuf = banks.next()
                for u, (d, h) in enumerate(UNITS):
                    S.op("pe", lambda e: e.matmul(pt[:, u * 64:(u + 1) * 64], lhsT=RF[z][:, u, :], rhs=XS[:, d, h * 64:(h + 1) * 64], start=True, stop=True),
                         reads=[bz[3], b_XS], **({"writes": [pbuf]} if u == 0 else {"wadd": [pbuf]}))
                dstU = USZ[:, :, :, :].rearrange("p d h (g v) -> p d h g v", g=2)
                for h in range(2):
                    S.op("dve", lambda e: e.tensor_copy(out=dstU[:, :, h, h, :], in_=pt[:, 0:256].rearrange("p (d h v) -> p d h v", d=2, h=2)[:, :, h, :]),
                         writes=[pbuf], wadd=[b_USZ])
                yield
                pty, pbufy = banks.next()
                ptd, pbufd = banks.next()
                for d in range(2):
                    out = pty[:, d * 128:(d + 1) * 128]
                    S.op("pe", lambda e: e.matmul(out, lhsT=SWb[:, d, :], rhs=pm(RH[d], s), start=True, stop=False),
                         reads=[segs[d], b_SWb], **({"writes": [pbufy]} if d == 0 else {"wadd": [pbufy]}))
                    for h in range(2):
                        S.op("pe", lambda e: e.matmul(out, lhsT=USZ[:, d, h, :], rhs=ARB[z][:, d * 2 + h, :], start=False, stop=False),
                             reads=[b_USZ, bz[1]], wadd=[pbufy])
                    for h in range(2):
                        S.op("pe", lambda e: e.matmul(out, lhsT=VZ[z][:, d, h, :], rhs=ARK[z][:, d * 2 + h, :], start=False, stop=(h == 1)),
                             reads=[bz[5], bz[2]], wadd=[pbufy])
                for d in range(2):
                    out = ptd[:, d * 128:(d + 1) * 128]
                    udiag = USZ[:, d, :, :].rearrange("p h (g v) -> p h g v", g=2)
                    vdiag = VZ[z][:, d, :, :].rearrange("p h (g v) -> p h g v", g=2)
                    for h in range(2):
                        oh = out[:, h * 64:(h + 1) * 64]
                        S.op("pe", lambda e: e.matmul(oh, lhsT=TOK[z][:, d, :], rhs=udiag[:, h, h, :], start=True, stop=False),
                             reads=[bz[4], b_USZ], **({"writes": [pbufd]} if (d == 0 and h == 0) else {"wadd": [pbufd]}))
                        S.op("pe", lambda e: e.matmul(oh, lhsT=TOK[z][:, 2 + d, :], rhs=vdiag[:, h, h, :], start=False, stop=True),
                             reads=[bz[4], bz[5]], wadd=[pbufd])
                for d in range(2):
                    n = nn[d]
                    dstY = fm(YT, d, n)
                    firstw = (n < NCH - 1 - n) if d == 0 else (n > NCH - 1 - n)
                    if firstw:
                        S.op("act", lambda e: e.activation(out=dstY, in_=pty[:, d * 128:(d + 1) * 128], func=AF.Copy), writes=[pbufy, b_YT[n]])
                    else:
                        S.op("dve", lambda e: e.tensor_tensor(out=dstY, in0=pty[:, d * 128:(d + 1) * 128], in1=dstY, op=ALU.add), writes=[pbufy, b_YT[n]])
                S.op("dve", lambda e: e.tensor_tensor(out=tmpS[:, :, :], in0=ptd[:, 0:256].rearrange("p (d c) -> p d c", d=2), in1=SW[:, :, :], op=ALU.add),
                     reads=[b_SW], writes=[pbufd, b_tmpS])
                for d in range(2):
                    S.op("dve", lambda e: e.scalar_tensor_tensor(out=SW[:, d, :], in0=tmpS[:, d, :], scalar=PcAll[:, s, d:d + 1], in1=blk_b[:, :],
                                                                 op0=ALU.mult, op1=ALU.mult), reads=[b_tmpS, b_pc, b_cst], **({"writes": [b_SW]} if d == 0 else {"wadd": [b_SW]}))
                S.op("act", lambda e: e.activation(out=SWb[:, :, :], in_=SW[:, :, :], func=AF.Copy), reads=[b_SW], writes=[b_SWb])
                yield

            def interleave(g1, g2, ratio):
                d1 = g1 is None
                d2 = False
                while not (d1 and d2):
                    for _ in range(ratio):
                        if not d1:
                            try:
                                next(g1)
                            except StopIteration:
                                d1 = True
                    if not d2:
                        try:
                            next(g2)
                        except StopIteration:
                            d2 = True

            for _ in phase1(0):
                pass
            for s in range(NCH):
                interleave(phase1(s + 1) if s + 1 < NCH else None, phase2(s), 4)
            if hp == 0:
                k.dump("YT", YT[:, :], [128, L], F32, b_YT)
                k.dump("RF", RF[(NCH - 1) % NSL][:, :, :], [128, 4, 128], BF16, [b_AA[(NCH - 1) % NSL][3]])
                k.dump("SW", SW[:, :, :], [128, 2, 128], F32, [b_SW])
            if bstop <= 5:
                continue

            for tl in range(L // 512):
                sl = slice(tl * 512, (tl + 1) * 512)
                W = slice(0, 512)
                ybufs = [b_YT[n] for n in range(tl * 4, tl * 4 + 4)]
                nT, nSQ, nRN, nU = ("t", "sq", "rn", "u") if tl % 2 == 0 else ("r", "k", "kk", "a")
                pt, pbuf = banks.next()
                S.op("pe", lambda e: e.matmul(pt[:, :], lhsT=ones_blk[:, :], rhs=YT[:, sl], start=True, stop=True), reads=ybufs + [b_cst], writes=[pbuf])
                S.op("dve", lambda e: e.scalar_tensor_tensor(out=T[nT][:, W], in0=pt[:, :], scalar=-1.0 / 64, in1=YT[:, sl], op0=ALU.mult, op1=ALU.add),
                     reads=ybufs, writes=[pbuf, BT[nT]])
                S.op("act", lambda e: e.activation(out=T[nSQ][:, W], in_=T[nT][:, W], func=AF.Square), reads=[BT[nT]], writes=[BT[nSQ]])
                pt, pbuf = banks.next()
                S.op("pe", lambda e: e.matmul(pt[:, :], lhsT=ones_blk[:, :], rhs=T[nSQ][:, W], start=True, stop=True), reads=[BT[nSQ], b_cst], writes=[pbuf])
                S.op("act", lambda e: e.activation(out=T[nRN][:, W], in_=pt[:, :], func=AF.Ln, scale=1.0 / 64, bias=eps12[:, 1:2]), reads=[b_cst], writes=[pbuf, BT[nRN]])
                S.op("act", lambda e: e.activation(out=T[nRN][:, W], in_=T[nRN][:, W], func=AF.Exp, scale=-0.5), writes=[BT[nRN]])
                S.op("pool", lambda e: e.tensor_tensor(out=T[nT][:, W], in0=T[nT][:, W], in1=T[nRN][:, W], op=ALU.mult), reads=[BT[nRN]], writes=[BT[nT]])
                S.op("act", lambda e: e.activation(out=T[nU][:, W], in_=T[nT][:, W], func=AF.Identity, scale=pcol[:, 3, hp:hp + 1], bias=pcol[:, 4, hp:hp + 1]),
                     reads=[BT[nT], b_par], writes=[BT[nU]])
                S.op("pool", lambda e: e.tensor_tensor(out=T[nU][:, W], in0=T[nU][:, W], in1=bonT[:, sl], op=ALU.add), reads=[b_bon], writes=[BT[nU]])
                pt, pbuf = banks.next()
                S.op("pe", lambda e: e.matmul(pt[:, :], lhsT=g2s[:, hc], rhs=glowT[:, sl], start=True, stop=True), reads=[b_lw, b_low], writes=[pbuf])
                S.op("dve", lambda e: e.tensor_tensor(out=Vb[:, sl], in0=pt[:, :], in1=T[nU][:, W], op=ALU.mult), reads=[BT[nU]], writes=[pbuf], wadd=[b_fm[tl]])
            S.op("sp", lambda e: e.dma_start(out=k.YG[hp * 128:(hp + 1) * 128, :], in_=Vb[:, :]), reads=b_fm, wadd=[k.b_YG], dma=True)
            if k.dbg:
                S.op("pool", lambda e: e.dma_start(out=k.YGd[hp * 128:(hp + 1) * 128, :], in_=Vb[:, :]), reads=b_fm, wadd=[k.b_YG], dma=True)
        S.barrier()
    stageB_s5(k)


def stageB_s5(k):
    import os
    nc, S, L = k.nc, k.S, k.L
    NSG = L // SEG
    GC1 = 1.5957691216057308
    with contextlib.ExitStack() as st:
        c = consts(k, st)
        sb = lambda name, shape, dt=F32: st.enter_context(nc.sbuf_tensor(U(name), shape, dt))
        banks = Banks(k, st, 6, "s5bk")
        ybank = st.enter_context(nc.psum_tensor(U("s5yb"), [128, 512], F32)); b_ybank = Buf()
        pbt = st.enter_context(nc.psum_tensor(U("s5pbt"), [128, 1024], BF16)); b_pbt = Buf()
        Are = sb("Are", [128, 32]); Aim = sb("Aim", [128, 32]); Dt = sb("Dt", [128, 32])
        b_prm = Buf()
        with nc.allow_non_contiguous_dma(reason="s5 params"):
            for m in range(2):
                pr_ = slice(m * 64, (m + 1) * 64)
                for nm, dst in (("s5_a_re", Are), ("s5_a_im", Aim)):
                    src = k.w[nm].rearrange("d g p -> (d g p)")[m * 64:].rearrange("(j p) -> p j", p=128) if False else None
                    flat = k.w[nm].rearrange("d g p -> (d g p)")
                    srcap = bass.AP(tensor=flat.tensor, offset=m * 64, ap=[[1, 64], [128, 32]])
                    S.op("sp", lambda e: e.dma_start(out=dst[pr_, :], in_=srcap), wadd=[b_prm], dma=True)
                flat = k.w["s5_log_step"].rearrange("d g -> (d g)")
                srcap = bass.AP(tensor=flat.tensor, offset=m, ap=[[0, 64], [2, 32]])
                S.op("sp", lambda e: e.dma_start(out=Dt[pr_, :], in_=srcap), wadd=[b_prm], dma=True)
        P_ = {}
        BP = {}
        for nm in ["rho", "th", "c", "s", "t1", "t2", "zr", "zi", "den", "qr", "qi"]:
            P_[nm] = sb("s5p_" + nm, [128, 32]); BP[nm] = Buf()
        CS = sb("CS", [128, 10, 2, 32]); b_CS = Buf()
        CSn = sb("CSn", [128, 32])

        def dve(fn, reads, writes):
            S.op("dve", fn, reads=reads, writes=writes)

        S.op("act", lambda e: e.activation(out=Dt[:, :], in_=Dt[:, :], func=AF.Exp), writes=[b_prm])
        dve(lambda e: e.tensor_tensor(out=P_["rho"][:, :], in0=Are[:, :], in1=Dt[:, :], op=ALU.mult), [b_prm], [BP["rho"]])
        S.op("act", lambda e: e.activation(out=P_["rho"][:, :], in_=P_["rho"][:, :], func=AF.Exp), writes=[BP["rho"]])
        dve(lambda e: e.tensor_tensor(out=P_["th"][:, :], in0=Aim[:, :], in1=Dt[:, :], op=ALU.mult), [b_prm], [BP["th"]])
        x_ = P_["zr"][:, :]; x2 = P_["zi"][:, :]; pp = P_["den"][:, :]
        dve(lambda e: e.tensor_scalar(out=x_, in0=P_["th"][:, :], scalar1=1.0 / 64, scalar2=None, op0=ALU.mult), [BP["th"]], [BP["zr"]])
        dve(lambda e: e.tensor_tensor(out=x2, in0=x_, in1=x_, op=ALU.mult), [BP["zr"]], [BP["zi"]])

        def horner(dst, dbuf, coefs):
            dve(lambda e: e.tensor_scalar(out=pp, in0=x2, scalar1=coefs[0], scalar2=coefs[1], op0=ALU.mult, op1=ALU.add), [BP["zi"]], [BP["den"]])
            for cf in coefs[2:]:
                dve(lambda e: e.tensor_tensor(out=pp, in0=pp, in1=x2, op=ALU.mult), [BP["zi"]], [BP["den"]])
                dve(lambda e: e.tensor_scalar(out=pp, in0=pp, scalar1=cf, scalar2=None, op0=ALU.add), [], [BP["den"]])
            return pp

        horner(None, None, [-1.0 / 5040, 1.0 / 120, -1.0 / 6, 1.0])
        dve(lambda e: e.tensor_tensor(out=P_["s"][:, :], in0=pp, in1=x_, op=ALU.mult), [BP["den"], BP["zr"]], [BP["s"]])
        horner(None, None, [1.0 / 40320, -1.0 / 720, 1.0 / 24, -0.5, 1.0])
        dve(lambda e: e.tensor_copy(out=P_["c"][:, :], in_=pp), [BP["den"]], [BP["c"]])

        def dbl(co, so, ci, si, wbufs, rbufs):
            dve(lambda e: e.tensor_tensor(out=P_["t1"][:, :], in0=ci, in1=ci, op=ALU.mult), rbufs, [BP["t1"]])
            dve(lambda e: e.tensor_tensor(out=P_["t2"][:, :], in0=si, in1=si, op=ALU.mult), rbufs, [BP["t2"]])
            dve(lambda e: e.scalar_tensor_tensor(out=so, in0=ci, scalar=2.0, in1=si, op0=ALU.mult, op1=ALU.mult), rbufs, wbufs)
            dve(lambda e: e.tensor_tensor(out=co, in0=P_["t1"][:, :], in1=P_["t2"][:, :], op=ALU.subtract), [BP["t1"], BP["t2"]], wbufs)

        tmpc = sb("tmpc", [128, 2, 32]); b_tmpc = Buf()
        pairA = (P_["c"][:, :], P_["s"][:, :]); pairB = (tmpc[:, 0, :], tmpc[:, 1, :])
        allb = [b_tmpc, BP["c"], BP["s"]]
        for it in range(6):
            src = pairA if it % 2 == 0 else pairB
            if it < 5:
                dstp = pairB if it % 2 == 0 else pairA
                dbl(dstp[0], dstp[1], src[0], src[1], allb, allb)
            else:
                dbl(CS[:, 0, 0, :], CS[:, 0, 1, :], src[0], src[1], [b_CS], allb + [b_CS])
        for lv in range(1, 10):
            dbl(CS[:, lv, 0, :], CS[:, lv, 1, :], CS[:, lv - 1, 0, :], CS[:, lv - 1, 1, :], [b_CS], [b_CS])
        S.op("dve", lambda e: e.tensor_scalar(out=CSn[:, :], in0=CS[:, 9, 1, :], scalar1=-1.0, scalar2=None, op0=ALU.mult), reads=[b_CS], wadd=[b_CS])
        dve(lambda e: e.tensor_tensor(out=P_["zr"][:, :], in0=P_["rho"][:, :], in1=CS[:, 0, 0, :], op=ALU.mult), [BP["rho"], b_CS], [BP["zr"]])
        dve(lambda e: e.tensor_scalar(out=P_["zr"][:, :], in0=P_["zr"][:, :], scalar1=-1.0, scalar2=None, op0=ALU.add), [], [BP["zr"]])
        dve(lambda e: e.tensor_tensor(out=P_["zi"][:, :], in0=P_["rho"][:, :], in1=CS[:, 0, 1, :], op=ALU.mult), [BP["rho"], b_CS], [BP["zi"]])
        dve(lambda e: e.tensor_tensor(out=P_["t1"][:, :], in0=Are[:, :], in1=Are[:, :], op=ALU.mult), [b_prm], [BP["t1"]])
        dve(lambda e: e.tensor_tensor(out=P_["den"][:, :], in0=Aim[:, :], in1=Aim[:, :], op=ALU.mult), [b_prm], [BP["den"]])
        dve(lambda e: e.tensor_tensor(out=P_["den"][:, :], in0=P_["den"][:, :], in1=P_["t1"][:, :], op=ALU.add), [BP["t1"]], [BP["den"]])
        dve(lambda e: e.reciprocal(out=P_["den"][:, :], in_=P_["den"][:, :]), [], [BP["den"]])
        dve(lambda e: e.tensor_tensor(out=P_["t1"][:, :], in0=P_["zr"][:, :], in1=Are[:, :], op=ALU.mult), [BP["zr"], b_prm], [BP["t1"]])
        dve(lambda e: e.tensor_tensor(out=P_["t2"][:, :], in0=P_["zi"][:, :], in1=Aim[:, :], op=ALU.mult), [BP["zi"], b_prm], [BP["t2"]])
        dve(lambda e: e.tensor_tensor(out=P_["qr"][:, :], in0=P_["t1"][:, :], in1=P_["t2"][:, :], op=ALU.add), [BP["t1"], BP["t2"]], [BP["qr"]])
        dve(lambda e: e.tensor_tensor(out=P_["qr"][:, :], in0=P_["qr"][:, :], in1=P_["den"][:, :], op=ALU.mult), [BP["den"]], [BP["qr"]])
        dve(lambda e: e.tensor_tensor(out=P_["t1"][:, :], in0=P_["zi"][:, :], in1=Are[:, :], op=ALU.mult), [BP["zi"], b_prm], [BP["t1"]])
        dve(lambda e: e.tensor_tensor(out=P_["t2"][:, :], in0=P_["zr"][:, :], in1=Aim[:, :], op=ALU.mult), [BP["zr"], b_prm], [BP["t2"]])
        dve(lambda e: e.tensor_tensor(out=P_["qi"][:, :], in0=P_["t1"][:, :], in1=P_["t2"][:, :], op=ALU.subtract), [BP["t1"], BP["t2"]], [BP["qi"]])
        dve(lambda e: e.tensor_tensor(out=P_["qi"][:, :], in0=P_["qi"][:, :], in1=P_["den"][:, :], op=ALU.mult), [BP["den"]], [BP["qi"]])
        Bre = sb("Bre", [128, 32, 16]); Bim = sb("Bim", [128, 32, 16]); b_B = Buf()
        for nm, dst in (("s5_b_re", Bre), ("s5_b_im", Bim)):
            for d in range(2):
                src = k.w[nm][d].rearrange("(q m) p c -> (m p) q c", m=2) if False else None
                flat = k.w[nm].rearrange("d g p c -> (d g p c)")
                for m in range(2):
                    srcap = bass.AP(tensor=flat.tensor, offset=d * 32 * 1024 + m * 1024, ap=[[16, 64], [2048, 16], [1, 16]])
                    S.op("sp", lambda e: e.dma_start(out=dst[m * 64:(m + 1) * 64, d * 16:(d + 1) * 16, :], in_=srcap), wadd=[b_B], dma=True)
        BB = sb("BB", [128, 2, 32, 16]); b_BB = Buf()
        tb = sb("tb16", [128, 16]); b_tb = Buf()
        for Uu in range(32):
            qr = P_["qr"][:, Uu:Uu + 1]; qi = P_["qi"][:, Uu:Uu + 1]
            dve(lambda e: e.tensor_scalar(out=tb[:, :], in0=Bim[:, Uu, :], scalar1=qi, scalar2=None, op0=ALU.mult), [b_B, BP["qi"]], [b_tb])
            dve(lambda e: e.scalar_tensor_tensor(out=BB[:, 0, Uu, :], in0=Bre[:, Uu, :], scalar=qr, in1=tb[:, :], op0=ALU.mult, op1=ALU.subtract),
                [b_B, BP["qr"], b_tb], [])
            dve(lambda e: e.tensor_scalar(out=tb[:, :], in0=Bre[:, Uu, :], scalar1=qi, scalar2=None, op0=ALU.mult), [b_B, BP["qi"]], [b_tb])
            S.op("dve", lambda e: e.scalar_tensor_tensor(out=BB[:, 1, Uu, :], in0=Bim[:, Uu, :], scalar=qr, in1=tb[:, :], op0=ALU.mult, op1=ALU.add),
                 reads=[b_B, BP["qr"], b_tb], wadd=[b_BB])
        LT = sb("LT", [128, 32, 2, 128], F32); b_LT = Buf()
        Wt = [sb("Wt%d" % i, [128, 2, 128], F32) for i in range(2)]; b_Wt = [Buf(), Buf()]
        for Uu in range(32):
            pr = (Uu % 16) % 4
            z = Uu % 2
            S.op("pool", lambda e: e.memset(Wt[z][:, :, :], 0.0), writes=[b_Wt[z]])
            for m in range(2):
                c0 = (pr * 2 + m) * 16
                S.op("pool", lambda e: e.tensor_copy(out=Wt[z][m * 64:(m + 1) * 64, :, c0:c0 + 16], in_=BB[m * 64:(m + 1) * 64, :, Uu, :]),
                     reads=[b_BB], wadd=[b_Wt[z]])
            ptf, pbf = banks.next()
            for ri in range(2):
                S.op("pe", lambda e: e.transpose(out=ptf[:, ri * 128:(ri + 1) * 128], in_=Wt[z][:, ri, :], identity=c.ident_f[:, :]),
                     reads=[b_Wt[z], c.b_ident], **({"writes": [pbf]} if ri == 0 else {"wadd": [pbf]}))
            S.op("act", lambda e: e.activation(out=LT[:, Uu, :, :], in_=ptf[:, 0:256].rearrange("p (r t) -> p r t", r=2), func=AF.Copy),
                 writes=[pbf], wadd=[b_LT])
        CN = sb("CN", [128, 2, 2, 64], F32); b_CN = Buf()
        cmask = sb("cmask", [128, 4, 128]); b_cm = Buf()
        S.op("pool", lambda e: e.memset(cmask[:, :, :], 0.0), writes=[b_cm])
        for pr in range(4):
            for m in range(2):
                c0 = (pr * 2 + m) * 16
                S.op("pool", lambda e: e.memset(cmask[m * 64:(m + 1) * 64, pr, c0:c0 + 16], 1.0), writes=[b_cm])
        CT = sb("CT", [128, 16, 2, 128], F32); b_CT = Buf()
        for uc in range(4):
            for ri, nm in enumerate(("s5_c_re", "s5_c_im")):
                src = k.w[nm].rearrange("g c p -> (g c) p")[uc * 128:(uc + 1) * 128, :]
                for m in range(2):
                    S.op("sp", lambda e: e.dma_start(out=CN[:, ri, m, :], in_=src), **({"writes": [b_CN]} if (ri == 0 and m == 0) else {"wadd": [b_CN]}), dma=True)
            ptf, pbf = banks.next()
            for ri in range(2):
                S.op("pe", lambda e: e.transpose(out=ptf[:, ri * 128:(ri + 1) * 128], in_=CN[:, ri, :, :].rearrange("p m q -> p (m q)"), identity=c.ident_f[:, :]),
                     reads=[b_CN, c.b_ident], **({"writes": [pbf]} if ri == 0 else {"wadd": [pbf]}))
            for pr in range(4):
                for ri in range(2):
                    S.op("dve", lambda e: e.scalar_tensor_tensor(out=CT[:, uc * 4 + pr, ri, :], in0=ptf[:, ri * 128:(ri + 1) * 128], scalar=(1.0 if ri == 0 else -1.0),
                                                                 in1=cmask[:, pr, :], op0=ALU.mult, op1=ALU.mult),
                         reads=[b_cm], writes=[pbf], wadd=[b_CT])
        dcol = sb("dcol", [128, 2, 4]); b_dc = Buf()
        with nc.allow_non_contiguous_dma(reason="tiny"):
            S.op("sp", lambda e: e.dma_start(out=dcol[:, 0, :], in_=k.w["s5_d"].rearrange("(c p) -> p c", p=128)), wadd=[b_dc], dma=True)
            S.op("sp", lambda e: e.dma_start(out=dcol[:, 1, :], in_=k.w["s5_b_glu"].rearrange("(c p) -> p c", p=128)), wadd=[b_dc], dma=True)
        wglu = sb("wglu", [128, 4, S5D], BF16); b_wg = Buf()
        S.op("pool", lambda e: e.dma_start(out=wglu[:, :, :], in_=k.w["s5_w_glu"].rearrange("(c p) f -> p c f", p=128)), writes=[b_wg], dma=True)

        uf = sb("uf", [128, L]); ub = sb("ub", [128, L], BF16); b_u = Buf()
        yacc = sb("yacc", [128, L]); b_ya = [Buf() for _ in range(NSG)]
        ygl = sb("ygl", [128, 4, L], BF16); b_yg = Buf()
        ET = sb("ET", [128, 8, 2, SEG]); b_ET = [Buf() for _ in range(8)]
        car = sb("car", [128, 8, 2]); b_car = [Buf() for _ in range(8)]
        TnS = []
        for zz in range(2):
            tn_ = {}; bn_ = {}
            for nm in ["br", "bi", "t1", "t2", "wr", "wi", "sr", "si"]:
                tn_[nm] = sb("s5t%d_" % zz + nm, [128, SEG]); bn_[nm] = Buf()
            tn_["xr"] = tn_["br"]; tn_["xi"] = tn_["bi"]
            bn_["xr"] = bn_["br"]; bn_["xi"] = bn_["bi"]
            tn_["ct"] = sb("s5ct%d" % zz, [128, 4]); bn_["ct"] = Buf()
            TnS.append((tn_, bn_))
        Tn, Bn = TnS[0]
        xr, xi, b_xr, b_xi, ct, b_ct = Tn["xr"], Tn["xi"], Bn["xr"], Bn["xi"], Tn["ct"], Bn["ct"]
        for uc in range(4):
            S.op("sp", lambda e: e.dma_start(out=uf[:, :], in_=k.PT[UOFF + uc * 128:UOFF + (uc + 1) * 128, :]), reads=[k.b_PT], writes=[b_u], dma=True)
            S.op("act", lambda e: e.activation(out=ub[:, :], in_=uf[:, :], func=AF.Copy), reads=[b_u], wadd=[b_u])
            units = [(d, pr, d * 16 + uc * 4 + pr) for d in range(2) for pr in range(4)]
            for j, (d, pr, Uu) in enumerate(units):
                E = ET[:, j, :, :]
                S.op("pool", lambda e: e.memset(E[:, 0, 0:1], 1.0), writes=[b_ET[j]])
                S.op("pool", lambda e: e.memset(E[:, 1, 0:1], 0.0), writes=[b_ET[j]])
                S.op("pool", lambda e: e.memset(car[:, j, :], 0.0), writes=[b_car[j]])
                for lv in range(9):
                    n = 1 << lv
                    cc_ = CS[:, lv, 0, Uu:Uu + 1]; ss_ = CS[:, lv, 1, Uu:Uu + 1]
                    dve(lambda e: e.tensor_scalar(out=Tn["t1"][:, 0:n], in0=E[:, 1, 0:n], scalar1=ss_, scalar2=None, op0=ALU.mult), [b_ET[j], b_CS], [Bn["t1"]])
                    dve(lambda e: e.tensor_scalar(out=Tn["t2"][:, 0:n], in0=E[:, 1, 0:n], scalar1=cc_, scalar2=None, op0=ALU.mult), [b_ET[j], b_CS], [Bn["t2"]])
                    dve(lambda e: e.scalar_tensor_tensor(out=E[:, 0, n:2 * n], in0=E[:, 0, 0:n], scalar=cc_, in1=Tn["t1"][:, 0:n], op0=ALU.mult, op1=ALU.subtract),
                        [Bn["t1"], b_CS], [b_ET[j]])
                    dve(lambda e: e.scalar_tensor_tensor(out=E[:, 1, n:2 * n], in0=E[:, 0, 0:n], scalar=ss_, in1=Tn["t2"][:, 0:n], op0=ALU.mult, op1=ALU.add),
                        [Bn["t2"], b_CS], [b_ET[j]])
            descs = []
            for i in range(NSG if int(os.environ.get("K_S5MAIN", "1")) else 0):
                for d in range(2):
                    for pr in range(4):
                        descs.append((i, d, pr))

            def geom(desc):
                i, d, pr = desc
                sg = i if d == 0 else NSG - 1 - i
                sl = slice(sg * SEG, (sg + 1) * SEG)
                rv = (lambda ap: ap) if d == 0 else (lambda ap: ap[:, ::-1])
                j = d * 4 + pr
                Uu = d * 16 + uc * 4 + pr
                Tn, Bn = TnS[pr % 2]
                return i, d, pr, sg, sl, rv, j, Uu, Tn, Bn

            def bu(desc):
                i, d, pr, sg, sl, rv, j, Uu, Tn, Bn = geom(desc)
                Ei = rv(ET[:, j, 1, :])
                for ri, nm in enumerate(("br", "bi")):
                    pt, pbuf = banks.next()
                    S.op("pe", lambda e: e.matmul(pt[:, :], lhsT=LT[:, Uu, ri, :], rhs=uf[:, sl], start=True, stop=True), reads=[b_LT, b_u], writes=[pbuf])
                    S.op("act", lambda e: e.activation(out=Tn[nm][:, :], in_=pt[:, :], func=AF.Copy), writes=[pbuf, Bn[nm]])
                S.op("pool", lambda e: e.tensor_tensor(out=Tn["t1"][:, :], in0=Tn["bi"][:, :], in1=Ei, op=ALU.mult), reads=[Bn["bi"], b_ET[j]], writes=[Bn["t1"]])
                S.op("pool", lambda e: e.tensor_tensor(out=Tn["t2"][:, :], in0=Tn["br"][:, :], in1=Ei, op=ALU.mult), reads=[Bn["br"], b_ET[j]], writes=[Bn["t2"]])
                Er_ = rv(ET[:, j, 0, :])
                S.op("pool", lambda e: e.tensor_tensor(out=Tn["wr"][:, :], in0=Tn["br"][:, :], in1=Er_, op=ALU.mult), reads=[Bn["br"], b_ET[j]], writes=[Bn["wr"]])
                S.op("pool", lambda e: e.tensor_tensor(out=Tn["wi"][:, :], in0=Tn["bi"][:, :], in1=Er_, op=ALU.mult), reads=[Bn["bi"], b_ET[j]], writes=[Bn["wi"]])

            def dvep(desc):
                i, d, pr, sg, sl, rv, j, Uu, Tn, Bn = geom(desc)
                ct, b_ct = Tn["ct"], Bn["ct"]
                Er = rv(ET[:, j, 0, :]); Ei = rv(ET[:, j, 1, :])
                dve(lambda e: e.tensor_tensor(out=Tn["wr"][:, :], in0=Tn["wr"][:, :], in1=Tn["t1"][:, :], op=ALU.add), [Bn["t1"]], [Bn["wr"]])
                dve(lambda e: e.tensor_tensor(out=Tn["wi"][:, :], in0=Tn["wi"][:, :], in1=Tn["t2"][:, :], op=ALU.subtract), [Bn["t2"]], [Bn["wi"]])
                rho_b = P_["rho"][:, Uu:Uu + 1].to_broadcast([128, SEG])
                for nm_w, nm_s, ci in (("wr", "sr", 0), ("wi", "si", 1)):
                    dve(lambda e: e.tensor_tensor_scan(out=rv(Tn[nm_s][:, :]), data0=rho_b, data1=rv(Tn[nm_w][:, :]), initial=car[:, j, ci:ci + 1],
                                                       op0=ALU.mult, op1=ALU.add), [Bn[nm_w], BP["rho"], b_car[j]], [Bn[nm_s]])
                lc = SEG - 1 if d == 0 else 0
                c9 = CS[:, 9, 0, Uu:Uu + 1]; s9 = CS[:, 9, 1, Uu:Uu + 1]
                ns9 = CSn[:, Uu:Uu + 1]
                S.op("act", lambda e: e.activation(out=ct[:, 0:1], in_=Tn["si"][:, lc:lc + 1], func=AF.Copy, scale=ns9), reads=[Bn["si"], b_CS], writes=[b_ct])
                S.op("act", lambda e: e.activation(out=ct[:, 1:2], in_=Tn["si"][:, lc:lc + 1], func=AF.Copy, scale=c9), reads=[Bn["si"], b_CS], wadd=[b_ct])
                S.op("act", lambda e: e.activation(out=car[:, j, 0:1], in_=Tn["sr"][:, lc:lc + 1], func=AF.Identity, scale=c9, bias=ct[:, 0:1]),
                     reads=[Bn["sr"], b_ct, b_CS], writes=[b_car[j]])
                S.op("act", lambda e: e.activation(out=car[:, j, 1:2], in_=Tn["sr"][:, lc:lc + 1], func=AF.Identity, scale=s9, bias=ct[:, 1:2]),
                     reads=[Bn["sr"], b_ct, b_CS], wadd=[b_car[j]])
                dve(lambda e: e.tensor_tensor(out=Tn["t1"][:, :], in0=Tn["si"][:, :], in1=Ei, op=ALU.mult), [Bn["si"], b_ET[j]], [Bn["t1"]])
                dve(lambda e: e.tensor_tensor(out=Tn["wr"][:, :], in0=Tn["sr"][:, :], in1=Er, op=ALU.mult), [Bn["sr"], b_ET[j]], [Bn["wr"]])
                dve(lambda e: e.tensor_tensor(out=Tn["br"][:, :], in0=Tn["wr"][:, :], in1=Tn["t1"][:, :], op=ALU.subtract), [Bn["wr"], Bn["t1"]], [Bn["br"]])
                dve(lambda e: e.tensor_tensor(out=Tn["t2"][:, :], in0=Tn["sr"][:, :], in1=Ei, op=ALU.mult), [Bn["sr"], b_ET[j]], [Bn["t2"]])
                dve(lambda e: e.tensor_tensor(out=Tn["wi"][:, :], in0=Tn["si"][:, :], in1=Er, op=ALU.mult), [Bn["si"], b_ET[j]], [Bn["wi"]])
                dve(lambda e: e.tensor_tensor(out=Tn["bi"][:, :], in0=Tn["wi"][:, :], in1=Tn["t2"][:, :], op=ALU.add), [Bn["wi"], Bn["t2"]], [Bn["bi"]])

            def outp(desc):
                i, d, pr, sg, sl, rv, j, Uu, Tn, Bn = geom(desc)
                pty, pbufy = ybank, b_ybank
                S.op("pe", lambda e: e.matmul(pty[:, :], lhsT=CT[:, uc * 4 + pr, 0, :], rhs=Tn["br"][:, :], start=(pr == 0), stop=False),
                     reads=[b_CT, Bn["br"]], **({"writes": [pbufy]} if pr == 0 else {"wadd": [pbufy]}))
                S.op("pe", lambda e: e.matmul(pty[:, :], lhsT=CT[:, uc * 4 + pr, 1, :], rhs=Tn["bi"][:, :], start=False, stop=(pr == 3)),
                     reads=[b_CT, Bn["bi"]], wadd=[pbufy])
                if pr == 3:
                    firstw = (sg <= NSG - 1 - sg) if d == 0 else (NSG - 1 - sg < sg)
                    if firstw:
                        S.op("act", lambda e: e.activation(out=yacc[:, sl], in_=pty[:, :], func=AF.Copy), writes=[pbufy, b_ya[sg]])
                    else:
                        dve(lambda e: e.tensor_tensor(out=yacc[:, sl], in0=pty[:, :], in1=yacc[:, sl], op=ALU.add), [], [pbufy, b_ya[sg]])

            if descs:
                bu(descs[0])
            for n_, desc in enumerate(descs):
                if n_ + 1 < len(descs):
                    bu(descs[n_ + 1])
                dvep(desc)
                outp(desc)
            Tn, Bn = TnS[0]
            ct, b_ct = Tn["ct"], Bn["ct"]
            for sg in range(NSG):
                sl = slice(sg * SEG, (sg + 1) * SEG)
                dve(lambda e: e.scalar_tensor_tensor(out=Tn["wr"][:, :], in0=uf[:, sl], scalar=dcol[:, 0, uc:uc + 1], in1=yacc[:, sl], op0=ALU.mult, op1=ALU.add),
                    [b_u, b_dc], [b_ya[sg], Bn["wr"]])
                S.op("act", lambda e: e.activation(out=Tn["t1"][:, :], in_=Tn["wr"][:, :], func=AF.Square), reads=[Bn["wr"]], writes=[Bn["t1"]])
                dve(lambda e: e.tensor_scalar(out=Tn["t1"][:, :], in0=Tn["t1"][:, :], scalar1=0.044715, scalar2=1.0, op0=ALU.mult, op1=ALU.add), [], [Bn["t1"]])
                S.op("pool", lambda e: e.tensor_tensor(out=Tn["t1"][:, :], in0=Tn["t1"][:, :], in1=Tn["wr"][:, :], op=ALU.mult), reads=[Bn["wr"]], writes=[Bn["t1"]])
                S.op("act", lambda e: e.activation(out=Tn["t1"][:, :], in_=Tn["t1"][:, :], func=AF.Sigmoid, scale=GC1), writes=[Bn["t1"]])
                dve(lambda e: e.tensor_tensor(out=ygl[:, uc, sl], in0=Tn["t1"][:, :], in1=Tn["wr"][:, :], op=ALU.mult), [Bn["t1"], Bn["wr"]], [])
                S.op("pool", lambda e: e.memset(ct[:, 2:3], 0.0), reads=[], wadd=[b_yg])
                b_yg.w = dict(b_yg.w); b_yg.w["dve#%d" % S.epoch["dve"]] = S.cnt["dve#%d" % S.epoch["dve"]]
        for oc in range(4):
            for sg in range(NSG):
                sl = slice(sg * SEG, (sg + 1) * SEG)
                pt, pbuf = banks.next()
                for kc in range(4):
                    S.op("pe", lambda e: e.matmul(pt[:, :], lhsT=wglu[:, kc, oc * 128:(oc + 1) * 128], rhs=ygl[:, kc, sl], start=(kc == 0), stop=(kc == 3)),
                         reads=[b_wg, b_yg], **({"writes": [pbuf]} if kc == 0 else {"wadd": [pbuf]}))
                S.op("act", lambda e: e.activation(out=Tn["t1"][:, :], in_=pt[:, :], func=AF.Sigmoid, bias=dcol[:, 1, oc:oc + 1]), reads=[b_dc], writes=[pbuf, Bn["t1"]])
                dve(lambda e: e.tensor_tensor(out=ub[:, sl], in0=Tn["t1"][:, :], in1=ygl[:, oc, sl], op=ALU.mult), [Bn["t1"], b_yg], [b_u])
            S.op("sp", lambda e: e.dma_start(out=k.YS[oc * 128:(oc + 1) * 128, :], in_=ub[:, :]), reads=[b_u], wadd=[k.b_YS], dma=True)
            if k.dbg:
                S.op("pool", lambda e: e.dma_start(out=k.YSd[oc * 128:(oc + 1) * 128, :], in_=ub[:, :]), reads=[b_u], wadd=[k.b_YS], dma=True)
        S.barrier()


def stageC(k):
    nc, S, L = k.nc, k.S, k.L
    with contextlib.ExitStack() as st:
        c = consts(k, st)
        f = alloc_block_bufs(k, st)
        ygT = st.enter_context(nc.sbuf_tensor(U("ygT"), [128, 12, TB], BF16)); b_yg = Buf()
        sig = [st.enter_context(nc.sbuf_tensor(U("sig%d" % i), [128, TB], F32)) for i in range(2)]; b_sig = [Buf(), Buf()]
        tm = [st.enter_context(nc.sbuf_tensor(U("tm%d" % i), [128, TB], F32)) for i in range(2)]; b_tm = [Buf(), Buf()]
        gfin = st.enter_context(nc.sbuf_tensor(U("gfin"), [128, D], F32)); b_gfin = Buf()
        load_gains(k, f, ["norm_mix", "norm_ffn2"])
        S.op("sp", lambda e: e.dma_start(out=gfin[:, :], in_=k.w["norm_final"][None, :].to_broadcast([128, D])), writes=[b_gfin], dma=True)
        mergedT = f.hb[:, :, :].rearrange("p a (b c) -> p (a b) c", c=TB)
        wprj_v = k.wprj.rearrange("(c p) f -> p c f", p=128)
        for b in range(k.NB):
            load_x_block(k, f, k.X1, b, k.b_X1)
            rmsnorm_T(k, f, c, 0)
            tsl = slice(b * TB, (b + 1) * TB)
            S.op("sp", lambda e: e.dma_start(out=ygT[:, 0:8, :], in_=k.YG.rearrange("(c p) t -> p c t", p=128)[:, :, tsl]),
                 reads=[k.b_YG], writes=[b_yg], dma=True)
            S.op("sp", lambda e: e.dma_start(out=ygT[:, 8:12, :], in_=k.YS.rearrange("(c p) t -> p c t", p=128)[:, :, tsl]),
                 reads=[k.b_YS], wadd=[b_yg], dma=True)
            for dg in range(4):
                s = dg % 2
                S.op("sp", lambda e: e.dma_start(out=f.wgu[s][:, 0, :, :], in_=k.winC[dg, :, :, :]),
                     reads=[k.b_conv["winC_%d" % dg]], writes=[f.b_wgu[s]], dma=True)
                S.op("sp", lambda e: e.dma_start(out=f.wgu[s][:, 1, :, :], in_=k.winC[4 + dg, :, :, :]),
                     reads=[k.b_conv["winC_%d" % (4 + dg)]], wadd=[f.b_wgu[s]], dma=True)
                wpj = f.wd[s][:, :, :].rearrange("p a b -> p (a b)")[:, 0:12 * 512].rearrange("p (a b) -> p a b", b=512)
                S.op("sp", lambda e: e.dma_start(out=wpj, in_=wprj_v[:, :, dg * 512:(dg + 1) * 512]),
                     reads=[k.b_conv["wprj"]], writes=[f.b_wd[s]], dma=True)
                for dj in range(4):
                    dc = dg * 4 + dj
                    cs_ = slice(dj * 128, (dj + 1) * 128)
                    q = f.n_gu % 2
                    f.n_gu += 1
                    for j, (pt, bp) in enumerate(((f.pg[q], f.b_pg[q]), (f.pu[q], f.b_pu[q]))):
                        for kc in range(NKC):
                            S.op("pe", lambda e: e.matmul(pt[:, :], lhsT=f.wgu[s][:, j, kc, cs_], rhs=f.hT[:, kc, :],
                                                          start=(kc == 0), stop=(kc == NKC - 1)),
                                 reads=[f.b_wgu[s], f.b_hT], **({"writes": [bp]} if kc == 0 else {"wadd": [bp]}))
                        S.op("act", lambda e: e.activation(out=sig[j][:, :], in_=pt[:, :], func=AF.Sigmoid), writes=[bp, b_sig[j]])
                    for j, (k0, k1) in enumerate(((0, 8), (8, 12))):
                        po, bpo = f.po[j], f.b_po[j]
                        for kc in range(k0, k1):
                            S.op("pe", lambda e: e.matmul(po[:, :], lhsT=wpj[:, kc, cs_], rhs=ygT[:, kc, :], start=(kc == k0), stop=(kc == k1 - 1)),
                                 reads=[f.b_wd[s], b_yg], **({"writes": [bpo]} if kc == k0 else {"wadd": [bpo]}))
                        S.op("dve", lambda e: e.tensor_tensor(out=tm[j][:, :], in0=po[:, :], in1=sig[j][:, :], op=ALU.mult),
                             reads=[b_sig[j]], writes=[bpo, b_tm[j]])
                    S.op("pool", lambda e: e.tensor_tensor(out=mergedT[:, dc, :], in0=tm[0][:, :], in1=tm[1][:, :], op=ALU.add),
                         reads=[b_tm[0], b_tm[1]], **({"writes": f.b_hb} if dc == 0 else {"wadd": f.b_hb}))
            for nb in range(4):
                s = nb % 2
                S.op("sp", lambda e: e.dma_start(out=f.wgu[s][:, 0, :, :], in_=k.wout[nb, :, :, :]),
                     reads=[k.b_conv["wout_%d" % nb]], writes=[f.b_wgu[s]], dma=True)
                for tt in range(NTT):
                    q = f.n_po % 2
                    f.n_po += 1
                    for kc in range(NKC):
                        S.op("pe", lambda e: e.matmul(f.po[q][:, :], lhsT=mergedT[:, kc, tt * 128:(tt + 1) * 128], rhs=f.wgu[s][:, 0, kc, :],
                                                      start=(kc == 0), stop=(kc == NKC - 1)),
                             reads=f.b_hb + [f.b_wgu[s]], **({"writes": [f.b_po[q]]} if kc == 0 else {"wadd": [f.b_po[q]]}))
                    S.op("dve", lambda e: e.tensor_tensor(out=f.xa[:, tt, nb * 512:(nb + 1) * 512], in0=f.xa[:, tt, nb * 512:(nb + 1) * 512],
                                                          in1=f.po[q][:, :], op=ALU.add), writes=[f.b_po[q], f.b_xa[tt]])
            rmsnorm_T(k, f, c, 1)
            double_x(k, f)
            ffn(k, f, 1)
            for tt in range(NTT):
                S.op("act", lambda e: e.activation(out=f.hb[:, tt, :], in_=f.xa[:, tt, :], func=AF.Square, accum_out=f.ss[:, tt:tt + 1]),
                     reads=[f.b_xa[tt]], writes=[f.b_hb[tt]], wadd=[f.b_ss])
            S.op("act", lambda e: e.activation(out=f.rstd[:, 0:NTT], in_=f.ss[:, 0:NTT], func=AF.Sqrt, scale=1.0 / D, bias=1e-6),
                 reads=[f.b_ss], writes=[f.b_rstd])
            S.op("dve", lambda e: e.reciprocal(out=f.rstd[:, 0:NTT], in_=f.rstd[:, 0:NTT]), reads=[f.b_rstd], writes=[f.b_rstd])
            for tt in range(NTT):
                S.op("dve", lambda e: e.scalar_tensor_tensor(out=f.xa[:, tt, :], in0=f.xa[:, tt, :], scalar=f.rstd[:, tt:tt + 1], in1=gfin[:, :],
                                                             op0=ALU.mult, op1=ALU.mult), reads=[f.b_rstd, b_gfin], writes=[f.b_xa[tt]])
                r0 = b * TB + tt * 128
                S.op("sp", lambda e: e.dma_start(out=k.y[r0:r0 + 128, :], in_=f.xa[:, tt, :]), reads=[f.b_xa[tt]], wadd=[k.b_y], dma=True)
        S.barrier()


_CACHE = {}


def kernel(**inputs):
    L = 4096
    xs = np.concatenate([np.asarray(inputs["x_prompt"], dtype=np.float32),
                         np.asarray(inputs["x_sample"], dtype=np.float32)], axis=0)
    nseq = xs.shape[0]
    wmap = {}
    for n in WNAMES:
        a = np.asarray(inputs[n], dtype=np.float32)
        wmap[n] = np.ascontiguousarray(a if n == "norm_final" else a[0])
    nc, k = build(L, stages="ABC", dbg=False)
    in_maps = []
    for core in range(8):
        m = dict(wmap)
        m["x"] = np.ascontiguousarray(xs[core % nseq])
        in_maps.append(m)
    res = run_bass_kernel_spmd(nc, in_maps, core_ids=list(range(8)))
    ys = [np.asarray(res.results[i]["y"], dtype=np.float32) for i in range(nseq)]
    nb = np.asarray(inputs["x_prompt"]).shape[0]
    y_prompt = np.stack(ys[:nb], 0)
    y_sample = np.stack(ys[nb:], 0)
    return (y_prompt, y_sample)
```

```python
import numpy as np
import concourse.bass as bass
import concourse.mybir as mybir

F32 = mybir.dt.float32
BF16 = mybir.dt.bfloat16
AF = mybir.ActivationFunctionType
ALU = mybir.AluOpType
AX = mybir.AxisListType


class Buf:
    __slots__ = ("w", "r", "name")

    def __init__(self, name=""):
        self.w = {}
        self.r = {}
        self.name = name


import os as _os
NO_SELF_WAIT = int(_os.environ.get("K_NOSELF", "0"))


class Sched:
    COMPUTE = ("pe", "act", "dve", "pool")
    NDMA = {"sp": 8, "pool": 4, "act": 2}
    EPOCH = 30000

    def __init__(self, nc, stack):
        self.nc = nc
        self.eng = {"pe": nc.tensor, "act": nc.scalar, "dve": nc.vector,
                    "pool": nc.gpsimd, "sp": nc.sync}
        self.stack = stack
        self.sems = {}
        self.cnt = {}
        self.waited = {e: {} for e in self.eng}
        self.dma_i = {q: 0 for q in self.NDMA}
        self.epoch = {e: 0 for e in self.COMPUTE}
        self.n_wait = 0
        self.n_ins = 0

    def _sem(self, key):
        if key not in self.sems:
            self.sems[key] = self.stack.enter_context(self.nc.semaphore(key.replace("#", "_")))
            self.cnt[key] = 0
        return self.sems[key]

    def _waits(self, eng, toks):
        need = {}
        for d in toks:
            for k, v in d.items():
                if v > need.get(k, 0):
                    need[k] = v
        out = []
        wd = self.waited[eng]
        for k, v in need.items():
            if wd.get(k, 0) < v:
                wd[k] = v
                out.append((k, v))
        return out

    def op(self, eng, fn, reads=(), writes=(), wadd=(), dma=False, pe_acc=False):
        toks = []
        for b in reads:
            toks.append(b.w)
        for b in writes:
            toks.append(b.w)
            toks.append(b.r)
        for b in wadd:
            toks.append(b.r)
        if dma:
            n = self.NDMA[eng]
            i = self.dma_i[eng]
            self.dma_i[eng] = i + 1
            key = "d_%s_%d" % (eng, i % n)
            self._sem(key)
            if self.cnt[key] > 0:
                toks.append({key: self.cnt[key]})
            inc = 16
        else:
            if self.cnt.get("%s#%d" % (eng, self.epoch[eng]), 0) >= self.EPOCH:
                self.epoch[eng] += 1
            key = "%s#%d" % (eng, self.epoch[eng])
            self._sem(key)
            inc = 1
        waits = self._waits(eng, toks)
        if eng == "pe":
            waits = [(k, v) for (k, v) in waits if not k.startswith("pe#")]
        elif NO_SELF_WAIT and not dma:
            waits = [(k, v) for (k, v) in waits if not k.startswith(eng + "#")]
        self.cnt[key] += inc
        val = self.cnt[key]
        eh = self.eng[eng]
        for (k, v) in waits[1:]:
            eh.wait_ge(self.sems[k], v)
        ins = fn(eh)
        if waits:
            ins._wait_ge(self.sems[waits[0][0]], waits[0][1])
        ins.then_inc(self.sems[key], inc)
        self.n_wait += len(waits)
        self.n_ins += 1
        for b in reads:
            if b.r.get(key, 0) < val:
                b.r[key] = val
        for b in writes:
            b.w = {key: val}
            b.r = {}
        for b in wadd:
            b.w = dict(b.w)
            b.w[key] = val
        return (key, val)

    def wait_all(self, eng, toks):
        for (k, v) in self._waits(eng, toks):
            self.eng[eng].wait_ge(self.sems[k], v)

    def final_wait(self, eng, bufs):
        self.wait_all(eng, [b.w for b in bufs])

    def barrier(self):
        allc = {k: v for k, v in self.cnt.items() if v > 0}
        for e in self.eng:
            self.wait_all(e, [allc])
from concourse.bass_utils import run_bass_kernel_spmd
import contextlib

D = 2048
DFF = 5504
NKC = 16
TB = 512
NTT = 4
RW = 1024
S5D = 512
SHIFT = 3456
UOFF = 3456
GOFF = 3968
INC = 8064
FF_GROUPS = [(g * 512, min(512, DFF - g * 512)) for g in range(11)]
WNAMES = ["norm_ffn1", "ffn1_w_gate", "ffn1_w_up", "ffn1_w_down", "norm_mix", "w_in", "shift_mu",
          "rwkv_w0", "rwkv_w2", "rwkv_a0", "rwkv_a2", "rwkv_g2", "rwkv_k_k", "rwkv_k_a", "rwkv_r_k",
          "rwkv_ln_w", "rwkv_ln_b", "s5_a_re", "s5_a_im", "s5_log_step", "s5_b_re", "s5_b_im",
          "s5_c_re", "s5_c_im", "s5_d", "s5_w_glu", "s5_b_glu", "proj_rwkv", "proj_s5", "w_out",
          "norm_ffn2", "ffn2_w_gate", "ffn2_w_up", "ffn2_w_down", "norm_final"]
WSHAPES = {
    "norm_ffn1": [D], "ffn1_w_gate": [D, DFF], "ffn1_w_up": [D, DFF], "ffn1_w_down": [DFF, D],
    "norm_mix": [D], "w_in": [D, INC], "shift_mu": [SHIFT], "rwkv_w0": [2, RW], "rwkv_w2": [2, 64, RW],
    "rwkv_a0": [2, RW], "rwkv_a2": [2, 64, RW], "rwkv_g2": [128, RW], "rwkv_k_k": [RW], "rwkv_k_a": [RW],
    "rwkv_r_k": [RW], "rwkv_ln_w": [RW], "rwkv_ln_b": [RW], "s5_a_re": [2, 32, 64], "s5_a_im": [2, 32, 64],
    "s5_log_step": [2, 32], "s5_b_re": [2, 32, 64, 16], "s5_b_im": [2, 32, 64, 16], "s5_c_re": [32, 16, 64],
    "s5_c_im": [32, 16, 64], "s5_d": [S5D], "s5_w_glu": [S5D, S5D], "s5_b_glu": [S5D],
    "proj_rwkv": [RW, D], "proj_s5": [S5D, D], "w_out": [D, D], "norm_ffn2": [D],
    "ffn2_w_gate": [D, DFF], "ffn2_w_up": [D, DFF], "ffn2_w_down": [DFF, D], "norm_final": [D],
}


_UID = [0]


def U(name):
    _UID[0] += 1
    return "%s_%d" % (name, _UID[0])


class K:
    def dump(self, name, ap, shape, dt, bufs):
        if not self.dbg:
            return
        t = self.nc.dram_tensor("dbg_" + name, shape, dt, kind="ExternalOutput").ap()
        b = Buf()
        self.S.op("sp", lambda e: e.dma_start(out=t, in_=ap), reads=bufs, writes=[b], dma=True)
        self.dbg_bufs.append(b)

    pass


def build(L, stages="ABC", dbg=False):
    nc = bass.Bass("TRN2", target_bir_lowering=False)
    k = K()
    k.nc = nc
    k.L = L
    k.NB = L // TB
    k.dbg = dbg
    k.x = nc.dram_tensor("x", [L, D], F32, kind="ExternalInput").ap()
    k.w = {n: nc.dram_tensor(n, WSHAPES[n], F32, kind="ExternalInput").ap() for n in WNAMES}
    k.y = nc.dram_tensor("y", [L, D], F32, kind="ExternalOutput").ap()
    sk = "ExternalOutput" if dbg else "Internal"
    k.wgu = [nc.dram_tensor("wgu%d" % i, [11, 128, 2, NKC, 512], BF16, kind="Internal").ap() for i in (1, 2)]
    k.wdn = [nc.dram_tensor("wdn%d" % i, [DFF, D], BF16, kind="Internal").ap() for i in (1, 2)]
    k.winA = nc.dram_tensor("winA", [8, 128, NKC, 512], BF16, kind="Internal").ap()
    k.winC = nc.dram_tensor("winC", [8, 128, NKC, 512], BF16, kind="Internal").ap()
    k.wprj = nc.dram_tensor("wprj", [RW + S5D, D], BF16, kind="Internal").ap()
    k.wout = nc.dram_tensor("woutb", [4, 128, NKC, 512], BF16, kind="Internal").ap()
    k.X1 = nc.dram_tensor("X1", [L, D], F32, kind=sk).ap()
    k.PT = nc.dram_tensor("PT", [GOFF, L], F32, kind=sk).ap()
    k.YG = nc.dram_tensor("YG", [RW, L], BF16, kind="Internal").ap()
    k.YS = nc.dram_tensor("YS", [S5D, L], BF16, kind="Internal").ap()
    if dbg:
        k.YGd = nc.dram_tensor("YGd", [RW, L], F32, kind="ExternalOutput").ap()
        k.YSd = nc.dram_tensor("YSd", [S5D, L], F32, kind="ExternalOutput").ap()
    k.b_X1 = Buf(); k.b_PT = Buf(); k.b_YG = Buf(); k.b_YS = Buf(); k.b_y = Buf()
    k.b_conv = {}
    k.dbg_bufs = []

    with contextlib.ExitStack() as st:
        S = Sched(nc, st)
        k.S = S
        k.st = st
        stage0(k)
        if "A" in stages:
            stageA(k)
        S.barrier()
        if "B" in stages:
            stageB(k)
        S.barrier()
        if "C" in stages:
            stageC(k)
        S.barrier()
        S.final_wait("sp", [k.b_y, k.b_X1, k.b_PT, k.b_YG, k.b_YS] + k.dbg_bufs)
    return nc, k


def stage0(k):
    S = k.S
    w = k.w

    import os
    lim = int(os.environ.get("K_CONV_LIMIT", "100000"))
    cnt = [0]

    def conv(name, out_ap, in_ap):
        b = k.b_conv.setdefault(name, Buf())
        cnt[0] += 1
        if cnt[0] > lim:
            return
        S.op("pool", lambda e: e.dma_start(out=out_ap, in_=in_ap), wadd=[b], dma=True)

    def conv_ffn(i, pre):
        for g, (c0, wg) in enumerate(FF_GROUPS):
            for j, nm in enumerate(("_w_gate", "_w_up")):
                src = w[pre + nm].rearrange("(kc p) f -> p kc f", p=128)[:, :, c0:c0 + wg]
                conv("wgu%d_%d" % (i, g), k.wgu[i][g, :, j, :, 0:wg], src)
            src = w[pre + "_w_down"][c0:c0 + wg, :].rearrange("(p c) f -> p c f", p=128)
            dst = k.wdn[i][c0:c0 + wg, :].rearrange("(p c) f -> p c f", p=128)
            conv("wdn%d_%d" % (i, g), dst, src)

    conv_ffn(0, "ffn1")
    win = w["w_in"].rearrange("(kc p) f -> p kc f", p=128)
    for g in range(8):
        wg = min(512, GOFF - g * 512)
        conv("winA_%d" % g, k.winA[g, :, :, 0:wg], win[:, :, g * 512:g * 512 + wg])
    for g in range(8):
        conv("winC_%d" % g, k.winC[g, :, :, :], win[:, :, GOFF + g * 512:GOFF + (g + 1) * 512])
    conv("wprj", k.wprj[0:RW, :].rearrange("(p c) f -> p c f", p=128),
         w["proj_rwkv"].rearrange("(p c) f -> p c f", p=128))
    conv("wprj", k.wprj[RW:RW + S5D, :].rearrange("(p c) f -> p c f", p=128),
         w["proj_s5"].rearrange("(p c) f -> p c f", p=128))
    wo = w["w_out"].rearrange("(kc p) f -> p kc f", p=128)
    for g in range(4):
        conv("wout_%d" % g, k.wout[g, :, :, :], wo[:, :, g * 512:(g + 1) * 512])
    conv_ffn(1, "ffn2")


def consts(k, st):
    nc, S = k.nc, k.S
    c = K()
    c.ident_f = st.enter_context(nc.sbuf_tensor(U("ident_f"), [128, 128], F32))
    c.ident_b = st.enter_context(nc.sbuf_tensor(U("ident_b"), [128, 128], BF16))
    c.b_ident = Buf()
    S.op("pool", lambda e: e.memset(c.ident_f[:, :], 1.0), writes=[c.b_ident])
    S.op("pool", lambda e: e.affine_select(out=c.ident_f[:, :], in_=c.ident_f[:, :], pattern=[[-1, 128]],
                                           compare_op=ALU.is_equal, fill=0.0, base=0, channel_multiplier=1),
         writes=[c.b_ident])
    S.op("dve", lambda e: e.tensor_copy(out=c.ident_b[:, :], in_=c.ident_f[:, :]), reads=[c.b_ident], wadd=[c.b_ident])
    return c


class FFNBufs:
    pass


def alloc_block_bufs(k, st):
    nc = k.nc
    f = FFNBufs()
    f.xa = st.enter_context(nc.sbuf_tensor(U("xa"), [128, NTT, D], F32))
    f.hb = st.enter_context(nc.sbuf_tensor(U("hb"), [128, NTT, D], BF16))
    f.hT = st.enter_context(nc.sbuf_tensor(U("hT"), [128, NKC, TB], BF16))
    f.actT = [st.enter_context(nc.sbuf_tensor(U("actT%d" % i), [128, 4, TB], BF16)) for i in range(2)]
    f.sg = [st.enter_context(nc.sbuf_tensor(U("sg%d" % i), [128, TB], F32)) for i in range(2)]
    f.wgu = [st.enter_context(nc.sbuf_tensor(U("wgu_s%d" % i), [128, 2, NKC, 512], BF16)) for i in range(2)]
    f.wd = [st.enter_context(nc.sbuf_tensor(U("wd_s%d" % i), [128, 4, D], BF16)) for i in range(2)]
    f.ss = st.enter_context(nc.sbuf_tensor(U("ss"), [128, 8], F32))
    f.rstd = st.enter_context(nc.sbuf_tensor(U("rstd"), [128, 8], F32))
    f.gT = st.enter_context(nc.sbuf_tensor(U("gT"), [128, 3, NKC], F32))
    f.pg = [st.enter_context(nc.psum_tensor(U("pg%d" % i), [128, 512], F32)) for i in range(2)]
    f.pu = [st.enter_context(nc.psum_tensor(U("pu%d" % i), [128, 512], F32)) for i in range(2)]
    f.po = [st.enter_context(nc.psum_tensor(U("po%d" % i), [128, 512], F32)) for i in range(2)]
    f.ptr = [st.enter_context(nc.psum_tensor(U("ptr%d" % i), [128, 1024], BF16)) for i in range(2)]
    f.b_xa = [Buf() for _ in range(NTT)]
    f.b_hb = [Buf() for _ in range(NTT)]
    f.b_hT = Buf()
    f.b_actT = [Buf(), Buf()]
    f.b_sg = [Buf(), Buf()]
    f.b_wgu = [Buf(), Buf()]
    f.b_wd = [Buf(), Buf()]
    f.b_ss = Buf(); f.b_rstd = Buf(); f.b_gT = Buf(); f.b_junk = Buf()
    f.b_pg = [Buf(), Buf()]; f.b_pu = [Buf(), Buf()]; f.b_po = [Buf(), Buf()]
    f.b_ptr = [Buf(), Buf()]
    f.n_ptr = 0
    f.n_po = 0
    f.n_gu = 0
    return f


def load_gains(k, f, names):
    S = k.S
    with k.nc.allow_non_contiguous_dma(reason="tiny gain vectors"):
        for i, nm in enumerate(names):
            src = k.w[nm].rearrange("(c p) -> p c", p=128)
            S.op("sp", lambda e: e.dma_start(out=f.gT[:, i, :], in_=src), wadd=[f.b_gT], dma=True)


def rmsnorm_T(k, f, c, gi):
    S = k.S
    for tt in range(NTT):
        S.op("act", lambda e: e.activation(out=f.hb[:, tt, :], in_=f.xa[:, tt, :], func=AF.Square,
                                           accum_out=f.ss[:, tt:tt + 1]),
             reads=[f.b_xa[tt]], writes=[f.b_hb[tt]], wadd=[f.b_ss])
    import os
    lv = int(os.environ.get("K_RMS", "99"))
    if lv <= 1:
        return
    S.op("act", lambda e: e.activation(out=f.rstd[:, 0:NTT], in_=f.ss[:, 0:NTT], func=AF.Sqrt, scale=1.0 / D, bias=1e-6),
         reads=[f.b_ss], writes=[f.b_rstd])
    S.op("dve", lambda e: e.reciprocal(out=f.rstd[:, 0:NTT], in_=f.rstd[:, 0:NTT]), reads=[f.b_rstd], writes=[f.b_rstd])
    if lv <= 2:
        return
    for tt in range(NTT):
        eng = "act" if tt % 2 == 0 else "dve"
        if eng == "act":
            S.op("act", lambda e: e.activation(out=f.hb[:, tt, :], in_=f.xa[:, tt, :], func=AF.Copy,
                                               scale=f.rstd[:, tt:tt + 1]),
                 reads=[f.b_xa[tt], f.b_rstd], writes=[f.b_hb[tt]])
        else:
            S.op("dve", lambda e: e.tensor_scalar(out=f.hb[:, tt, :], in0=f.xa[:, tt, :], scalar1=f.rstd[:, tt:tt + 1],
                                                  scalar2=None, op0=ALU.mult),
                 reads=[f.b_xa[tt], f.b_rstd], writes=[f.b_hb[tt]])
    if lv <= 3:
        return
    first = True
    for kc in range(int(os.environ.get("K_NKC", NKC))):
        s = f.n_ptr % 2
        f.n_ptr += 1
        for tt in range(NTT):
            S.op("pe", lambda e: e.transpose(out=f.ptr[s][:, tt * 128:(tt + 1) * 128],
                                             in_=f.hb[:, tt, kc * 128:(kc + 1) * 128], identity=c.ident_b[:, :]),
                 reads=[f.b_hb[tt], c.b_ident], **({"writes": [f.b_ptr[s]]} if tt == 0 else {"wadd": [f.b_ptr[s]]}))
        if lv <= 4:
            continue
        eng = "act" if kc % 2 == 0 else "dve"
        eng = os.environ.get("K_EVAC", eng)
        wr = {"writes": [f.b_hT]} if first else {"wadd": [f.b_hT]}
        first = False
        if eng == "act":
            S.op("act", lambda e: e.activation(out=f.hT[:, kc, :], in_=f.ptr[s][:, 0:512], func=AF.Copy,
                                               scale=f.gT[:, gi, kc:kc + 1]),
                 reads=[f.b_ptr[s], f.b_gT], **wr)
        else:
            S.op("dve", lambda e: e.tensor_scalar(out=f.hT[:, kc, :], in0=f.ptr[s][:, 0:512], scalar1=f.gT[:, gi, kc:kc + 1],
                                                  scalar2=None, op0=ALU.mult),
                 reads=[f.b_ptr[s], f.b_gT], **wr)


def ffn(k, f, wi):
    S = k.S
    ng = len(FF_GROUPS)

    def load_gu(g):
        s = g % 2
        c0, wg = FF_GROUPS[g]
        S.op("sp", lambda e: e.dma_start(out=f.wgu[s][:, :, :, 0:wg], in_=k.wgu[wi][g, :, :, :, 0:wg]),
             reads=[k.b_conv["wgu%d_%d" % (wi, g)]], writes=[f.b_wgu[s]], dma=True)

    def load_d(g):
        s = g % 2
        c0, wg = FF_GROUPS[g]
        ncg = wg // 128
        src = k.wdn[wi].rearrange("(c p) f -> p c f", p=128)[:, c0 // 128:c0 // 128 + ncg, :]
        S.op("sp", lambda e: e.dma_start(out=f.wd[s][:, 0:ncg, :], in_=src),
             reads=[k.b_conv["wdn%d_%d" % (wi, g)]], writes=[f.b_wd[s]], dma=True)

    def gate_up(g):
        s = g % 2
        c0, wg = FF_GROUPS[g]
        ncg = wg // 128
        for c in range(ncg):
            q = f.n_gu % 2
            f.n_gu += 1
            for j, (pt, bp) in enumerate(((f.pg[q], f.b_pg[q]), (f.pu[q], f.b_pu[q]))):
                for kc in range(NKC):
                    S.op("pe", lambda e: e.matmul(pt[:, :], lhsT=f.wgu[s][:, j, kc, c * 128:(c + 1) * 128],
                                                  rhs=f.hT[:, kc, :], start=(kc == 0), stop=(kc == NKC - 1)),
                         reads=[f.b_wgu[s], f.b_hT], **({"writes": [bp]} if kc == 0 else {"wadd": [bp]}))
            S.op("act", lambda e: e.activation(out=f.sg[q][:, :], in_=f.pg[q][:, :], func=AF.Silu),
                 reads=[f.b_pg[q]], writes=[f.b_sg[q]])
            S.op("dve", lambda e: e.tensor_tensor(out=f.actT[s][:, c, :], in0=f.sg[q][:, :], in1=f.pu[q][:, :], op=ALU.mult),
                 reads=[f.b_sg[q], f.b_pu[q]], **({"writes": [f.b_actT[s]]} if c == 0 else {"wadd": [f.b_actT[s]]}))

    def down(g):
        s = g % 2
        c0, wg = FF_GROUPS[g]
        ncg = wg // 128
        for tt in range(NTT):
            for nb in range(4):
                q = f.n_po % 2
                f.n_po += 1
                for c in range(ncg):
                    S.op("pe", lambda e: e.matmul(f.po[q][:, :], lhsT=f.actT[s][:, c, tt * 128:(tt + 1) * 128],
                                                  rhs=f.wd[s][:, c, nb * 512:(nb + 1) * 512], start=(c == 0), stop=(c == ncg - 1)),
                         reads=[f.b_actT[s], f.b_wd[s]], **({"writes": [f.b_po[q]]} if c == 0 else {"wadd": [f.b_po[q]]}))
                S.op("dve", lambda e: e.tensor_tensor(out=f.xa[:, tt, nb * 512:(nb + 1) * 512], in0=f.xa[:, tt, nb * 512:(nb + 1) * 512],
                                                      in1=f.po[q][:, :], op=ALU.add),
                     reads=[f.b_po[q]], writes=[f.b_xa[tt]])

    load_gu(0)
    load_d(0)
    for g in range(ng + 1):
        if g < ng:
            if g + 1 < ng:
                load_gu(g + 1)
            gate_up(g)
        if g >= 1:
            down(g - 1)
        if g + 1 < ng:
            load_d(g + 1)
    for tt in range(NTT):
        S.op("act", lambda e: e.activation(out=f.xa[:, tt, :], in_=f.xa[:, tt, :], func=AF.Copy, scale=0.5),
             reads=[], writes=[f.b_xa[tt]])


def load_x_block(k, f, src, b, b_src=None):
    S = k.S
    for tt in range(NTT):
        r0 = b * TB + tt * 128
        S.op("sp", lambda e: e.dma_start(out=f.xa[:, tt, :], in_=src[r0:r0 + 128, :]),
             reads=([b_src] if b_src is not None else []), writes=[f.b_xa[tt]], dma=True)


def double_x(k, f):
    S = k.S
    for tt in range(NTT):
        S.op("pool", lambda e: e.tensor_scalar(out=f.xa[:, tt, :], in0=f.xa[:, tt, :], scalar1=2.0, scalar2=None, op0=ALU.mult),
             reads=[f.b_hb[tt]], writes=[f.b_xa[tt]])


def stageA(k):
    nc, S = k.nc, k.S
    with contextlib.ExitStack() as st:
        c = consts(k, st)
        f = alloc_block_bufs(k, st)
        stg = [st.enter_context(nc.sbuf_tensor(U("stg%d" % i), [128, 4, TB], F32)) for i in range(2)]
        b_stg = [Buf(), Buf()]
        load_gains(k, f, ["norm_ffn1", "norm_mix"])
        import os
        stop = int(os.environ.get("K_STOP", "99"))
        for b in range(k.NB):
            load_x_block(k, f, k.x, b)
            if stop <= 1:
                break
            rmsnorm_T(k, f, c, 0)
            if stop <= 2:
                break
            double_x(k, f)
            ffn(k, f, 0)
            if stop <= 3:
                break
            for tt in range(NTT):
                r0 = b * TB + tt * 128
                S.op("sp", lambda e: e.dma_start(out=k.X1[r0:r0 + 128, :], in_=f.xa[:, tt, :]),
                     reads=[f.b_xa[tt]], wadd=[k.b_X1], dma=True)
            rmsnorm_T(k, f, c, 1)
            for g in range(8):
                wg = min(512, GOFF - g * 512)
                ncg = wg // 128
                s = g % 2
                S.op("sp", lambda e: e.dma_start(out=f.wgu[s][:, 0, :, 0:wg], in_=k.winA[g, :, :, 0:wg]),
                     reads=[k.b_conv["winA_%d" % g]], writes=[f.b_wgu[s]], dma=True)
                for cc in range(ncg):
                    q = f.n_gu % 2
                    f.n_gu += 1
                    for kc in range(NKC):
                        S.op("pe", lambda e: e.matmul(f.pg[q][:, :], lhsT=f.wgu[s][:, 0, kc, cc * 128:(cc + 1) * 128],
                                                      rhs=f.hT[:, kc, :], start=(kc == 0), stop=(kc == NKC - 1)),
                             reads=[f.b_wgu[s], f.b_hT], **({"writes": [f.b_pg[q]]} if kc == 0 else {"wadd": [f.b_pg[q]]}))
                    S.op("act", lambda e: e.activation(out=stg[s][:, cc, :], in_=f.pg[q][:, :], func=AF.Copy),
                         reads=[f.b_pg[q]], **({"writes": [b_stg[s]]} if cc == 0 else {"wadd": [b_stg[s]]}))
                dst = k.PT[g * 512:g * 512 + wg, b * TB:(b + 1) * TB].rearrange("(c p) t -> p c t", p=128)
                S.op("sp", lambda e: e.dma_start(out=dst, in_=stg[s][:, 0:ncg, :]),
                     reads=[b_stg[s]], wadd=[k.b_PT], dma=True)
        S.barrier()


KAPPA = 0.6065306597126334
SEG = 512
CH = 128


class Banks:
    def __init__(self, k, st, n, prefix):
        self.t = [st.enter_context(k.nc.psum_tensor(U("%s%d" % (prefix, i)), [128, 512], F32)) for i in range(n)]
        self.tb = None
        self.b = [Buf() for _ in range(n)]
        self.i = 0

    def next(self):
        j = self.i % len(self.t)
        self.i += 1
        return self.t[j], self.b[j]


def mm_group(S, out_ap, bank_buf, pairs, extra_reads=()):
    n = len(pairs)
    for i, (lhsT, rhs, reads) in enumerate(pairs):
        S.op("pe", lambda e: e.matmul(out_ap, lhsT=lhsT, rhs=rhs, start=(i == 0), stop=(i == n - 1)),
             reads=list(reads) + list(extra_reads), wadd=[bank_buf])


def col_load(k, dst, name, sl=None):
    src = k.w[name] if sl is None else sl
    return src


def stageB(k):
    nc, S, L = k.nc, k.S, k.L
    NCH = L // CH
    NSEG = L // SEG
    with contextlib.ExitStack() as st:
        c = consts(k, st)
        sb = lambda name, shape, dt=F32: st.enter_context(nc.sbuf_tensor(U(name), shape, dt))
        wlowT = sb("wlowT", [128, L], BF16); alowT = sb("alowT", [128, L], BF16); glowT = sb("glowT", [128, L], BF16)
        b_low = Buf()
        w2s = sb("w2s", [128, RW], BF16); a2s = sb("a2s", [128, RW], BF16); g2s = sb("g2s", [128, RW], BF16)
        b_lw = Buf()
        muT = sb("muT", [128, 27]); omu = sb("omu", [128, 27]); hmu = sb("hmu", [128, 27])
        w0T = sb("w0T", [128, 2, 8]); a0T = sb("a0T", [128, 2, 8])
        pcol = sb("pcol", [128, 6, 8])
        b_par = Buf()
        ones_blk = sb("ones_blk", [128, 128])
        blk_b = sb("blk_b", [128, 128], F32)
        maskS_tj = sb("maskS_tj", [128, 4, 128], BF16)
        maskS_jt = sb("maskS_jt", [128, 4, 128], BF16)
        maskI_jt = sb("maskI_jt", [128, 4, 128], BF16)
        ident4 = sb("ident4", [128, 4, 128], BF16)
        eps12 = sb("eps12", [128, 2])
        rmask = sb("rmask", [128, SEG])
        rmaskB = sb("rmaskB", [128, SEG])
        b_cst = Buf()
        mtmp = sb("mtmp", [128, 4, 128])

        def mk_mask(dst, pat, cm, base, op):
            S.op("pool", lambda e: e.memset(mtmp[:, :, :], 1.0), writes=[b_cst])
            S.op("pool", lambda e: e.affine_select(out=mtmp[:, :, :], in_=mtmp[:, :, :], pattern=[[0, 4], [pat, 128]],
                                                   compare_op=op, fill=0.0, base=base, channel_multiplier=cm), writes=[b_cst])
            S.op("pool", lambda e: e.tensor_copy(out=dst, in_=mtmp[:, :, :]), writes=[b_cst])

        mk_mask(maskS_tj[:, :, :], -1, 1, -1, ALU.is_ge)
        mk_mask(maskS_jt[:, :, :], 1, -1, -1, ALU.is_ge)
        mk_mask(maskI_jt[:, :, :], 1, -1, 0, ALU.is_ge)
        mk_mask(ident4[:, :, :], -1, 1, 0, ALU.is_equal)
        S.op("pool", lambda e: e.memset(ones_blk[:, :], 0.0), writes=[b_cst])
        S.op("pool", lambda e: e.memset(ones_blk[0:64, 0:64], 1.0), writes=[b_cst])
        S.op("pool", lambda e: e.memset(ones_blk[64:128, 64:128], 1.0), writes=[b_cst])
        S.op("pool", lambda e: e.tensor_copy(out=blk_b[:, :], in_=ones_blk[:, :]), writes=[b_cst])
        S.op("pool", lambda e: e.tensor_copy(out=blk_b[:, :], in_=ones_blk[:, :]), writes=[b_cst])
        S.op("pool", lambda e: e.memset(eps12[:, 0:1], 1e-12), writes=[b_cst])
        S.op("pool", lambda e: e.memset(eps12[:, 1:2], 64e-5), writes=[b_cst])
        S.op("pool", lambda e: e.memset(rmask[:, :], 1.0), writes=[b_cst])
        S.op("pool", lambda e: e.memset(rmask[:, 0:SEG:CH], 0.0), writes=[b_cst])
        S.op("pool", lambda e: e.memset(rmaskB[:, :], 1.0), writes=[b_cst])
        S.op("pool", lambda e: e.memset(rmaskB[:, CH - 1:SEG:CH], 0.0), writes=[b_cst])
        with nc.allow_non_contiguous_dma(reason="tiny param vectors"):
            S.op("sp", lambda e: e.dma_start(out=muT[:, :], in_=k.w["shift_mu"].rearrange("(c p) -> p c", p=128)), wadd=[b_par], dma=True)
            for d in range(2):
                S.op("sp", lambda e: e.dma_start(out=w0T[:, d, :], in_=k.w["rwkv_w0"][d].rearrange("(c p) -> p c", p=128)), wadd=[b_par], dma=True)
                S.op("sp", lambda e: e.dma_start(out=a0T[:, d, :], in_=k.w["rwkv_a0"][d].rearrange("(c p) -> p c", p=128)), wadd=[b_par], dma=True)
            for i, nm in enumerate(["rwkv_k_k", "rwkv_k_a", "rwkv_r_k", "rwkv_ln_w", "rwkv_ln_b"]):
                S.op("sp", lambda e: e.dma_start(out=pcol[:, i, :], in_=k.w[nm].rearrange("(c p) -> p c", p=128)), wadd=[b_par], dma=True)
        S.op("dve", lambda e: e.tensor_scalar(out=omu[:, :], in0=muT[:, :], scalar1=-1.0, scalar2=1.0, op0=ALU.mult, op1=ALU.add), reads=[b_par], wadd=[b_par])
        S.op("dve", lambda e: e.tensor_scalar(out=hmu[:, :], in0=muT[:, :], scalar1=0.5, scalar2=None, op0=ALU.mult), reads=[b_par], wadd=[b_par])
        S.op("dve", lambda e: e.tensor_scalar(out=pcol[:, 5, :], in0=pcol[:, 1, :], scalar1=-1.0, scalar2=1.0, op0=ALU.mult, op1=ALU.add), reads=[b_par], wadd=[b_par])
        S.op("pool", lambda e: e.dma_start(out=w2s[:, :], in_=k.w["rwkv_w2"].rearrange("d r c -> (d r) c")), wadd=[b_lw], dma=True)
        S.op("pool", lambda e: e.dma_start(out=a2s[:, :], in_=k.w["rwkv_a2"].rearrange("d r c -> (d r) c")), wadd=[b_lw], dma=True)
        S.op("pool", lambda e: e.dma_start(out=g2s[:, :], in_=k.w["rwkv_g2"][:, :]), wadd=[b_lw], dma=True)

        import os
        bstop = int(os.environ.get("K_BSTOP", "99"))
        nhp = int(os.environ.get("K_NHP", "8"))
        banks = Banks(k, st, 7, "bk")
        pbt = st.enter_context(nc.psum_tensor(U("pbt"), [128, 1024], BF16))
        b_pbt = Buf()

        T = {}
        BT = {}
        for nm in ["raw", "t", "u", "r", "k", "v", "kk", "sq", "rn", "lwn", "a", "cs", "e1", "e2", "e3", "tmp", "kd", "kds"]:
            T[nm] = sb("tp_" + nm, [128, SEG + 2])
            BT[nm] = Buf()
        for nm, al in (("rk", "sq"), ("t1", "tmp"), ("m", "rn")):
            T[nm] = T[al]
            BT[nm] = BT[al]

        def shifted(dst, dbuf, cc, s0):
            lo = max(0, s0 - 1)
            hi = min(L, s0 + SEG + 1)
            off = lo - (s0 - 1)
            raw = T["raw"]
            if s0 == 0:
                S.op("pool", lambda e: e.memset(raw[:, 0:1], 0.0), writes=[BT["raw"]])
            if s0 + SEG == L:
                S.op("pool", lambda e: e.memset(raw[:, SEG + 1:SEG + 2], 0.0), writes=[BT["raw"]])
            S.op("sp", lambda e: e.dma_start(out=raw[:, off:off + hi - lo], in_=k.PT[cc * 128:(cc + 1) * 128, lo:hi]),
                 reads=[k.b_PT], writes=[BT["raw"]], dma=True)
            S.op("pool", lambda e: e.tensor_tensor(out=T["t"][:, 0:SEG], in0=raw[:, 0:SEG], in1=raw[:, 2:SEG + 2], op=ALU.add),
                 reads=[BT["raw"]], writes=[BT["t"]])
            S.op("act", lambda e: e.activation(out=T["u"][:, 0:SEG], in_=raw[:, 1:SEG + 1], func=AF.Copy, scale=omu[:, cc:cc + 1]),
                 reads=[BT["raw"], b_par], writes=[BT["u"]])
            S.op("dve", lambda e: e.scalar_tensor_tensor(out=dst, in0=T["t"][:, 0:SEG], scalar=hmu[:, cc:cc + 1], in1=T["u"][:, 0:SEG],
                                                         op0=ALU.mult, op1=ALU.add),
                 reads=[BT["t"], BT["u"], b_par], writes=[dbuf])

        for sg in range(NSEG if bstop >= 2 else 0):
            s0 = sg * SEG
            for cc, dstT, fn in ((24, wlowT, AF.Tanh), (25, alowT, AF.Copy), (26, glowT, AF.Sigmoid)):
                shifted(T["r"][:, 0:SEG], BT["r"], cc, s0)
                S.op("act", lambda e: e.activation(out=dstT[:, s0:s0 + SEG], in_=T["r"][:, 0:SEG], func=fn),
                     reads=[BT["r"]], wadd=[b_low])

        RA = [sb("RA%d" % d, [128, NCH, 2, 128], BF16) for d in range(2)]
        BH = [sb("BH%d" % d, [128, L], BF16) for d in range(2)]
        KH = [sb("KH%d" % d, [128, L], BF16) for d in range(2)]
        Vb = sb("Vb", [128, L], BF16)
        VbR = sb("VbR", [128, L], BF16)
        b_fm = [Buf() for _ in range(NSEG)]
        YT = sb("YT", [128, L])
        b_YT = [Buf() for _ in range(NCH)]
        bonT = sb("bonT", [128, L], BF16)
        b_bon = Buf()
        PcAll = sb("PcAll", [128, NCH, 2])
        b_pc = Buf()
        SW = sb("SW", [128, 2, 128]); SWb = sb("SWb", [128, 2, 128], BF16)
        b_SW = Buf(); b_SWb = Buf()
        tmpS = sb("tmpS", [128, 2, 128]); b_tmpS = Buf()
        NSL = 2
        Pm = [sb("Pm%d" % i, [128, 4, 128], BF16) for i in range(2)]
        Qm = [sb("Qm%d" % i, [128, 4, 128], BF16) for i in range(2)]
        Rm = [sb("Rm%d" % i, [128, 4, 128], BF16) for i in range(2)]
        b_Pm = [Buf(), Buf()]; b_Qm = [Buf(), Buf()]; b_Rm = [Buf(), Buf()]
        AAK = [sb("AAK%d" % i, [128, 4, 128], BF16) for i in range(NSL)]
        ARB = [sb("ARB%d" % i, [128, 4, 128], BF16) for i in range(NSL)]
        ARK = [sb("ARK%d" % i, [128, 4, 128], BF16) for i in range(NSL)]
        RF = [sb("RF%d" % i, [128, 4, 128], BF16) for i in range(NSL)]
        TOK = [sb("TOK%d" % i, [128, 4, 128], BF16) for i in range(NSL)]
        VZ = [sb("VZ%d" % i, [128, 2, 2, 128], BF16) for i in range(NSL)]
        b_AA = [[Buf() for _ in range(6)] for _ in range(NSL)]
        XS = sb("XS", [128, 2, 128], BF16); b_XS = Buf()
        USZ = sb("USZ", [128, 2, 2, 128], BF16); b_USZ = Buf()
        for i in range(NSL):
            S.op("pool", lambda e: e.memset(VZ[i][:, :, :, :], 0.0), writes=[b_AA[i][5]])
        S.op("pool", lambda e: e.memset(USZ[:, :, :, :], 0.0), writes=[b_USZ])

        def fm(arr, d, n):
            a = arr[:, n * CH:(n + 1) * CH]
            return a if d == 0 else a[:, ::-1]

        def pm(arr, s):
            return arr[:, s * CH:(s + 1) * CH]

        for hp in range(nhp if bstop >= 3 else 0):
            hc = slice(hp * 128, (hp + 1) * 128)
            for sg in range(NSEG):
                s0 = sg * SEG
                sl = slice(s0, s0 + SEG)
                W = slice(0, SEG)
                shifted(T["r"][:, W], BT["r"], hp, s0)
                shifted(T["k"][:, W], BT["k"], 8 + hp, s0)
                shifted(T["v"][:, W], BT["v"], 16 + hp, s0)
                def osl(arr, d):
                    return arr[:, sl] if d == 0 else arr[:, L - s0 - SEG:L - s0][:, ::-1]

                def osl_ra(d, f_):
                    if d == 0:
                        return RA[0][:, s0 // CH:s0 // CH + 4, f_, :]
                    rn0 = (L - s0 - SEG) // CH
                    return RA[1][:, rn0:rn0 + 4, f_, :][:, ::-1, ::-1]

                def c4(ap):
                    return ap.rearrange("p (c t) -> p c t", c=4)
                S.op("act", lambda e: e.activation(out=Vb[:, sl], in_=T["v"][:, W], func=AF.Copy), reads=[BT["v"]], writes=[b_fm[sg]])
                S.op("pool", lambda e: e.tensor_copy(out=osl(VbR, 1), in_=T["v"][:, W]), reads=[BT["v"]], wadd=[b_fm[sg]])
                S.op("act", lambda e: e.activation(out=T["kk"][:, W], in_=T["k"][:, W], func=AF.Copy, scale=pcol[:, 0, hp:hp + 1]),
                     reads=[BT["k"], b_par], writes=[BT["kk"]])
                S.op("act", lambda e: e.activation(out=T["sq"][:, W], in_=T["kk"][:, W], func=AF.Square), reads=[BT["kk"]], writes=[BT["sq"]])
                pt, pbuf = banks.next()
                S.op("pe", lambda e: e.matmul(pt[:, :], lhsT=ones_blk[:, :], rhs=T["sq"][:, W], start=True, stop=True),
                     reads=[BT["sq"], b_cst], writes=[pbuf])
                S.op("act", lambda e: e.activation(out=T["rn"][:, W], in_=pt[:, :], func=AF.Ln, bias=eps12[:, 0:1]), reads=[b_cst], writes=[pbuf, BT["rn"]])
                S.op("act", lambda e: e.activation(out=T["rn"][:, W], in_=T["rn"][:, W], func=AF.Exp, scale=-0.5), writes=[BT["rn"]])
                S.op("dve", lambda e: e.tensor_tensor(out=T["kk"][:, W], in0=T["kk"][:, W], in1=T["rn"][:, W], op=ALU.mult),
                     reads=[BT["rn"]], writes=[BT["kk"]])
                for d in range(2):
                    dr = slice(d * 64, (d + 1) * 64)
                    pt, pbuf = banks.next()
                    S.op("pe", lambda e: e.matmul(pt[:, :], lhsT=w2s[dr, hc], rhs=wlowT[dr, sl], start=True, stop=True),
                         reads=[b_lw, b_low], writes=[pbuf])
                    S.op("act", lambda e: e.activation(out=T["lwn"][:, W], in_=pt[:, :], func=AF.Sigmoid, bias=w0T[:, d, hp:hp + 1]),
                         reads=[b_par], writes=[pbuf, BT["lwn"]])
                    pt, pbuf = banks.next()
                    S.op("pe", lambda e: e.matmul(pt[:, :], lhsT=a2s[dr, hc], rhs=alowT[dr, sl], start=True, stop=True),
                         reads=[b_lw, b_low], writes=[pbuf])
                    S.op("act", lambda e: e.activation(out=T["a"][:, W], in_=pt[:, :], func=AF.Sigmoid, bias=a0T[:, d, hp:hp + 1]),
                         reads=[b_par], writes=[pbuf, BT["a"]])
                    rv = (lambda ap: ap) if d == 0 else (lambda ap: ap[:, ::-1])
                    S.op("dve", lambda e: e.tensor_tensor_scan(out=rv(T["cs"][:, W]), data0=rv((rmask if d == 0 else rmaskB)[:, :]), data1=rv(T["lwn"][:, W]),
                                                               initial=0.0, op0=ALU.mult, op1=ALU.add),
                         reads=[BT["lwn"], b_cst], writes=[BT["cs"]])
                    S.op("act", lambda e: e.activation(out=T["e1"][:, W], in_=T["cs"][:, W], func=AF.Exp, scale=-KAPPA), reads=[BT["cs"]], writes=[BT["e1"]])
                    S.op("act", lambda e: e.activation(out=T["e2"][:, W], in_=T["cs"][:, W], func=AF.Exp, scale=KAPPA), reads=[BT["cs"]], writes=[BT["e2"]])
                    S.op("dve", lambda e: e.tensor_tensor(out=T["tmp"][:, W], in0=T["cs"][:, W], in1=T["lwn"][:, W], op=ALU.subtract),
                         reads=[BT["cs"], BT["lwn"]], writes=[BT["tmp"]])
                    S.op("act", lambda e: e.activation(out=T["e3"][:, W], in_=T["tmp"][:, W], func=AF.Exp, scale=-KAPPA), reads=[BT["tmp"]], writes=[BT["e3"]])
                    n0 = s0 // CH
                    if d == 0:
                        S.op("act", lambda e: e.activation(func=AF.Copy, out=PcAll[:, n0:n0 + 4, 0], in_=T["e1"][:, CH - 1:SEG:CH]), reads=[BT["e1"]], wadd=[b_pc])
                    else:
                        st0 = NCH - 1 - (n0 + 3)
                        S.op("act", lambda e: e.activation(func=AF.Copy, out=PcAll[:, st0:st0 + 4, 1][:, ::-1], in_=T["e1"][:, 0:SEG:CH]), reads=[BT["e1"]], wadd=[b_pc])
                    S.op("dve", lambda e: e.tensor_tensor(out=osl_ra(d, 0), in0=c4(T["r"][:, W]), in1=c4(T["e1"][:, W]), op=ALU.mult),
                         reads=[BT["r"], BT["e1"]], wadd=[b_fm[sg]])
                    S.op("dve", lambda e: e.scalar_tensor_tensor(out=osl_ra(d, 1), in0=c4(T["kk"][:, W]), scalar=-1.0, in1=c4(T["e3"][:, W]), op0=ALU.mult, op1=ALU.mult),
                         reads=[BT["kk"], BT["e3"]], wadd=[b_fm[sg]])
                    S.op("dve", lambda e: e.tensor_tensor(out=T["t1"][:, W], in0=T["kk"][:, W], in1=T["a"][:, W], op=ALU.mult),
                         reads=[BT["kk"], BT["a"]], writes=[BT["t1"]])
                    S.op("dve", lambda e: e.tensor_tensor(out=osl(BH[d], d), in0=T["t1"][:, W], in1=T["e2"][:, W], op=ALU.mult),
                         reads=[BT["t1"], BT["e2"]], wadd=[b_fm[sg]])
                    S.op("act", lambda e: e.activation(out=T["m"][:, W], in_=T["a"][:, W], func=AF.Identity, scale=pcol[:, 1, hp:hp + 1], bias=pcol[:, 5, hp:hp + 1]),
                         reads=[BT["a"], b_par], writes=[BT["m"]])
                    S.op("dve", lambda e: e.tensor_tensor(out=T["kd"][:, W], in0=T["k"][:, W], in1=T["m"][:, W], op=ALU.mult),
                         reads=[BT["k"], BT["m"]], writes=[BT["kd"]])
                    S.op("dve", lambda e: e.tensor_tensor(out=osl(KH[d], d), in0=T["kd"][:, W], in1=T["e2"][:, W], op=ALU.mult),
                         reads=[BT["kd"], BT["e2"]], wadd=[b_fm[sg]])
                    if d == 0:
                        S.op("pool", lambda e: e.tensor_copy(out=T["kds"][:, W], in_=T["kd"][:, W]), reads=[BT["kd"]], writes=[BT["kds"]])
                    else:
                        S.op("pool", lambda e: e.tensor_tensor(out=T["kds"][:, W], in0=T["kds"][:, W], in1=T["kd"][:, W], op=ALU.add),
                             reads=[BT["kd"]], writes=[BT["kds"]])
                S.op("dve", lambda e: e.scalar_tensor_tensor(out=T["rk"][:, W], in0=T["r"][:, W], scalar=pcol[:, 2, hp:hp + 1], in1=T["kds"][:, W],
                                                             op0=ALU.mult, op1=ALU.mult), reads=[BT["r"], BT["kds"], b_par], writes=[BT["rk"]])
                pt, pbuf = banks.next()
                S.op("pe", lambda e: e.matmul(pt[:, :], lhsT=ones_blk[:, :], rhs=T["rk"][:, W], start=True, stop=True),
                     reads=[BT["rk"], b_cst], writes=[pbuf])
                S.op("dve", lambda e: e.tensor_tensor(out=bonT[:, sl], in0=pt[:, :], in1=T["v"][:, W], op=ALU.mult),
                     reads=[BT["v"]], writes=[pbuf], wadd=[b_bon])

            if hp == 0:
                for d in range(2):
                    k.dump("BH%d" % d, BH[d][:, :], [128, L], BF16, b_fm)
                    k.dump("KH%d" % d, KH[d][:, :], [128, L], BF16, b_fm)
                k.dump("Pc", PcAll[:, :, :], [128, NCH, 2], F32, [b_pc])
                k.dump("bon", bonT[:, :], [128, L], BF16, [b_bon])
            if bstop <= 3:
                continue
            S.op("pool", lambda e: e.memset(SW[:, :, :], 0.0), writes=[b_SW])
            S.op("pool", lambda e: e.memset(SWb[:, :, :], 0.0), writes=[b_SWb])
            UNITS = [(d, h) for d in range(2) for h in range(2)]

            def phase1(s):
                z = s % NSL
                nn = [s, NCH - 1 - s]
                segs = [b_fm[nn[0] * CH // SEG], b_fm[nn[1] * CH // SEG]]

                for h in range(2):
                    hk = slice(h * 64, (h + 1) * 64)
                    pt, pbuf = banks.next()
                    for d in range(2):
                        S.op("pe", lambda e: e.matmul(pt[:, d * 128:(d + 1) * 128], lhsT=RA[d][hk, s, 1, :], rhs=pm(BH[d], s)[hk, :], start=True, stop=True),
                             reads=[segs[d]], **({"writes": [pbuf]} if d == 0 else {"wadd": [pbuf]}))
                    S.op("dve", lambda e: e.tensor_tensor(out=Pm[0][:, h::2, :], in0=pt[:, 0:256].rearrange("p (u t) -> p u t", u=2), in1=maskS_tj[:, 0:2, :], op=ALU.mult),
                         reads=[b_cst], **({"writes": [pbuf, b_Pm[0]]} if h == 0 else {"writes": [pbuf], "wadd": [b_Pm[0]]}))
                yield
                for lh_arr, dstI, bI, dstS, bS in ((BH, ARB[z], b_AA[z][1], Qm[0], b_Qm[0]), (KH, ARK[z], b_AA[z][2], AAK[z], b_AA[z][0])):
                    for h in range(2):
                        hk = slice(h * 64, (h + 1) * 64)
                        pt, pbuf = banks.next()
                        for d in range(2):
                            S.op("pe", lambda e: e.matmul(pt[:, d * 256:(d + 1) * 256], lhsT=pm(lh_arr[d], s)[hk, :], rhs=RA[d][hk, s, :, :], start=True, stop=True),
                                 reads=[segs[d]], **({"writes": [pbuf]} if d == 0 else {"wadd": [pbuf]}))
                        v4 = pt[:, :].rearrange("p (d f t) -> p d f t", d=2, f=2)
                        S.op("dve", lambda e: e.tensor_tensor(out=dstI[:, h::2, :], in0=v4[:, :, 0, :], in1=maskI_jt[:, 0:2, :], op=ALU.mult),
                             reads=[b_cst], **({"writes": [pbuf, bI]} if h == 0 else {"writes": [pbuf], "wadd": [bI]}))
                        S.op("dve", lambda e: e.tensor_tensor(out=dstS[:, h::2, :], in0=v4[:, :, 1, :], in1=maskS_jt[:, 0:2, :], op=ALU.mult),
                             reads=[b_cst], **({"writes": [pbuf, bS]} if h == 0 else {"writes": [pbuf], "wadd": [bS]}))
                    yield
                first = True
                for i, (arr, d) in enumerate(((BH, 0), (BH, 1), (KH, 0), (KH, 1), (None, 0), (None, 1))):
                    src = pm((Vb if d == 0 else VbR) if arr is None else arr[d], s)
                    S.op("pe", lambda e: e.transpose(out=pbt[:, i * 128:(i + 1) * 128], in_=src, identity=c.ident_b[:, :]),
                         reads=[segs[d], c.b_ident], **({"writes": [b_pbt]} if first else {"wadd": [b_pbt]}))
                    first = False
                S.op("act", lambda e: e.activation(out=TOK[z][:, :, :], in_=pbt[:, 0:512].rearrange("p (u t) -> p u t", u=4), func=AF.Copy),
                     writes=[b_pbt, b_AA[z][4]])
                for d in range(2):
                    src = pbt[:, 512 + d * 128:512 + (d + 1) * 128].rearrange("p (h v) -> p h v", h=2)
                    dst = VZ[z][:, d, :, :].rearrange("p h (g v) -> p h g v", g=2)
                    for h in range(2):
                        S.op("act", lambda e: e.activation(out=dst[:, h, h, :], in_=src[:, h, :], func=AF.Copy),
                             writes=[b_pbt], wadd=[b_AA[z][5]])
                yield
                S.op("pool", lambda e: e.tensor_tensor(out=Rm[0][:, :, :], in0=Qm[0][:, :, :], in1=ident4[:, :, :], op=ALU.add),
                     reads=[b_Qm[0], b_cst], writes=[b_Rm[0]])
                cur = 0
                for lv in range(1, 7):
                    nx = 1 - cur
                    last = (lv == 6)
                    pt, pbuf = banks.next()
                    for u in range(4):
                        S.op("pe", lambda e: e.matmul(pt[:, u * 128:(u + 1) * 128], lhsT=Qm[cur][:, u, :], rhs=Pm[cur][:, u, :], start=True, stop=True),
                             reads=[b_Qm[cur], b_Pm[cur]], **({"writes": [pbuf]} if u == 0 else {"wadd": [pbuf]}))
                    S.op("act", lambda e: e.activation(out=Pm[nx][:, :, :], in_=pt[:, :].rearrange("p (u t) -> p u t", u=4), func=AF.Copy),
                         writes=[pbuf, b_Pm[nx]])
                    if not last:
                        pt2, pbuf2 = banks.next()
                        for u in range(4):
                            S.op("pe", lambda e: e.matmul(pt2[:, u * 128:(u + 1) * 128], lhsT=Pm[cur][:, u, :], rhs=Qm[cur][:, u, :], start=True, stop=True),
                                 reads=[b_Qm[cur], b_Pm[cur]], **({"writes": [pbuf2]} if u == 0 else {"wadd": [pbuf2]}))
                        S.op("dve", lambda e: e.tensor_copy(out=Qm[nx][:, :, :], in_=pt2[:, :].rearrange("p (u t) -> p u t", u=4)),
                             writes=[pbuf2, b_Qm[nx]])
                    pt3, pbuf3 = banks.next()
                    for u in range(4):
                        S.op("pe", lambda e: e.matmul(pt3[:, u * 128:(u + 1) * 128], lhsT=Pm[nx][:, u, :], rhs=Rm[cur][:, u, :], start=True, stop=True),
                             reads=[b_Pm[nx], b_Rm[cur]], **({"writes": [pbuf3]} if u == 0 else {"wadd": [pbuf3]}))
                    dstR = RF[z] if last else Rm[nx]
                    dbR = b_AA[z][3] if last else b_Rm[nx]
                    S.op("dve", lambda e: e.tensor_tensor(out=dstR[:, :, :], in0=pt3[:, :].rearrange("p (u t) -> p u t", u=4), in1=Rm[cur][:, :, :], op=ALU.add),
                         reads=[b_Rm[cur]], writes=[pbuf3, dbR])
                    cur = nx
                    yield

            def phase2(s):
                z = s % NSL
                nn = [s, NCH - 1 - s]
                segs = [b_fm[nn[0] * CH // SEG], b_fm[nn[1] * CH // SEG]]
                bz = b_AA[z]
                pt, pbuf = banks.next()
                for d in range(2):
                    out = pt[:, d * 128:(d + 1) * 128]
                    S.op("pe", lambda e: e.matmul(out, lhsT=RA[d][:, s, 1, :], rhs=SWb[:, d, :], start=True, stop=False),
                         reads=[segs[d], b_SWb], **({"writes": [pbuf]} if d == 0 else {"wadd": [pbuf]}))
                    for h in range(2):
                        S.op("pe", lambda e: e.matmul(out[:, h * 64:(h + 1) * 64], lhsT=AAK[z][:, d * 2 + h, :], rhs=VZ[z][:, d, h, h * 64:(h + 1) * 64],
                                                      start=False, stop=(h == 1)), reads=[bz[0], bz[5]], wadd=[pbuf])
                S.op("act", lambda e: e.activation(out=XS[:, :, :], in_=pt[:, 0:256].rearrange("p (d c) -> p d c", d=2), func=AF.Copy),
                     writes=[pbuf, b_XS])
                yield
                pt, pbuf = banks.next()
                for u, (d, h) in enumerate(UNITS):
                    S.op("pe", lambda e: e.matmul(pt[:, u * 64:(u + 1) * 64], lhsT=RF[z][:, u, :], rhs=XS[:, d, h * 64:(h + 1) * 64], start=True, stop=True),
                         reads=[bz[3], b_XS], **({"writes": [pbuf]} if u == 0 else {"wadd": [pbuf]}))
                dstU = USZ[:, :, :, :].rearrange("p d h (g v) -> p d h g v", g=2)
                for h in range(2):
                    S.op("dve", lambda e: e.tensor_copy(out=dstU[:, :, h, h, :], in_=pt[:, 0:256].rearrange("p (d h v) -> p d h v", d=2, h=2)[:, :, h, :]),
                         writes=[pbuf], wadd=[b_USZ])
                yield
                pty, pbufy = banks.next()
                ptd, pbufd = banks.next()
                for d in range(2):
                    out = pty[:, d * 128:(d + 1) * 128]
                    S.op("pe", lambda e: e.matmul(out, lhsT=SWb[:, d, :], rhs=RA[d][:, s, 0, :], start=True, stop=False),
                         reads=[segs[d], b_SWb], **({"writes": [pbufy]} if d == 0 else {"wadd": [pbufy]}))
                    for h in range(2):
                        S.op("pe", lambda e: e.matmul(out, lhsT=USZ[:, d, h, :], rhs=ARB[z][:, d * 2 + h, :], start=False, stop=False),
                             reads=[b_USZ, bz[1]], wadd=[pbufy])
                    for h in range(2):
                        S.op("pe", lambda e: e.matmul(out, lhsT=VZ[z][:, d, h, :], rhs=ARK[z][:, d * 2 + h, :], start=False, stop=(h == 1)),
                             reads=[bz[5], bz[2]], wadd=[pbufy])
                for d in range(2):
                    out = ptd[:, d * 128:(d + 1) * 128]
                    ud = bass.AP(tensor=USZ, offset=d * 256, ap=[[512, 128], [192, 2], [1, 64]])
                    vd = bass.AP(tensor=VZ[z], offset=d * 256, ap=[[512, 128], [192, 2], [1, 64]])
                    S.op("pe", lambda e: e.matmul(out, lhsT=TOK[z][:, d, :], rhs=ud, start=True, stop=False),
                         reads=[bz[4], b_USZ], **({"writes": [pbufd]} if d == 0 else {"wadd": [pbufd]}))
                    S.op("pe", lambda e: e.matmul(out, lhsT=TOK[z][:, 2 + d, :], rhs=vd, start=False, stop=True),
                         reads=[bz[4], bz[5]], wadd=[pbufd])
                for d in range(2):
                    n = nn[d]
                    dstY = fm(YT, d, n)
                    firstw = (n < NCH - 1 - n) if d == 0 else (n > NCH - 1 - n)
                    if firstw:
                        S.op("act", lambda e: e.activation(out=dstY, in_=pty[:, d * 128:(d + 1) * 128], func=AF.Copy), writes=[pbufy, b_YT[n]])
                    else:
                        S.op("dve", lambda e: e.tensor_tensor(out=dstY, in0=pty[:, d * 128:(d + 1) * 128], in1=dstY, op=ALU.add), writes=[pbufy, b_YT[n]])
                S.op("dve", lambda e: e.tensor_tensor(out=tmpS[:, :, :], in0=ptd[:, 0:256].rearrange("p (d c) -> p d c", d=2), in1=SW[:, :, :], op=ALU.add),
                     reads=[b_SW], writes=[pbufd, b_tmpS])
                for d in range(2):
                    S.op("dve", lambda e: e.scalar_tensor_tensor(out=SW[:, d, :], in0=tmpS[:, d, :], scalar=PcAll[:, s, d:d + 1], in1=blk_b[:, :],
                                                                 op0=ALU.mult, op1=ALU.mult), reads=[b_tmpS, b_pc, b_cst], **({"writes": [b_SW]} if d == 0 else {"wadd": [b_SW]}))
                S.op("act", lambda e: e.activation(out=SWb[:, :, :], in_=SW[:, :, :], func=AF.Copy), reads=[b_SW], writes=[b_SWb])
                yield

            def interleave(g1, g2, ratio):
                d1 = g1 is None
                d2 = False
                while not (d1 and d2):
                    for _ in range(ratio):
                        if not d1:
                            try:
                                next(g1)
                            except StopIteration:
                                d1 = True
                    if not d2:
                        try:
                            next(g2)
                        except StopIteration:
                            d2 = True

            for _ in phase1(0):
                pass
            for s in range(NCH):
                interleave(phase1(s + 1) if s + 1 < NCH else None, phase2(s), 4)
            if hp == 0:
                k.dump("YT", YT[:, :], [128, L], F32, b_YT)
                k.dump("RF", RF[(NCH - 1) % NSL][:, :, :], [128, 4, 128], BF16, [b_AA[(NCH - 1) % NSL][3]])
                k.dump("SW", SW[:, :, :], [128, 2, 128], F32, [b_SW])
            if bstop <= 5:
                continue

            for tl in range(L // 512):
                sl = slice(tl * 512, (tl + 1) * 512)
                W = slice(0, 512)
                ybufs = [b_YT[n] for n in range(tl * 4, tl * 4 + 4)]
                nT, nSQ, nRN, nU = ("t", "sq", "rn", "u") if tl % 2 == 0 else ("r", "k", "kk", "a")
                pt, pbuf = banks.next()
                S.op("pe", lambda e: e.matmul(pt[:, :], lhsT=ones_blk[:, :], rhs=YT[:, sl], start=True, stop=True), reads=ybufs + [b_cst], writes=[pbuf])
                S.op("dve", lambda e: e.scalar_tensor_tensor(out=T[nT][:, W], in0=pt[:, :], scalar=-1.0 / 64, in1=YT[:, sl], op0=ALU.mult, op1=ALU.add),
                     reads=ybufs, writes=[pbuf, BT[nT]])
                S.op("act", lambda e: e.activation(out=T[nSQ][:, W], in_=T[nT][:, W], func=AF.Square), reads=[BT[nT]], writes=[BT[nSQ]])
                pt, pbuf = banks.next()
                S.op("pe", lambda e: e.matmul(pt[:, :], lhsT=ones_blk[:, :], rhs=T[nSQ][:, W], start=True, stop=True), reads=[BT[nSQ], b_cst], writes=[pbuf])
                S.op("act", lambda e: e.activation(out=T[nRN][:, W], in_=pt[:, :], func=AF.Ln, scale=1.0 / 64, bias=eps12[:, 1:2]), reads=[b_cst], writes=[pbuf, BT[nRN]])
                S.op("act", lambda e: e.activation(out=T[nRN][:, W], in_=T[nRN][:, W], func=AF.Exp, scale=-0.5), writes=[BT[nRN]])
                S.op("pool", lambda e: e.tensor_tensor(out=T[nT][:, W], in0=T[nT][:, W], in1=T[nRN][:, W], op=ALU.mult), reads=[BT[nRN]], writes=[BT[nT]])
                S.op("act", lambda e: e.activation(out=T[nU][:, W], in_=T[nT][:, W], func=AF.Identity, scale=pcol[:, 3, hp:hp + 1], bias=pcol[:, 4, hp:hp + 1]),
                     reads=[BT[nT], b_par], writes=[BT[nU]])
                S.op("pool", lambda e: e.tensor_tensor(out=T[nU][:, W], in0=T[nU][:, W], in1=bonT[:, sl], op=ALU.add), reads=[b_bon], writes=[BT[nU]])
                pt, pbuf = banks.next()
                S.op("pe", lambda e: e.matmul(pt[:, :], lhsT=g2s[:, hc], rhs=glowT[:, sl], start=True, stop=True), reads=[b_lw, b_low], writes=[pbuf])
                S.op("dve", lambda e: e.tensor_tensor(out=Vb[:, sl], in0=pt[:, :], in1=T[nU][:, W], op=ALU.mult), reads=[BT[nU]], writes=[pbuf], wadd=[b_fm[tl]])
            S.op("sp", lambda e: e.dma_start(out=k.YG[hp * 128:(hp + 1) * 128, :], in_=Vb[:, :]), reads=b_fm, wadd=[k.b_YG], dma=True)
            if k.dbg:
                S.op("pool", lambda e: e.dma_start(out=k.YGd[hp * 128:(hp + 1) * 128, :], in_=Vb[:, :]), reads=b_fm, wadd=[k.b_YG], dma=True)
        S.barrier()
    stageB_s5(k)


def stageB_s5(k):
    import os
    nc, S, L = k.nc, k.S, k.L
    NSG = L // SEG
    GC1 = 1.5957691216057308
    with contextlib.ExitStack() as st:
        c = consts(k, st)
        sb = lambda name, shape, dt=F32: st.enter_context(nc.sbuf_tensor(U(name), shape, dt))
        banks = Banks(k, st, 6, "s5bk")
        ybank = st.enter_context(nc.psum_tensor(U("s5yb"), [128, 512], F32)); b_ybank = Buf()
        pbt = st.enter_context(nc.psum_tensor(U("s5pbt"), [128, 1024], BF16)); b_pbt = Buf()
        Are = sb("Are", [128, 32]); Aim = sb("Aim", [128, 32]); Dt = sb("Dt", [128, 32])
        b_prm = Buf()
        with nc.allow_non_contiguous_dma(reason="s5 params"):
            for m in range(2):
                pr_ = slice(m * 64, (m + 1) * 64)
                for nm, dst in (("s5_a_re", Are), ("s5_a_im", Aim)):
                    src = k.w[nm].rearrange("d g p -> (d g p)")[m * 64:].rearrange("(j p) -> p j", p=128) if False else None
                    flat = k.w[nm].rearrange("d g p -> (d g p)")
                    srcap = bass.AP(tensor=flat.tensor, offset=m * 64, ap=[[1, 64], [128, 32]])
                    S.op("sp", lambda e: e.dma_start(out=dst[pr_, :], in_=srcap), wadd=[b_prm], dma=True)
                flat = k.w["s5_log_step"].rearrange("d g -> (d g)")
                srcap = bass.AP(tensor=flat.tensor, offset=m, ap=[[0, 64], [2, 32]])
                S.op("sp", lambda e: e.dma_start(out=Dt[pr_, :], in_=srcap), wadd=[b_prm], dma=True)
        P_ = {}
        BP = {}
        for nm in ["rho", "th", "c", "s", "t1", "t2", "zr", "zi", "den", "qr", "qi"]:
            P_[nm] = sb("s5p_" + nm, [128, 32]); BP[nm] = Buf()
        CS = sb("CS", [128, 10, 2, 32]); b_CS = Buf()
        CSn = sb("CSn", [128, 32])

        def dve(fn, reads, writes):
            S.op("dve", fn, reads=reads, writes=writes)

        S.op("act", lambda e: e.activation(out=Dt[:, :], in_=Dt[:, :], func=AF.Exp), writes=[b_prm])
        dve(lambda e: e.tensor_tensor(out=P_["rho"][:, :], in0=Are[:, :], in1=Dt[:, :], op=ALU.mult), [b_prm], [BP["rho"]])
        S.op("act", lambda e: e.activation(out=P_["rho"][:, :], in_=P_["rho"][:, :], func=AF.Exp), writes=[BP["rho"]])
        dve(lambda e: e.tensor_tensor(out=P_["th"][:, :], in0=Aim[:, :], in1=Dt[:, :], op=ALU.mult), [b_prm], [BP["th"]])
        x_ = P_["zr"][:, :]; x2 = P_["zi"][:, :]; pp = P_["den"][:, :]
        dve(lambda e: e.tensor_scalar(out=x_, in0=P_["th"][:, :], scalar1=1.0 / 64, scalar2=None, op0=ALU.mult), [BP["th"]], [BP["zr"]])
        dve(lambda e: e.tensor_tensor(out=x2, in0=x_, in1=x_, op=ALU.mult), [BP["zr"]], [BP["zi"]])

        def horner(dst, dbuf, coefs):
            dve(lambda e: e.tensor_scalar(out=pp, in0=x2, scalar1=coefs[0], scalar2=coefs[1], op0=ALU.mult, op1=ALU.add), [BP["zi"]], [BP["den"]])
            for cf in coefs[2:]:
                dve(lambda e: e.tensor_tensor(out=pp, in0=pp, in1=x2, op=ALU.mult), [BP["zi"]], [BP["den"]])
                dve(lambda e: e.tensor_scalar(out=pp, in0=pp, scalar1=cf, scalar2=None, op0=ALU.add), [], [BP["den"]])
            return pp

        horner(None, None, [-1.0 / 5040, 1.0 / 120, -1.0 / 6, 1.0])
        dve(lambda e: e.tensor_tensor(out=P_["s"][:, :], in0=pp, in1=x_, op=ALU.mult), [BP["den"], BP["zr"]], [BP["s"]])
        horner(None, None, [1.0 / 40320, -1.0 / 720, 1.0 / 24, -0.5, 1.0])
        dve(lambda e: e.tensor_copy(out=P_["c"][:, :], in_=pp), [BP["den"]], [BP["c"]])

        def dbl(co, so, ci, si, wbufs, rbufs):
            dve(lambda e: e.tensor_tensor(out=P_["t1"][:, :], in0=ci, in1=ci, op=ALU.mult), rbufs, [BP["t1"]])
            dve(lambda e: e.tensor_tensor(out=P_["t2"][:, :], in0=si, in1=si, op=ALU.mult), rbufs, [BP["t2"]])
            dve(lambda e: e.scalar_tensor_tensor(out=so, in0=ci, scalar=2.0, in1=si, op0=ALU.mult, op1=ALU.mult), rbufs, wbufs)
            dve(lambda e: e.tensor_tensor(out=co, in0=P_["t1"][:, :], in1=P_["t2"][:, :], op=ALU.subtract), [BP["t1"], BP["t2"]], wbufs)

        tmpc = sb("tmpc", [128, 2, 32]); b_tmpc = Buf()
        pairA = (P_["c"][:, :], P_["s"][:, :]); pairB = (tmpc[:, 0, :], tmpc[:, 1, :])
        allb = [b_tmpc, BP["c"], BP["s"]]
        for it in range(6):
            src = pairA if it % 2 == 0 else pairB
            if it < 5:
                dstp = pairB if it % 2 == 0 else pairA
                dbl(dstp[0], dstp[1], src[0], src[1], allb, allb)
            else:
                dbl(CS[:, 0, 0, :], CS[:, 0, 1, :], src[0], src[1], [b_CS], allb + [b_CS])
        for lv in range(1, 10):
            dbl(CS[:, lv, 0, :], CS[:, lv, 1, :], CS[:, lv - 1, 0, :], CS[:, lv - 1, 1, :], [b_CS], [b_CS])
        S.op("dve", lambda e: e.tensor_scalar(out=CSn[:, :], in0=CS[:, 9, 1, :], scalar1=-1.0, scalar2=None, op0=ALU.mult), reads=[b_CS], wadd=[b_CS])
        dve(lambda e: e.tensor_tensor(out=P_["zr"][:, :], in0=P_["rho"][:, :], in1=CS[:, 0, 0, :], op=ALU.mult), [BP["rho"], b_CS], [BP["zr"]])
        dve(lambda e: e.tensor_scalar(out=P_["zr"][:, :], in0=P_["zr"][:, :], scalar1=-1.0, scalar2=None, op0=ALU.add), [], [BP["zr"]])
        dve(lambda e: e.tensor_tensor(out=P_["zi"][:, :], in0=P_["rho"][:, :], in1=CS[:, 0, 1, :], op=ALU.mult), [BP["rho"], b_CS], [BP["zi"]])
        dve(lambda e: e.tensor_tensor(out=P_["t1"][:, :], in0=Are[:, :], in1=Are[:, :], op=ALU.mult), [b_prm], [BP["t1"]])
        dve(lambda e: e.tensor_tensor(out=P_["den"][:, :], in0=Aim[:, :], in1=Aim[:, :], op=ALU.mult), [b_prm], [BP["den"]])
        dve(lambda e: e.tensor_tensor(out=P_["den"][:, :], in0=P_["den"][:, :], in1=P_["t1"][:, :], op=ALU.add), [BP["t1"]], [BP["den"]])
        dve(lambda e: e.reciprocal(out=P_["den"][:, :], in_=P_["den"][:, :]), [], [BP["den"]])
        dve(lambda e: e.tensor_tensor(out=P_["t1"][:, :], in0=P_["zr"][:, :], in1=Are[:, :], op=ALU.mult), [BP["zr"], b_prm], [BP["t1"]])
        dve(lambda e: e.tensor_tensor(out=P_["t2"][:, :], in0=P_["zi"][:, :], in1=Aim[:, :], op=ALU.mult), [BP["zi"], b_prm], [BP["t2"]])
        dve(lambda e: e.tensor_tensor(out=P_["qr"][:, :], in0=P_["t1"][:, :], in1=P_["t2"][:, :], op=ALU.add), [BP["t1"], BP["t2"]], [BP["qr"]])
        dve(lambda e: e.tensor_tensor(out=P_["qr"][:, :], in0=P_["qr"][:, :], in1=P_["den"][:, :], op=ALU.mult), [BP["den"]], [BP["qr"]])
        dve(lambda e: e.tensor_tensor(out=P_["t1"][:, :], in0=P_["zi"][:, :], in1=Are[:, :], op=ALU.mult), [BP["zi"], b_prm], [BP["t1"]])
        dve(lambda e: e.tensor_tensor(out=P_["t2"][:, :], in0=P_["zr"][:, :], in1=Aim[:, :], op=ALU.mult), [BP["zr"], b_prm], [BP["t2"]])
        dve(lambda e: e.tensor_tensor(out=P_["qi"][:, :], in0=P_["t1"][:, :], in1=P_["t2"][:, :], op=ALU.subtract), [BP["t1"], BP["t2"]], [BP["qi"]])
        dve(lambda e: e.tensor_tensor(out=P_["qi"][:, :], in0=P_["qi"][:, :], in1=P_["den"][:, :], op=ALU.mult), [BP["den"]], [BP["qi"]])
        Bre = sb("Bre", [128, 32, 16]); Bim = sb("Bim", [128, 32, 16]); b_B = Buf()
        for nm, dst in (("s5_b_re", Bre), ("s5_b_im", Bim)):
            for d in range(2):
                src = k.w[nm][d].rearrange("(q m) p c -> (m p) q c", m=2) if False else None
                flat = k.w[nm].rearrange("d g p c -> (d g p c)")
                for m in range(2):
                    srcap = bass.AP(tensor=flat.tensor, offset=d * 32 * 1024 + m * 1024, ap=[[16, 64], [2048, 16], [1, 16]])
                    S.op("sp", lambda e: e.dma_start(out=dst[m * 64:(m + 1) * 64, d * 16:(d + 1) * 16, :], in_=srcap), wadd=[b_B], dma=True)
        BB = sb("BB", [128, 2, 32, 16]); b_BB = Buf()
        tb = sb("tb16", [128, 16]); b_tb = Buf()
        for Uu in range(32):
            qr = P_["qr"][:, Uu:Uu + 1]; qi = P_["qi"][:, Uu:Uu + 1]
            dve(lambda e: e.tensor_scalar(out=tb[:, :], in0=Bim[:, Uu, :], scalar1=qi, scalar2=None, op0=ALU.mult), [b_B, BP["qi"]], [b_tb])
            dve(lambda e: e.scalar_tensor_tensor(out=BB[:, 0, Uu, :], in0=Bre[:, Uu, :], scalar=qr, in1=tb[:, :], op0=ALU.mult, op1=ALU.subtract),
                [b_B, BP["qr"], b_tb], [])
            dve(lambda e: e.tensor_scalar(out=tb[:, :], in0=Bre[:, Uu, :], scalar1=qi, scalar2=None, op0=ALU.mult), [b_B, BP["qi"]], [b_tb])
            S.op("dve", lambda e: e.scalar_tensor_tensor(out=BB[:, 1, Uu, :], in0=Bim[:, Uu, :], scalar=qr, in1=tb[:, :], op0=ALU.mult, op1=ALU.add),
                 reads=[b_B, BP["qr"], b_tb], wadd=[b_BB])
        LT = sb("LT", [128, 32, 2, 128], F32); b_LT = Buf()
        Wt = [sb("Wt%d" % i, [128, 2, 128], F32) for i in range(2)]; b_Wt = [Buf(), Buf()]
        for Uu in range(32):
            pr = (Uu % 16) % 4
            z = Uu % 2
            S.op("pool", lambda e: e.memset(Wt[z][:, :, :], 0.0), writes=[b_Wt[z]])
            for m in range(2):
                c0 = (pr * 2 + m) * 16
                S.op("pool", lambda e: e.tensor_copy(out=Wt[z][m * 64:(m + 1) * 64, :, c0:c0 + 16], in_=BB[m * 64:(m + 1) * 64, :, Uu, :]),
                     reads=[b_BB], wadd=[b_Wt[z]])
            ptf, pbf = banks.next()
            for ri in range(2):
                S.op("pe", lambda e: e.transpose(out=ptf[:, ri * 128:(ri + 1) * 128], in_=Wt[z][:, ri, :], identity=c.ident_f[:, :]),
                     reads=[b_Wt[z], c.b_ident], **({"writes": [pbf]} if ri == 0 else {"wadd": [pbf]}))
            S.op("act", lambda e: e.activation(out=LT[:, Uu, :, :], in_=ptf[:, 0:256].rearrange("p (r t) -> p r t", r=2), func=AF.Copy),
                 writes=[pbf], wadd=[b_LT])
        CN = sb("CN", [128, 2, 2, 64], F32); b_CN = Buf()
        cmask = sb("cmask", [128, 4, 128]); b_cm = Buf()
        S.op("pool", lambda e: e.memset(cmask[:, :, :], 0.0), writes=[b_cm])
        for pr in range(4):
            for m in range(2):
                c0 = (pr * 2 + m) * 16
                S.op("pool", lambda e: e.memset(cmask[m * 64:(m + 1) * 64, pr, c0:c0 + 16], 1.0), writes=[b_cm])
        CT = sb("CT", [128, 16, 2, 128], F32); b_CT = Buf()
        for uc in range(4):
            for ri, nm in enumerate(("s5_c_re", "s5_c_im")):
                src = k.w[nm].rearrange("g c p -> (g c) p")[uc * 128:(uc + 1) * 128, :]
                for m in range(2):
                    S.op("sp", lambda e: e.dma_start(out=CN[:, ri, m, :], in_=src), **({"writes": [b_CN]} if (ri == 0 and m == 0) else {"wadd": [b_CN]}), dma=True)
            ptf, pbf = banks.next()
            for ri in range(2):
                S.op("pe", lambda e: e.transpose(out=ptf[:, ri * 128:(ri + 1) * 128], in_=CN[:, ri, :, :].rearrange("p m q -> p (m q)"), identity=c.ident_f[:, :]),
                     reads=[b_CN, c.b_ident], **({"writes": [pbf]} if ri == 0 else {"wadd": [pbf]}))
            for pr in range(4):
                for ri in range(2):
                    S.op("dve", lambda e: e.scalar_tensor_tensor(out=CT[:, uc * 4 + pr, ri, :], in0=ptf[:, ri * 128:(ri + 1) * 128], scalar=(1.0 if ri == 0 else -1.0),
                                                                 in1=cmask[:, pr, :], op0=ALU.mult, op1=ALU.mult),
                         reads=[b_cm], writes=[pbf], wadd=[b_CT])
        dcol = sb("dcol", [128, 2, 4]); b_dc = Buf()
        with nc.allow_non_contiguous_dma(reason="tiny"):
            S.op("sp", lambda e: e.dma_start(out=dcol[:, 0, :], in_=k.w["s5_d"].rearrange("(c p) -> p c", p=128)), wadd=[b_dc], dma=True)
            S.op("sp", lambda e: e.dma_start(out=dcol[:, 1, :], in_=k.w["s5_b_glu"].rearrange("(c p) -> p c", p=128)), wadd=[b_dc], dma=True)
        wglu = sb("wglu", [128, 4, S5D], BF16); b_wg = Buf()
        S.op("pool", lambda e: e.dma_start(out=wglu[:, :, :], in_=k.w["s5_w_glu"].rearrange("(c p) f -> p c f", p=128)), writes=[b_wg], dma=True)

        uf = sb("uf", [128, L]); ub = sb("ub", [128, L], BF16); b_u = Buf()
        yacc = sb("yacc", [128, L]); b_ya = [Buf() for _ in range(NSG)]
        ygl = sb("ygl", [128, 4, L], BF16); b_yg = Buf()
        ET = sb("ET", [128, 8, 2, SEG]); b_ET = [Buf() for _ in range(8)]
        car = sb("car", [128, 8, 2]); b_car = [Buf() for _ in range(8)]
        TnS = []
        for zz in range(2):
            tn_ = {}; bn_ = {}
            for nm in ["br", "bi", "t1", "t2", "wr", "wi", "sr", "si"]:
                tn_[nm] = sb("s5t%d_" % zz + nm, [128, SEG]); bn_[nm] = Buf()
            tn_["xr"] = tn_["br"]; tn_["xi"] = tn_["bi"]
            bn_["xr"] = bn_["br"]; bn_["xi"] = bn_["bi"]
            tn_["ct"] = sb("s5ct%d" % zz, [128, 4]); bn_["ct"] = Buf()
            TnS.append((tn_, bn_))
        Tn, Bn = TnS[0]
        xr, xi, b_xr, b_xi, ct, b_ct = Tn["xr"], Tn["xi"], Bn["xr"], Bn["xi"], Tn["ct"], Bn["ct"]
        for uc in range(4):
            S.op("sp", lambda e: e.dma_start(out=uf[:, :], in_=k.PT[UOFF + uc * 128:UOFF + (uc + 1) * 128, :]), reads=[k.b_PT], writes=[b_u], dma=True)
            S.op("act", lambda e: e.activation(out=ub[:, :], in_=uf[:, :], func=AF.Copy), reads=[b_u], wadd=[b_u])
            units = [(d, pr, d * 16 + uc * 4 + pr) for d in range(2) for pr in range(4)]
            for j, (d, pr, Uu) in enumerate(units):
                E = ET[:, j, :, :]
                S.op("pool", lambda e: e.memset(E[:, 0, 0:1], 1.0), writes=[b_ET[j]])
                S.op("pool", lambda e: e.memset(E[:, 1, 0:1], 0.0), writes=[b_ET[j]])
                S.op("pool", lambda e: e.memset(car[:, j, :], 0.0), writes=[b_car[j]])
                for lv in range(9):
                    n = 1 << lv
                    cc_ = CS[:, lv, 0, Uu:Uu + 1]; ss_ = CS[:, lv, 1, Uu:Uu + 1]
                    dve(lambda e: e.tensor_scalar(out=Tn["t1"][:, 0:n], in0=E[:, 1, 0:n], scalar1=ss_, scalar2=None, op0=ALU.mult), [b_ET[j], b_CS], [Bn["t1"]])
                    dve(lambda e: e.tensor_scalar(out=Tn["t2"][:, 0:n], in0=E[:, 1, 0:n], scalar1=cc_, scalar2=None, op0=ALU.mult), [b_ET[j], b_CS], [Bn["t2"]])
                    dve(lambda e: e.scalar_tensor_tensor(out=E[:, 0, n:2 * n], in0=E[:, 0, 0:n], scalar=cc_, in1=Tn["t1"][:, 0:n], op0=ALU.mult, op1=ALU.subtract),
                        [Bn["t1"], b_CS], [b_ET[j]])
                    dve(lambda e: e.scalar_tensor_tensor(out=E[:, 1, n:2 * n], in0=E[:, 0, 0:n], scalar=ss_, in1=Tn["t2"][:, 0:n], op0=ALU.mult, op1=ALU.add),
                        [Bn["t2"], b_CS], [b_ET[j]])
            descs = []
            for i in range(NSG if int(os.environ.get("K_S5MAIN", "1")) else 0):
                for d in range(2):
                    for pr in range(4):
                        descs.append((i, d, pr))

            def geom(desc):
                i, d, pr = desc
                sg = i if d == 0 else NSG - 1 - i
                sl = slice(sg * SEG, (sg + 1) * SEG)
                rv = (lambda ap: ap) if d == 0 else (lambda ap: ap[:, ::-1])
                j = d * 4 + pr
                Uu = d * 16 + uc * 4 + pr
                Tn, Bn = TnS[pr % 2]
                return i, d, pr, sg, sl, rv, j, Uu, Tn, Bn

            def bu(desc):
                i, d, pr, sg, sl, rv, j, Uu, Tn, Bn = geom(desc)
                Ei = rv(ET[:, j, 1, :])
                for ri, nm in enumerate(("br", "bi")):
                    pt, pbuf = banks.next()
                    S.op("pe", lambda e: e.matmul(pt[:, :], lhsT=LT[:, Uu, ri, :], rhs=uf[:, sl], start=True, stop=True), reads=[b_LT, b_u], writes=[pbuf])
                    S.op("act", lambda e: e.activation(out=Tn[nm][:, :], in_=pt[:, :], func=AF.Copy), writes=[pbuf, Bn[nm]])
                S.op("pool", lambda e: e.tensor_tensor(out=Tn["t1"][:, :], in0=Tn["bi"][:, :], in1=Ei, op=ALU.mult), reads=[Bn["bi"], b_ET[j]], writes=[Bn["t1"]])
                S.op("pool", lambda e: e.tensor_tensor(out=Tn["t2"][:, :], in0=Tn["br"][:, :], in1=Ei, op=ALU.mult), reads=[Bn["br"], b_ET[j]], writes=[Bn["t2"]])
                Er_ = rv(ET[:, j, 0, :])
                S.op("pool", lambda e: e.tensor_tensor(out=Tn["wr"][:, :], in0=Tn["br"][:, :], in1=Er_, op=ALU.mult), reads=[Bn["br"], b_ET[j]], writes=[Bn["wr"]])
                S.op("pool", lambda e: e.tensor_tensor(out=Tn["wi"][:, :], in0=Tn["bi"][:, :], in1=Er_, op=ALU.mult), reads=[Bn["bi"], b_ET[j]], writes=[Bn["wi"]])

            def dvep(desc):
                i, d, pr, sg, sl, rv, j, Uu, Tn, Bn = geom(desc)
                ct, b_ct = Tn["ct"], Bn["ct"]
                Er = rv(ET[:, j, 0, :]); Ei = rv(ET[:, j, 1, :])
                dve(lambda e: e.tensor_tensor(out=Tn["wr"][:, :], in0=Tn["wr"][:, :], in1=Tn["t1"][:, :], op=ALU.add), [Bn["t1"]], [Bn["wr"]])
                dve(lambda e: e.tensor_tensor(out=Tn["wi"][:, :], in0=Tn["wi"][:, :], in1=Tn["t2"][:, :], op=ALU.subtract), [Bn["t2"]], [Bn["wi"]])
                rho_b = P_["rho"][:, Uu:Uu + 1].to_broadcast([128, SEG])
                for nm_w, nm_s, ci in (("wr", "sr", 0), ("wi", "si", 1)):
                    dve(lambda e: e.tensor_tensor_scan(out=rv(Tn[nm_s][:, :]), data0=rho_b, data1=rv(Tn[nm_w][:, :]), initial=car[:, j, ci:ci + 1],
                                                       op0=ALU.mult, op1=ALU.add), [Bn[nm_w], BP["rho"], b_car[j]], [Bn[nm_s]])
                lc = SEG - 1 if d == 0 else 0
                c9 = CS[:, 9, 0, Uu:Uu + 1]; s9 = CS[:, 9, 1, Uu:Uu + 1]
                ns9 = CSn[:, Uu:Uu + 1]
                S.op("act", lambda e: e.activation(out=ct[:, 0:1], in_=Tn["si"][:, lc:lc + 1], func=AF.Copy, scale=ns9), reads=[Bn["si"], b_CS], writes=[b_ct])
                S.op("act", lambda e: e.activation(out=ct[:, 1:2], in_=Tn["si"][:, lc:lc + 1], func=AF.Copy, scale=c9), reads=[Bn["si"], b_CS], wadd=[b_ct])
                S.op("act", lambda e: e.activation(out=car[:, j, 0:1], in_=Tn["sr"][:, lc:lc + 1], func=AF.Identity, scale=c9, bias=ct[:, 0:1]),
                     reads=[Bn["sr"], b_ct, b_CS], writes=[b_car[j]])
                S.op("act", lambda e: e.activation(out=car[:, j, 1:2], in_=Tn["sr"][:, lc:lc + 1], func=AF.Identity, scale=s9, bias=ct[:, 1:2]),
                     reads=[Bn["sr"], b_ct, b_CS], wadd=[b_car[j]])
                dve(lambda e: e.tensor_tensor(out=Tn["t1"][:, :], in0=Tn["si"][:, :], in1=Ei, op=ALU.mult), [Bn["si"], b_ET[j]], [Bn["t1"]])
                dve(lambda e: e.tensor_tensor(out=Tn["wr"][:, :], in0=Tn["sr"][:, :], in1=Er, op=ALU.mult), [Bn["sr"], b_ET[j]], [Bn["wr"]])
                dve(lambda e: e.tensor_tensor(out=Tn["br"][:, :], in0=Tn["wr"][:, :], in1=Tn["t1"][:, :], op=ALU.subtract), [Bn["wr"], Bn["t1"]], [Bn["br"]])
                dve(lambda e: e.tensor_tensor(out=Tn["t2"][:, :], in0=Tn["sr"][:, :], in1=Ei, op=ALU.mult), [Bn["sr"], b_ET[j]], [Bn["t2"]])
                dve(lambda e: e.tensor_tensor(out=Tn["wi"][:, :], in0=Tn["si"][:, :], in1=Er, op=ALU.mult), [Bn["si"], b_ET[j]], [Bn["wi"]])
                dve(lambda e: e.tensor_tensor(out=Tn["bi"][:, :], in0=Tn["wi"][:, :], in1=Tn["t2"][:, :], op=ALU.add), [Bn["wi"], Bn["t2"]], [Bn["bi"]])

            def outp(desc):
                i, d, pr, sg, sl, rv, j, Uu, Tn, Bn = geom(desc)
                pty, pbufy = ybank, b_ybank
                S.op("pe", lambda e: e.matmul(pty[:, :], lhsT=CT[:, uc * 4 + pr, 0, :], rhs=Tn["br"][:, :], start=(pr == 0), stop=False),
                     reads=[b_CT, Bn["br"]], **({"writes": [pbufy]} if pr == 0 else {"wadd": [pbufy]}))
                S.op("pe", lambda e: e.matmul(pty[:, :], lhsT=CT[:, uc * 4 + pr, 1, :], rhs=Tn["bi"][:, :], start=False, stop=(pr == 3)),
                     reads=[b_CT, Bn["bi"]], wadd=[pbufy])
                if pr == 3:
                    firstw = (sg <= NSG - 1 - sg) if d == 0 else (NSG - 1 - sg < sg)
                    if firstw:
                        S.op("act", lambda e: e.activation(out=yacc[:, sl], in_=pty[:, :], func=AF.Copy), writes=[pbufy, b_ya[sg]])
                    else:
                        dve(lambda e: e.tensor_tensor(out=yacc[:, sl], in0=pty[:, :], in1=yacc[:, sl], op=ALU.add), [], [pbufy, b_ya[sg]])

            if descs:
                bu(descs[0])
            for n_, desc in enumerate(descs):
                if n_ + 1 < len(descs):
                    bu(descs[n_ + 1])
                dvep(desc)
                outp(desc)
            Tn, Bn = TnS[0]
            ct, b_ct = Tn["ct"], Bn["ct"]
            for sg in range(NSG):
                sl = slice(sg * SEG, (sg + 1) * SEG)
                dve(lambda e: e.scalar_tensor_tensor(out=Tn["wr"][:, :], in0=uf[:, sl], scalar=dcol[:, 0, uc:uc + 1], in1=yacc[:, sl], op0=ALU.mult, op1=ALU.add),
                    [b_u, b_dc], [b_ya[sg], Bn["wr"]])
                S.op("act", lambda e: e.activation(out=Tn["t1"][:, :], in_=Tn["wr"][:, :], func=AF.Square), reads=[Bn["wr"]], writes=[Bn["t1"]])
                dve(lambda e: e.tensor_scalar(out=Tn["t1"][:, :], in0=Tn["t1"][:, :], scalar1=0.044715, scalar2=1.0, op0=ALU.mult, op1=ALU.add), [], [Bn["t1"]])
                S.op("pool", lambda e: e.tensor_tensor(out=Tn["t1"][:, :], in0=Tn["t1"][:, :], in1=Tn["wr"][:, :], op=ALU.mult), reads=[Bn["wr"]], writes=[Bn["t1"]])
                S.op("act", lambda e: e.activation(out=Tn["t1"][:, :], in_=Tn["t1"][:, :], func=AF.Sigmoid, scale=GC1), writes=[Bn["t1"]])
                dve(lambda e: e.tensor_tensor(out=ygl[:, uc, sl], in0=Tn["t1"][:, :], in1=Tn["wr"][:, :], op=ALU.mult), [Bn["t1"], Bn["wr"]], [])
                S.op("pool", lambda e: e.memset(ct[:, 2:3], 0.0), reads=[], wadd=[b_yg])
                b_yg.w = dict(b_yg.w); b_yg.w["dve#%d" % S.epoch["dve"]] = S.cnt["dve#%d" % S.epoch["dve"]]
        for oc in range(4):
            for sg in range(NSG):
                sl = slice(sg * SEG, (sg + 1) * SEG)
                pt, pbuf = banks.next()
                for kc in range(4):
                    S.op("pe", lambda e: e.matmul(pt[:, :], lhsT=wglu[:, kc, oc * 128:(oc + 1) * 128], rhs=ygl[:, kc, sl], start=(kc == 0), stop=(kc == 3)),
                         reads=[b_wg, b_yg], **({"writes": [pbuf]} if kc == 0 else {"wadd": [pbuf]}))
                S.op("act", lambda e: e.activation(out=Tn["t1"][:, :], in_=pt[:, :], func=AF.Sigmoid, bias=dcol[:, 1, oc:oc + 1]), reads=[b_dc], writes=[pbuf, Bn["t1"]])
                dve(lambda e: e.tensor_tensor(out=ub[:, sl], in0=Tn["t1"][:, :], in1=ygl[:, oc, sl], op=ALU.mult), [Bn["t1"], b_yg], [b_u])
            S.op("sp", lambda e: e.dma_start(out=k.YS[oc * 128:(oc + 1) * 128, :], in_=ub[:, :]), reads=[b_u], wadd=[k.b_YS], dma=True)
            if k.dbg:
                S.op("pool", lambda e: e.dma_start(out=k.YSd[oc * 128:(oc + 1) * 128, :], in_=ub[:, :]), reads=[b_u], wadd=[k.b_YS], dma=True)
        S.barrier()


def stageC(k):
    nc, S, L = k.nc, k.S, k.L
    with contextlib.ExitStack() as st:
        c = consts(k, st)
        f = alloc_block_bufs(k, st)
        ygT = st.enter_context(nc.sbuf_tensor(U("ygT"), [128, 12, TB], BF16)); b_yg = Buf()
        sig = [st.enter_context(nc.sbuf_tensor(U("sig%d" % i), [128, TB], F32)) for i in range(2)]; b_sig = [Buf(), Buf()]
        tm = [st.enter_context(nc.sbuf_tensor(U("tm%d" % i), [128, TB], F32)) for i in range(2)]; b_tm = [Buf(), Buf()]
        gfin = st.enter_context(nc.sbuf_tensor(U("gfin"), [128, D], F32)); b_gfin = Buf()
        load_gains(k, f, ["norm_mix", "norm_ffn2"])
        S.op("sp", lambda e: e.dma_start(out=gfin[:, :], in_=k.w["norm_final"][None, :].to_broadcast([128, D])), writes=[b_gfin], dma=True)
        mergedT = f.hb[:, :, :].rearrange("p a (b c) -> p (a b) c", c=TB)
        wprj_v = k.wprj.rearrange("(c p) f -> p c f", p=128)
        for b in range(k.NB):
            load_x_block(k, f, k.X1, b, k.b_X1)
            rmsnorm_T(k, f, c, 0)
            tsl = slice(b * TB, (b + 1) * TB)
            S.op("sp", lambda e: e.dma_start(out=ygT[:, 0:8, :], in_=k.YG.rearrange("(c p) t -> p c t", p=128)[:, :, tsl]),
                 reads=[k.b_YG], writes=[b_yg], dma=True)
            S.op("sp", lambda e: e.dma_start(out=ygT[:, 8:12, :], in_=k.YS.rearrange("(c p) t -> p c t", p=128)[:, :, tsl]),
                 reads=[k.b_YS], wadd=[b_yg], dma=True)
            for dg in range(4):
                s = dg % 2
                S.op("sp", lambda e: e.dma_start(out=f.wgu[s][:, 0, :, :], in_=k.winC[dg, :, :, :]),
                     reads=[k.b_conv["winC_%d" % dg]], writes=[f.b_wgu[s]], dma=True)
                S.op("sp", lambda e: e.dma_start(out=f.wgu[s][:, 1, :, :], in_=k.winC[4 + dg, :, :, :]),
                     reads=[k.b_conv["winC_%d" % (4 + dg)]], wadd=[f.b_wgu[s]], dma=True)
                wpj = f.wd[s][:, :, :].rearrange("p a b -> p (a b)")[:, 0:12 * 512].rearrange("p (a b) -> p a b", b=512)
                S.op("sp", lambda e: e.dma_start(out=wpj, in_=wprj_v[:, :, dg * 512:(dg + 1) * 512]),
                     reads=[k.b_conv["wprj"]], writes=[f.b_wd[s]], dma=True)
                for dj in range(4):
                    dc = dg * 4 + dj
                    cs_ = slice(dj * 128, (dj + 1) * 128)
                    q = f.n_gu % 2
                    f.n_gu += 1
                    for j, (pt, bp) in enumerate(((f.pg[q], f.b_pg[q]), (f.pu[q], f.b_pu[q]))):
                        for kc in range(NKC):
                            S.op("pe", lambda e: e.matmul(pt[:, :], lhsT=f.wgu[s][:, j, kc, cs_], rhs=f.hT[:, kc, :],
                                                          start=(kc == 0), stop=(kc == NKC - 1)),
                                 reads=[f.b_wgu[s], f.b_hT], **({"writes": [bp]} if kc == 0 else {"wadd": [bp]}))
                        S.op("act", lambda e: e.activation(out=sig[j][:, :], in_=pt[:, :], func=AF.Sigmoid), writes=[bp, b_sig[j]])
                    for j, (k0, k1) in enumerate(((0, 8), (8, 12))):
                        po, bpo = f.po[j], f.b_po[j]
                        for kc in range(k0, k1):
                            S.op("pe", lambda e: e.matmul(po[:, :], lhsT=wpj[:, kc, cs_], rhs=ygT[:, kc, :], start=(kc == k0), stop=(kc == k1 - 1)),
                                 reads=[f.b_wd[s], b_yg], **({"writes": [bpo]} if kc == k0 else {"wadd": [bpo]}))
                        S.op("dve", lambda e: e.tensor_tensor(out=tm[j][:, :], in0=po[:, :], in1=sig[j][:, :], op=ALU.mult),
                             reads=[b_sig[j]], writes=[bpo, b_tm[j]])
                    S.op("pool", lambda e: e.tensor_tensor(out=mergedT[:, dc, :], in0=tm[0][:, :], in1=tm[1][:, :], op=ALU.add),
                         reads=[b_tm[0], b_tm[1]], **({"writes": f.b_hb} if dc == 0 else {"wadd": f.b_hb}))
            for nb in range(4):
                s = nb % 2
                S.op("sp", lambda e: e.dma_start(out=f.wgu[s][:, 0, :, :], in_=k.wout[nb, :, :, :]),
                     reads=[k.b_conv["wout_%d" % nb]], writes=[f.b_wgu[s]], dma=True)
                for tt in range(NTT):
                    q = f.n_po % 2
                    f.n_po += 1
                    for kc in range(NKC):
                        S.op("pe", lambda e: e.matmul(f.po[q][:, :], lhsT=mergedT[:, kc, tt * 128:(tt + 1) * 128], rhs=f.wgu[s][:, 0, kc, :],
                                                      start=(kc == 0), stop=(kc == NKC - 1)),
                             reads=f.b_hb + [f.b_wgu[s]], **({"writes": [f.b_po[q]]} if kc == 0 else {"wadd": [f.b_po[q]]}))
                    S.op("dve", lambda e: e.tensor_tensor(out=f.xa[:, tt, nb * 512:(nb + 1) * 512], in0=f.xa[:, tt, nb * 512:(nb + 1) * 512],
                                                          in1=f.po[q][:, :], op=ALU.add), writes=[f.b_po[q], f.b_xa[tt]])
            rmsnorm_T(k, f, c, 1)
            double_x(k, f)
            ffn(k, f, 1)
            for tt in range(NTT):
                S.op("act", lambda e: e.activation(out=f.hb[:, tt, :], in_=f.xa[:, tt, :], func=AF.Square, accum_out=f.ss[:, tt:tt + 1]),
                     reads=[f.b_xa[tt]], writes=[f.b_hb[tt]], wadd=[f.b_ss])
            S.op("act", lambda e: e.activation(out=f.rstd[:, 0:NTT], in_=f.ss[:, 0:NTT], func=AF.Sqrt, scale=1.0 / D, bias=1e-6),
                 reads=[f.b_ss], writes=[f.b_rstd])
            S.op("dve", lambda e: e.reciprocal(out=f.rstd[:, 0:NTT], in_=f.rstd[:, 0:NTT]), reads=[f.b_rstd], writes=[f.b_rstd])
            for tt in range(NTT):
                S.op("dve", lambda e: e.scalar_tensor_tensor(out=f.xa[:, tt, :], in0=f.xa[:, tt, :], scalar=f.rstd[:, tt:tt + 1], in1=gfin[:, :],
                                                             op0=ALU.mult, op1=ALU.mult), reads=[f.b_rstd, b_gfin], writes=[f.b_xa[tt]])
                r0 = b * TB + tt * 128
                S.op("sp", lambda e: e.dma_start(out=k.y[r0:r0 + 128, :], in_=f.xa[:, tt, :]), reads=[f.b_xa[tt]], wadd=[k.b_y], dma=True)
        S.barrier()


_CACHE = {}


def kernel(**inputs):
    L = 4096
    xs = np.concatenate([np.asarray(inputs["x_prompt"], dtype=np.float32),
                         np.asarray(inputs["x_sample"], dtype=np.float32)], axis=0)
    nseq = xs.shape[0]
    wmap = {}
    for n in WNAMES:
        a = np.asarray(inputs[n], dtype=np.float32)
        wmap[n] = np.ascontiguousarray(a if n == "norm_final" else a[0])
    nc, k = build(L, stages="ABC", dbg=False)
    in_maps = []
    for core in range(8):
        m = dict(wmap)
        m["x"] = np.ascontiguousarray(xs[core % nseq])
        in_maps.append(m)
    res = run_bass_kernel_spmd(nc, in_maps, core_ids=list(range(8)))
    ys = [np.asarray(res.results[i]["y"], dtype=np.float32) for i in range(nseq)]
    nb = np.asarray(inputs["x_prompt"]).shape[0]
    y_prompt = np.stack(ys[:nb], 0)
    y_sample = np.stack(ys[nb:], 0)
    return (y_prompt, y_sample)
```
